# Optimizing a Trainium2 kernel written in Bass

```python
import jax, jax.numpy as jnp
from jax import lax
import numpy as np


D_MODEL = 2048
BATCH = 8
SEQ = 4096
DEPTH = 1

CTX_LEN = 256
GRID_W = 64
N_MOD = 6
HG_WIDTH = D_MODEL // 2
HG_DK = 128
HG_HEADS = HG_WIDTH // HG_DK
HG_DV = HG_WIDTH // HG_HEADS
HG_CHUNK = 64
HEAD_DIM = 128
ATT_WIDTH = D_MODEL // 2
ATT_HEADS = ATT_WIDTH // HEAD_DIM
ATT_KV_HEADS = ATT_HEADS // 4
ATT_GROUP = ATT_HEADS // ATT_KV_HEADS
KV_WIDTH = ATT_KV_HEADS * HEAD_DIM
WINDOW = 128
ATT_BLOCK = 128
ROPE_THETA = 10000.0
D_FF = 256 * ((8 * D_MODEL // 3 + 255) // 256)
CONV_W = 3
EPS = 1e-6
PROJ_SIZES = (HG_WIDTH, HG_WIDTH, HG_WIDTH, HG_WIDTH, HG_WIDTH, ATT_WIDTH, KV_WIDTH, KV_WIDTH, D_MODEL, D_MODEL)
PROJ_TOTAL = sum(PROJ_SIZES)
F32 = jnp.float32

kernel_name = 'hybrid_hgrn2_swa_convffn_dit'


def rms_norm(x, gain):
    xf = x.astype(F32)
    y = xf * lax.rsqrt(jnp.mean(jnp.square(xf), axis=-1, keepdims=True) + EPS)
    return (y * gain.astype(F32)).astype(x.dtype)


def adaln(cond, w, b):
    mod = jax.nn.silu(cond) @ w + b
    return mod.reshape(cond.shape[0], N_MOD, D_MODEL)


def modulate(x, gain, shift, scale):
    return rms_norm(x, gain) * (1 + scale[:, None, :]) + shift[:, None, :]


def split_projection(p):
    return jnp.split(p, np.cumsum(PROJ_SIZES)[:-1].tolist(), axis=-1)


def hgrn2_chunk_scan(q, k, v, log_f, s0):
    B, H, T, _ = q.shape
    n = T // HG_CHUNK
    q, k, v, log_f = [a.astype(F32).reshape(B, H, n, HG_CHUNK, a.shape[-1]) for a in (q, k, v, log_f)]
    b = jnp.cumsum(log_f, axis=3)
    b_ref = b[:, :, :, HG_CHUNK // 2:HG_CHUNK // 2 + 1]
    b_last = b[:, :, :, -1:]
    tri = jnp.tril(jnp.ones((HG_CHUNK, HG_CHUNK), dtype=bool))
    intra = jnp.einsum('bhncd,bhnsd->bhncs', q * jnp.exp(b - b_ref), k * jnp.exp(b_ref - b))
    intra = jnp.where(tri, intra, 0.0)
    o_intra = jnp.einsum('bhncs,bhnse->bhnce', intra, v)
    q_in = q * jnp.exp(b)
    k_out = k * jnp.exp(b_last - b)
    chunk_decay = jnp.exp(b_last[:, :, :, 0])

    def step(s, xs):
        q_n, k_n, v_n, d_n = xs
        o_n = jnp.einsum('bhcd,bhde->bhce', q_n, s)
        s = d_n[..., None] * s + jnp.einsum('bhsd,bhse->bhde', k_n, v_n)
        return s, o_n

    xs = tuple(jnp.moveaxis(a, 2, 0) for a in (q_in, k_out, v, chunk_decay))
    s_final, o_inter = lax.scan(step, s0.astype(F32), xs)
    o = o_intra + jnp.moveaxis(o_inter, 0, 2)
    return o.reshape(B, H, T, -1), s_final


def hgrn2_direction(q, f_logit, v, lb, s0, reverse):
    if reverse:
        q, f_logit, v = [jnp.flip(a, axis=1) for a in (q, f_logit, v)]
    B, T, _ = q.shape
    lb = lb.astype(F32)
    f = lb + (1.0 - lb) * jax.nn.sigmoid(f_logit.astype(F32))
    heads = lambda a: a.astype(F32).reshape(B, T, HG_HEADS, -1).transpose(0, 2, 1, 3)
    o, s_final = hgrn2_chunk_scan(heads(q), heads(1.0 - f), heads(v), heads(jnp.log(f)), s0)
    o = o.transpose(0, 2, 1, 3)
    if reverse:
        o = jnp.flip(o, axis=1)
    return o, s_final


def hgrn2_bidirectional(q, f_fwd, f_bwd, v, g, lb, norm_g, s0_fwd, s0_bwd):
    B, T, _ = q.shape
    q = jax.nn.silu(q)
    o_f, s_f = hgrn2_direction(q, f_fwd, v, lb[0], s0_fwd, False)
    o_b, s_b = hgrn2_direction(q, f_bwd, v, lb[1], s0_bwd, True)
    o = rms_norm(o_f + o_b, norm_g).astype(v.dtype)
    y = o.reshape(B, T, HG_WIDTH) * jax.nn.silu(g)
    return y, s_f, s_b


def axial_rope(x, rows, cols):
    half = HEAD_DIM // 2
    quarter = half // 2
    inv_freq = ROPE_THETA ** (-jnp.arange(quarter, dtype=F32) / quarter)
    bshape = (1, x.shape[1]) + (1,) * (x.ndim - 3) + (quarter,)

    def rotate(xa, pos):
        ang = pos.astype(F32)[:, None] * inv_freq
        cos = jnp.cos(ang).reshape(bshape).astype(x.dtype)
        sin = jnp.sin(ang).reshape(bshape).astype(x.dtype)
        x1, x2 = xa[..., :quarter], xa[..., quarter:]
        return jnp.concatenate([x1 * cos - x2 * sin, x2 * cos + x1 * sin], axis=-1)

    return jnp.concatenate([rotate(x[..., :half], rows), rotate(x[..., half:], cols)], axis=-1)


def attention_heads(aq, ak, av, q_g, k_g):
    B, T, _ = aq.shape
    q = rms_norm(aq.reshape(B, T, ATT_KV_HEADS, ATT_GROUP, HEAD_DIM), q_g)
    k = rms_norm(ak.reshape(B, T, ATT_KV_HEADS, HEAD_DIM), k_g)
    v = av.reshape(B, T, ATT_KV_HEADS, HEAD_DIM)
    return q, k, v


def softmax_with_sink(logits, sink):
    sink = jnp.broadcast_to(sink.astype(F32), logits.shape[:-1] + (1,))
    p = jax.nn.softmax(jnp.concatenate([logits, sink], axis=-1), axis=-1)
    return p[..., :-1]


def context_attention(q, k, v, sink):
    B, L = q.shape[:2]
    s = jnp.einsum('blkgd,bmkd->bkglm', q, k).astype(F32) * (HEAD_DIM ** -0.5)
    p = softmax_with_sink(s, sink.reshape(1, ATT_KV_HEADS, ATT_GROUP, 1, 1)).astype(v.dtype)
    o = jnp.einsum('bkglm,bmkd->blkgd', p, v)
    return o.reshape(B, L, ATT_WIDTH)


def windowed_attention(q, k, v, k_ctx, v_ctx, sink):
    B, T = q.shape[:2]
    n = T // ATT_BLOCK
    qb = q.reshape(B, n, ATT_BLOCK, ATT_KV_HEADS, ATT_GROUP, HEAD_DIM)

    def neighbourhood(a):
        ap = jnp.pad(a, ((0, 0), (ATT_BLOCK, ATT_BLOCK), (0, 0), (0, 0)))
        ap = ap.reshape(B, n + 2, ATT_BLOCK, ATT_KV_HEADS, HEAD_DIM)
        return jnp.concatenate([ap[:, :-2], ap[:, 1:-1], ap[:, 2:]], axis=2)

    kw, vw = neighbourhood(k), neighbourhood(v)
    qi = jnp.arange(ATT_BLOCK)[:, None]
    kj = jnp.arange(3 * ATT_BLOCK)[None, :] - ATT_BLOCK
    key_pos = jnp.arange(n)[:, None, None] * ATT_BLOCK + kj
    mask = (jnp.abs(kj - qi) <= WINDOW) & (key_pos >= 0) & (key_pos < T)
    scale = HEAD_DIM ** -0.5
    s_w = jnp.einsum('bnqkgd,bnskd->bkgnqs', qb, kw).astype(F32) * scale
    s_w = jnp.where(mask, s_w, -jnp.inf)
    s_c = jnp.einsum('bnqkgd,bmkd->bkgnqm', qb, k_ctx).astype(F32) * scale
    p = softmax_with_sink(jnp.concatenate([s_w, s_c], axis=-1),
                          sink.reshape(1, ATT_KV_HEADS, ATT_GROUP, 1, 1, 1)).astype(v.dtype)
    p_w, p_c = p[..., :3 * ATT_BLOCK], p[..., 3 * ATT_BLOCK:]
    o = jnp.einsum('bkgnqs,bnskd->bnqkgd', p_w, vw) + jnp.einsum('bkgnqm,bmkd->bnqkgd', p_c, v_ctx)
    return o.reshape(B, T, ATT_WIDTH)


def gated_merge(y_rec, y_att, gate_rec, gate_att, w_rec, w_att, w_o):
    y = jax.nn.sigmoid(gate_rec) * (y_rec @ w_rec) + jax.nn.sigmoid(gate_att) * (y_att @ w_att)
    return y @ w_o


def depthwise_conv(a, w, b):
    out = lax.conv_general_dilated(a, w[:, None, :].astype(a.dtype), (1,), ((CONV_W // 2, CONV_W // 2),),
                                   dimension_numbers=('NWC', 'WIO', 'NWC'), feature_group_count=a.shape[-1])
    return out + b


def conv_ffn(h, w_up, conv_w, conv_b, w_down):
    a, u = jnp.split(h @ w_up, 2, axis=-1)
    return (jax.nn.silu(depthwise_conv(a, conv_w, conv_b)) * u) @ w_down


def setup_inputs(seed: int = 0) -> dict:
    key = jax.random.key(seed)
    ks = jax.random.split(key, 21)
    nrm = lambda k, shape, s: jax.random.normal(k, shape, F32) * s
    D = D_MODEL
    return {
        'x': nrm(ks[0], (BATCH, SEQ, D), 1.0),
        'c': nrm(ks[1], (BATCH, D), 1.0),
        'ctx': nrm(ks[2], (BATCH, CTX_LEN, D), 1.0),
        'c_ctx': nrm(ks[3], (D,), 1.0),
        'w_ada': nrm(ks[4], (DEPTH, D, N_MOD * D), D ** -0.5),
        'b_ada': nrm(ks[5], (DEPTH, N_MOD * D), 0.01),
        'norm1_g': 1.0 + nrm(ks[6], (DEPTH, D), 0.1),
        'w_in': nrm(ks[7], (DEPTH, D, PROJ_TOTAL), D ** -0.5),
        'hgrn_lower_bounds': 1.0 + nrm(ks[8], (DEPTH + 1, 2, HG_WIDTH), 0.1),
        'hgrn_norm_g': 1.0 + nrm(ks[9], (DEPTH, HG_DV), 0.1),
        'q_norm_g': 1.0 + nrm(ks[10], (DEPTH, HEAD_DIM), 0.1),
        'k_norm_g': 1.0 + nrm(ks[11], (DEPTH, HEAD_DIM), 0.1),
        'attn_sink': nrm(ks[12], (DEPTH, ATT_HEADS), 0.5),
        'w_rec_proj': nrm(ks[13], (DEPTH, HG_WIDTH, D), HG_WIDTH ** -0.5),
        'w_att_proj': nrm(ks[14], (DEPTH, ATT_WIDTH, D), ATT_WIDTH ** -0.5),
        'w_out': nrm(ks[15], (DEPTH, D, D), D ** -0.5),
        'norm2_g': 1.0 + nrm(ks[16], (DEPTH, D), 0.1),
        'w_up': nrm(ks[17], (DEPTH, D, 2 * D_FF), D ** -0.5),
        'conv_w': nrm(ks[18], (DEPTH, CONV_W, D_FF), CONV_W ** -0.5),
        'conv_b': nrm(ks[19], (DEPTH, D_FF), 0.01),
        'w_down': nrm(ks[20], (DEPTH, D_FF, D), D_FF ** -0.5),
    }


def reference(x, c, ctx, c_ctx, w_ada, b_ada, norm1_g, w_in, hgrn_lower_bounds, hgrn_norm_g, q_norm_g,
              k_norm_g, attn_sink, w_rec_proj, w_att_proj, w_out, norm2_g, w_up, conv_w, conv_b, w_down):
    B, T, _ = x.shape
    n_rows = T // GRID_W
    rows = jnp.broadcast_to(jnp.arange(n_rows)[:, None], (n_rows, GRID_W)).reshape(-1)
    cols = jnp.broadcast_to(jnp.arange(GRID_W)[None, :], (n_rows, GRID_W)).reshape(-1)
    lower_bounds = jnp.cumsum(jax.nn.softmax(hgrn_lower_bounds.astype(F32), axis=0), axis=0)
    s_zero = jnp.zeros((B, HG_HEADS, HG_DK, HG_DV), F32)
    for l in range(DEPTH):
        mod_x = adaln(c, w_ada[l], b_ada[l])
        mod_c = adaln(c_ctx[None, :], w_ada[l], b_ada[l])
        pc = split_projection(modulate(ctx, norm1_g[l], mod_c[:, 0], mod_c[:, 1]) @ w_in[l])
        px = split_projection(modulate(x, norm1_g[l], mod_x[:, 0], mod_x[:, 1]) @ w_in[l])
        lb = lower_bounds[l]
        rec_c, s_f, s_b = hgrn2_bidirectional(pc[0], pc[1], pc[2], pc[3], pc[4], lb, hgrn_norm_g[l], s_zero, s_zero)
        rec_x, _, _ = hgrn2_bidirectional(px[0], px[1], px[2], px[3], px[4], lb, hgrn_norm_g[l], s_f, s_b)
        qc, kc, vc = attention_heads(pc[5], pc[6], pc[7], q_norm_g[l], k_norm_g[l])
        qx, kx, vx = attention_heads(px[5], px[6], px[7], q_norm_g[l], k_norm_g[l])
        qx = axial_rope(qx, rows, cols)
        kx = axial_rope(kx, rows, cols)
        att_x = windowed_attention(qx, kx, vx, kc, vc, attn_sink[l])
        mix_x = gated_merge(rec_x, att_x, px[8], px[9], w_rec_proj[l], w_att_proj[l], w_out[l])
        x_mid = x + mod_x[:, 2][:, None, :] * mix_x
        ffn_x = conv_ffn(modulate(x_mid, norm2_g[l], mod_x[:, 3], mod_x[:, 4]), w_up[l], conv_w[l], conv_b[l], w_down[l])
        new_x = x_mid + mod_x[:, 5][:, None, :] * ffn_x
        if l < DEPTH - 1:
            att_c = context_attention(qc, kc, vc, attn_sink[l])
            mix_c = gated_merge(rec_c, att_c, pc[8], pc[9], w_rec_proj[l], w_att_proj[l], w_out[l])
            ctx = ctx + mod_c[:, 2][:, None, :] * mix_c
            ffn_c = conv_ffn(modulate(ctx, norm2_g[l], mod_c[:, 3], mod_c[:, 4]), w_up[l], conv_w[l], conv_b[l], w_down[l])
            ctx = ctx + mod_c[:, 5][:, None, :] * ffn_c
        x = new_x
    return x
```

```python
import numpy as np
from contextlib import ExitStack
import concourse.bass as bass
import concourse.mybir as mybir
from concourse.bass_utils import run_bass_kernel_spmd

F32 = mybir.dt.float32
BF16 = mybir.dt.bfloat16
AF = mybir.ActivationFunctionType
ALU = mybir.AluOpType

PE, ACT, DVE, POOL, SP = "pe", "act", "dve", "pool", "sp"
ENGS = (PE, ACT, DVE, POOL, SP)
DMA_ENGS = (SP, ACT, POOL)
N_DMA_SEMS = 16

T = 4096
L = 256
TL = T + L
D = 2048
KC = 16
DFF = 5632
NFB = 44
EPS = 1e-6
NCH = TL // 64
NPAIR = TL // 128

DEBUG_OUT = []
STOP_AFTER = 99


class Buf:
    __slots__ = ("name", "w", "r", "excl")

    def __init__(self, name="", excl=False):
        self.name = name
        self.w = None
        self.r = {}
        self.excl = excl


class Op:
    __slots__ = ("eng", "fn", "dma", "deps", "marked", "cnt", "sem_i", "barrier")

    def __init__(self, eng, fn, dma):
        self.eng = eng
        self.fn = fn
        self.dma = dma
        self.deps = []
        self.marked = False
        self.cnt = 0
        self.sem_i = -1
        self.barrier = 0


class Prog:
    def __init__(self):
        self.ops = []
        self.nbar = 0

    def add(self, eng, fn, reads=(), writes=(), dma=False):
        i = len(self.ops)
        op = Op(eng, fn, dma)
        writes = [b for b in writes if b is not None] + [b for b in reads if b is not None and b.excl]
        reads = [b for b in reads if b is not None and not b.excl]
        deps = set()
        for b in reads:
            if b.w is not None:
                deps.add(b.w)
        for b in writes:
            if b.w is not None:
                deps.add(b.w)
            for r in b.r.values():
                if isinstance(r, list):
                    deps.update(r)
                else:
                    deps.add(r)
        for b in reads:
            if dma:
                b.r.setdefault("dma", []).append(i)
            else:
                b.r[eng] = i
        for b in writes:
            b.w = i
            b.r = {}
        ops = self.ops
        for d in deps:
            p = ops[d]
            if p.eng == PE and eng == PE and not p.dma and not dma:
                continue
            op.deps.append(d)
            p.marked = True
        ops.append(op)
        return i

    def barrier(self):
        self.nbar += 1
        for e in ENGS:
            op = Op(e, None, False)
            op.barrier = self.nbar
            self.ops.append(op)

    def emit(self, nc, es):
        ops = self.ops
        eng_sem = {e: es.enter_context(nc.semaphore("s_" + e)) for e in ENGS}
        bar_sem = es.enter_context(nc.semaphore("s_bar"))
        dma_sems = {e: [es.enter_context(nc.semaphore("d_%s%d" % (e, k))) for k in range(N_DMA_SEMS)]
                    for e in DMA_ENGS}
        cnt = {e: 0 for e in ENGS}
        dcnt = {e: [0] * N_DMA_SEMS for e in DMA_ENGS}
        drr = {e: 0 for e in DMA_ENGS}
        for op in ops:
            if op.barrier:
                continue
            if op.dma:
                k = drr[op.eng]
                drr[op.eng] = (k + 1) % N_DMA_SEMS
                dcnt[op.eng][k] += 16
                op.sem_i = k
                op.cnt = dcnt[op.eng][k]
            elif op.marked:
                cnt[op.eng] += 1
                op.cnt = cnt[op.eng]
        per_eng = {e: [] for e in ENGS}
        for op in ops:
            per_eng[op.eng].append(op)

        def run(engobj, ename):
            known = {}
            issued = [0] * N_DMA_SEMS
            for op in per_eng[ename]:
                if op.barrier:
                    if ename in dma_sems:
                        for k in range(N_DMA_SEMS):
                            v = issued[k]
                            if v > 0 and known.get((ename, k), 0) < v:
                                engobj.wait_ge(dma_sems[ename][k], v)
                                known[(ename, k)] = v
                    engobj.drain().then_inc(bar_sem, 1)
                    engobj.wait_ge(bar_sem, len(ENGS) * op.barrier)
                    continue
                need = {}
                for d in op.deps:
                    p = ops[d]
                    key = (p.eng, p.sem_i) if p.dma else (p.eng, -1)
                    if need.get(key, 0) < p.cnt:
                        need[key] = p.cnt
                if op.dma and op.cnt > 16:
                    key = (ename, op.sem_i)
                    if need.get(key, 0) < op.cnt - 16:
                        need[key] = op.cnt - 16
                for key, v in need.items():
                    if known.get(key, 0) >= v:
                        continue
                    known[key] = v
                    sem = dma_sems[key[0]][key[1]] if key[1] >= 0 else eng_sem[key[0]]
                    engobj.wait_ge(sem, v)
                ins = op.fn(engobj)
                if op.dma:
                    ins.then_inc(dma_sems[ename][op.sem_i], 16)
                    issued[op.sem_i] = op.cnt
                elif op.marked:
                    ins.then_inc(eng_sem[ename], 1)
            if ename in dma_sems:
                for k in range(N_DMA_SEMS):
                    v = issued[k]
                    if v > 0 and known.get((ename, k), 0) < v:
                        engobj.wait_ge(dma_sems[ename][k], v)

        with nc.Block() as block:
            @block.tensor
            def _(e):
                run(e, PE)

            @block.scalar
            def _(e):
                run(e, ACT)

            @block.vector
            def _(e):
                run(e, DVE)

            @block.gpsimd
            def _(e):
                run(e, POOL)

            @block.sync
            def _(e):
                run(e, SP)


class Rot:
    def __init__(self, items):
        self.items = items
        self.i = 0

    def next(self):
        it = self.items[self.i % len(self.items)]
        self.i += 1
        return it


class KB:
    def __init__(self):
        self.nc = bass.Bass("TRN2", target_bir_lowering=False)
        self.P = Prog()
        self.dr = {}
        self.drb = {}

    def din(self, name, shape, dt=F32):
        t = self.nc.dram_tensor(name, list(shape), dt, kind="ExternalInput")
        self.dr[name] = t
        self.drb[name] = Buf(name)
        return t

    def dscr(self, name, shape, dt):
        kind = "ExternalOutput" if name in DEBUG_OUT else "Internal"
        t = self.nc.dram_tensor(name, list(shape), dt, kind=kind)
        self.dr[name] = t
        return t

    def sb(self, es, name, free, dt):
        return es.enter_context(self.nc.sbuf_tensor(name, [128, free], dt))

    def sbs(self, es, name, free, dt, n):
        return Rot([(self.sb(es, "%s%d" % (name, i), free, dt), Buf("%s%d" % (name, i))) for i in range(n)])

    def mm(self, out, lhsT, rhs, R, W, start=True, stop=True):
        self.P.add(PE, lambda e: e.matmul(out, lhsT=lhsT, rhs=rhs, start=start, stop=stop), R, W)

    def tr(self, out, in_, ident, R, W):
        self.P.add(PE, lambda e: e.transpose(out=out, in_=in_, identity=ident), R, W)

    def act(self, out, in_, func, R, W, scale=1.0, bias=0.0, accum=None):
        if accum is None:
            self.P.add(ACT, lambda e: e.activation(out=out, in_=in_, func=func, bias=bias, scale=scale), R, W)
        else:
            self.P.add(ACT, lambda e: e.activation(out=out, in_=in_, func=func, bias=bias, scale=scale,
                                                   accum_out=accum), R, W)

    def ts(self, out, in0, s1, s2, op0, op1, R, W, eng=DVE):
        self.P.add(eng, lambda e: e.tensor_scalar(out=out, in0=in0, scalar1=s1, scalar2=s2, op0=op0, op1=op1), R, W)

    def tt(self, out, in0, in1, op, R, W, eng=DVE):
        self.P.add(eng, lambda e: e.tensor_tensor(out=out, in0=in0, in1=in1, op=op), R, W)

    def stt(self, out, in0, scalar, in1, op0, op1, R, W):
        self.P.add(DVE, lambda e: e.scalar_tensor_tensor(out=out, in0=in0, scalar=scalar, in1=in1, op0=op0, op1=op1),
                   R, W)

    def scan(self, out, d0, d1, R, W):
        self.P.add(DVE, lambda e: e.tensor_tensor_scan(out=out, data0=d0, data1=d1, initial=0.0,
                                                       op0=ALU.mult, op1=ALU.add), R, W)

    def cp(self, out, in_, R, W, eng=DVE):
        if eng == ACT:
            self.P.add(ACT, lambda e: e.activation(out=out, in_=in_, func=AF.Copy), R, W)
        else:
            self.P.add(eng, lambda e: e.tensor_copy(out=out, in_=in_), R, W)

    def memset(self, ap, val, W, eng=DVE):
        self.P.add(eng, lambda e: e.memset(ap, val), (), W)

    def dma(self, q, out, in_, R, W):
        self.P.add(q, lambda e: e.dma_start(out=out, in_=in_), R, W, dma=True)


def fap(t, col, dims, p0=0, npart=128):
    F = t.shape[1]
    return bass.AP(t, p0 * F + col, [[F, npart]] + [list(d) for d in dims])


MT = 512


def build_program():
    kb = KB()
    nc = kb.nc
    P = kb.P
    x_d = kb.din("x", [T, D])
    ctx_d = kb.din("ctx", [L, D])
    cvec_d = kb.din("cvecT", [128, 32])
    wada_d = kb.din("w_ada_t", [24, 128, KC * 512])
    badaT_d = kb.din("b_adaT", [128, 96])
    bgate_d = kb.din("b_gate_bc", [128, 2 * D])
    g1_d = kb.din("g1T", [128, KC])
    g2_d = kb.din("g2T", [128, KC])
    win_d = kb.din("w_in_t", [21, 128, KC * 512])
    lbraw_d = kb.din("lbrawT", [128, 32])
    vec128_d = kb.din("vec128", [128, 3])
    sink_d = kb.din("sink_bc", [128, 8])
    wrec_d = kb.din("w_rec_t", [4, 128, 8 * 512])
    watt_d = kb.din("w_att_t", [4, 128, 8 * 512])
    wout_d = kb.din("w_out_t", [4, 128, KC * 512])
    wup_d = kb.din("w_up_t", [NFB, 128, KC * 256])
    convw_d = kb.din("convwT", [128, 3 * NFB])
    convb_d = kb.din("convbT", [128, NFB])
    wdown_d = kb.din("w_down_t", [4, 4, 128, 11 * 512])
    ident_d = kb.din("ident", [128, 128])
    rt_d = kb.din("ropeRT", [128, 128])
    mask_d = kb.din("masks", [128, 4 * 128])
    cos_d = kb.din("ropeC", [128, T])
    sin_d = kb.din("ropeS", [128, T])
    out_d = nc.dram_tensor("out", [T, D], F32, kind="ExternalOutput")
    out_b = None

    QDF = kb.dscr("QDF", [8, 128, TL], BF16)
    KDF = kb.dscr("KDF", [8, 128, TL], BF16)
    QDB = kb.dscr("QDB", [8, 128, TL], BF16)
    KDB = kb.dscr("KDB", [8, 128, TL], BF16)
    KDFt = kb.dscr("KDFt", [8, 128, NPAIR * 128], BF16)
    KDBt = kb.dscr("KDBt", [8, 128, NPAIR * 128], BF16)
    VS = kb.dscr("VS", [8, 128, NPAIR * 128], BF16)
    GT = kb.dscr("GT", [8, 128, T], BF16)
    AQT = kb.dscr("AQT", [8, 128, T], BF16)
    AKT = kb.dscr("AKT", [2, 128, TL], BF16)
    AVS = kb.dscr("AVS", [2, 128, NPAIR * 128], BF16)
    SGR = kb.dscr("SGR", [16, 128, T], BF16)
    SGA = kb.dscr("SGA", [16, 128, T], BF16)
    YREC = kb.dscr("YREC", [8, 128, T], BF16)
    YATT = kb.dscr("YATT", [8, 128, T], BF16)
    H2T = kb.dscr("H2T", [KC, 128, T], BF16)
    CSC = kb.dscr("CSC", [128, 3 * 16 * NCH], F32)
    MODS = kb.dscr("MODS", [128, 6 * KC + 2 * D], F32)
    scr_b = {n: None for n in ("QDF", "KDF", "QDB", "KDB", "KDFt", "KDBt", "VS", "GT", "AQT", "AKT", "AVS",
                                 "SGR", "SGA", "YREC", "YATT", "H2T", "CSC", "MODS")}

    with ExitStack() as ges:
        ident_f = kb.sb(ges, "ident_f", 128, F32)
        ident_b = kb.sb(ges, "ident_b", 128, BF16)
        ones_f = kb.sb(ges, "ones_f", 128, F32)
        b_const = Buf("const")

        psum = [ges.enter_context(nc.psum_tensor("ps%d" % i, [128, 512], F32)) for i in range(8)]
        psb = [Buf("ps%d" % i, excl=True) for i in range(8)]

        kb.dma(SP, ident_f[:], ident_d.ap(), [], [b_const])
        kb.cp(ident_b[:], ident_f[:], [b_const], [b_const])
        kb.memset(ones_f[:], 1.0, [b_const])

        with ExitStack() as es:
            cv_f = kb.sb(es, "cv_f", 32, F32)
            csil = kb.sb(es, "csil", 32, BF16)
            crep = kb.sb(es, "crep", KC * 128, BF16)
            badaT = kb.sb(es, "badaT", 96, F32)
            bgate = kb.sb(es, "bgate", 2 * D, F32)
            g1s = kb.sb(es, "g1s", KC, F32)
            g2s = kb.sb(es, "g2s", KC, F32)
            modT = kb.sb(es, "modT", 192, F32)
            mods = kb.sb(es, "mods", 6 * KC + 2 * D, F32)
            b0 = Buf("p0")
            b_mods = Buf("mods")
            wb = kb.sbs(es, "wb0_", KC * 512, BF16, 3)
            kb.dma(SP, cv_f[:], cvec_d.ap(), [], [b0])
            kb.dma(SP, badaT[:], badaT_d.ap(), [], [b0])
            kb.dma(SP, bgate[:], bgate_d.ap(), [], [b0])
            kb.dma(SP, g1s[:], g1_d.ap(), [], [b0])
            kb.dma(SP, g2s[:], g2_d.ap(), [], [b0])
            kb.act(csil[:], cv_f[:], AF.Silu, [b0], [b0])
            kb.cp(fap(crep, 0, [[128, KC], [1, 128]]), fap(csil, 0, [[1, KC], [0, 128]]), [b0], [b0])
            ps_mod = psum[0]
            gi = 0
            for cb in range(24):
                j = cb // 4
                wt, wbuf = wb.next()
                kb.dma(POOL, wt[:], wada_d.ap()[cb], [], [wbuf])
                if j in (2, 5):
                    ps = psum[1 + gi % 2]
                    pb = psb[1 + gi % 2]
                    gi += 1
                    for kc in range(KC):
                        kb.mm(ps[:], crep[:, kc * 128:(kc + 1) * 128], wt[:, kc * 512:(kc + 1) * 512],
                              [b0, wbuf], [pb], start=(kc == 0), stop=(kc == KC - 1))
                    gcol = ((0 if j == 2 else 1) * D) + (cb % 4) * 512
                    kb.tt(mods[:, 6 * KC + gcol: 6 * KC + gcol + 512], ps[:], bgate[:, gcol:gcol + 512], ALU.add,
                          [pb, b0], [b_mods])
                else:
                    for f in range(4):
                        blk = cb * 4 + f
                        for kc in range(KC):
                            kb.mm(ps_mod[:, blk * 2: blk * 2 + 2],
                                  wt[:, kc * 512 + f * 128: kc * 512 + (f + 1) * 128],
                                  fap(csil, kc, [[16, 2]]),
                                  [b0, wbuf], [psb[0]], start=(kc == 0), stop=(kc == KC - 1))
            kb.tt(fap(modT, 0, [[2, 96], [1, 2]]), fap(ps_mod, 0, [[2, 96], [1, 2]]),
                  fap(badaT, 0, [[1, 96], [0, 2]]), ALU.add, [psb[0], b0], [b0])

            def modv(j, v):
                return fap(modT, (j * 16) * 2 + v, [[2, KC]])
            kb.stt(mods[:, 0:16], modv(1, 0), 1.0, g1s[:], ALU.add, ALU.mult, [b0], [b_mods])
            kb.cp(mods[:, 16:32], modv(0, 0), [b0], [b_mods])
            kb.stt(mods[:, 32:48], modv(1, 1), 1.0, g1s[:], ALU.add, ALU.mult, [b0], [b_mods])
            kb.cp(mods[:, 48:64], modv(0, 1), [b0], [b_mods])
            kb.stt(mods[:, 64:80], modv(4, 0), 1.0, g2s[:], ALU.add, ALU.mult, [b0], [b_mods])
            kb.cp(mods[:, 80:96], modv(3, 0), [b0], [b_mods])
            kb.dma(SP, MODS.ap(), mods[:], [b_mods], [scr_b["MODS"]])
        P.barrier()
        G = locals()
        if STOP_AFTER >= 1:
            phase1(kb, ges, G)
        if STOP_AFTER >= 2:
            phase2a(kb, ges, G)
        if STOP_AFTER >= 3:
            phase2b(kb, ges, G)
        if STOP_AFTER >= 4:
            phase3(kb, ges, G)
        if STOP_AFTER >= 5:
            phase4(kb, ges, G)
        P.emit(nc, ges)
    return nc


def phase1(kb, ges, G):
    nc = kb.nc
    P = kb.P
    psum, psb = G["psum"], G["psb"]
    ident_b, ones_f, b_const = G["ident_b"], G["ones_f"], G["b_const"]
    dr = kb.dr
    with ExitStack() as es:
        mods = kb.sb(es, "mods1", 6 * KC, F32)
        lbr = kb.sb(es, "lbr", 32, F32)
        lbv = kb.sb(es, "lbv", 16, F32)
        oml = kb.sb(es, "oml", 16, F32)
        noml = kb.sb(es, "noml", 16, F32)
        v128 = kb.sb(es, "v128", 3, F32)
        rt_f = kb.sb(es, "rt_f", 128, F32)
        rt_b = kb.sb(es, "rt_b", 128, BF16)
        smask = kb.sb(es, "smask", MT, F32)
        csc = kb.sb(es, "csc", 3 * 16 * NCH, F32)
        b_c = Buf("p1const")
        b_csc = Buf("csc")
        kb.dma(SP, mods[:], dr["MODS"].ap()[:, 0:6 * KC], [], [b_c])
        kb.dma(SP, lbr[:], dr["lbrawT"].ap(), [], [b_c])
        kb.dma(SP, v128[:], dr["vec128"].ap(), [], [b_c])
        kb.dma(SP, rt_f[:], dr["ropeRT"].ap(), [], [b_c])
        kb.cp(rt_b[:], rt_f[:], [b_c], [b_c])
        kb.tt(lbv[:], lbr[:, 0:16], lbr[:, 16:32], ALU.subtract, [b_c], [b_c])
        kb.act(lbv[:], lbv[:], AF.Sigmoid, [b_c], [b_c])
        kb.ts(oml[:], lbv[:], -1.0, 1.0, ALU.mult, ALU.add, [b_c], [b_c])
        kb.ts(noml[:], lbv[:], 1.0, -1.0, ALU.mult, ALU.add, [b_c], [b_c])
        kb.memset(smask[:], 1.0, [b_c])
        kb.memset(fap(smask, 0, [[64, MT // 64]]), 0.0, [b_c])
        kb.memset(csc[:], 1.0, [b_csc])

        hTs = [(kb.sb(es, "hT%d" % i, KC * MT, BF16), [Buf("hT%d_%d" % (i, k)) for k in range(KC)]) for i in range(2)]
        xs = kb.sbs(es, "xs", D, F32, 2)
        xn = kb.sbs(es, "xn", D, BF16, MT // 128)
        stat = kb.sbs(es, "stat", 4, F32, 2)
        wb = kb.sbs(es, "wb1_", KC * 512, BF16, 2)
        ev = [[(kb.sb(es, "ev%d_%d" % (s, i), MT, F32), Buf()) for i in range(3)] for s in range(2)]
        sh = [(kb.sb(es, "sh%d" % i, MT, F32), Buf()) for i in range(13)]
        stg = {nm: kb.sbs(es, "st_" + nm, MT, BF16, 2) for nm in ("qdf", "kdf", "qdb", "kdb", "g", "kof", "kob")}
        stg_kt = kb.sbs(es, "st_kt", 2 * MT, BF16, 2)
        stg_v = kb.sbs(es, "st_v", 4 * MT, BF16, 1)
        stg_q = kb.sbs(es, "st_q", MT, BF16, 3)
        stg_av = kb.sbs(es, "st_av", 2 * MT, BF16, 1)
        ropeC = kb.sb(es, "ropeC_sb", MT, F32)
        ropeS = kb.sb(es, "ropeS_sb", MT, F32)
        b_rope = Buf("rope")
        xg_t = kb.sbs(es, "xg", MT, BF16, 3)

        tiles = [("c", 0, L)] + [("x", t0, MT) for t0 in range(0, T, MT)]
        pend = []
        cnt = {"u": 0, "q": 0, "s": 0, "r": 0}

        def flush(all_=False):
            keep = []
            for dly, fn in pend:
                if all_ or dly <= 1:
                    fn()
                else:
                    keep.append((dly - 1, fn))
            pend[:] = keep

        def stage_a1(ti):
            kind, t0, n = tiles[ti]
            src = dr["x"] if kind == "x" else dr["ctx"]
            res = []
            for tb in range(n // 128):
                xt, xb_ = xs.next()
                xnt, xnb = xn.next()
                st, stb = stat.next()
                kb.dma(SP, xt[:], src.ap()[t0 + tb * 128: t0 + (tb + 1) * 128, :], [], [xb_])
                kb.act(xnt[:], xt[:], AF.Square, [xb_], [xnb, stb], accum=st[:, 0:1])
                kb.ts(st[:, 1:2], st[:, 0:1], 1.0 / D, EPS, ALU.mult, ALU.add, [stb], [stb])
                kb.act(st[:, 2:3], st[:, 1:2], AF.Ln, [stb], [stb])
                kb.act(st[:, 3:4], st[:, 2:3], AF.Exp, [stb], [stb], scale=-0.5)
                kb.act(xnt[:], xt[:], AF.Identity, [xb_, stb], [xnb], scale=st[:, 3:4])
                res.append((xnt, xnb))
            return res

        def stage_a2(ti, xns):
            kind, t0, n = tiles[ti]
            hT, hTb = hTs[ti % 2]
            a_off = 0 if kind == "x" else 32
            for tb, (xnt, xnb) in enumerate(xns):
                for half in range(2):
                    pt, ptb = psum[6 + half], psb[6 + half]
                    ptv = pt.bitcast(BF16)
                    for j in range(8):
                        kc = half * 8 + j
                        kb.tr(ptv[:, j * 128:(j + 1) * 128], xnt[:, kc * 128:(kc + 1) * 128], ident_b[:],
                              [xnb, b_const], [ptb])
                    for j in range(8):
                        kc = half * 8 + j
                        kb.ts(hT[:, kc * MT + tb * 128: kc * MT + (tb + 1) * 128], ptv[:, j * 128:(j + 1) * 128],
                              mods[:, a_off + kc: a_off + kc + 1], mods[:, a_off + 16 + kc: a_off + 17 + kc],
                              ALU.mult, ALU.add, [ptb, b_c], [hTb[kc]])

        xns0 = stage_a1(0)
        stage_a2(0, xns0)
        for ti, (kind, t0, n) in enumerate(tiles):
            is_x = kind == "x"
            g0 = (L + t0) if is_x else 0
            nblk = n // 128
            hT, hTb = hTs[ti % 2]
            if is_x:
                kb.dma(SP, ropeC[:, 0:n], dr["ropeC"].ap()[:, t0:t0 + n], [], [b_rope])
                kb.dma(SP, ropeS[:, 0:n], dr["ropeS"].ap()[:, t0:t0 + n], [], [b_rope])
            blocks = list(range(21)) if is_x else list(range(8)) + [8, 9, 12]
            k1 = 12 if is_x else 2
            k2 = 16 if is_x else 6
            nxt = None
            for bi, blk in enumerate(blocks):
                wt, wbuf = wb.next()
                kb.dma(POOL, wt[:], dr["w_in_t"].ap()[blk], [], [wbuf])

                def gemmB(ps, pb, col0, wt=wt, wbuf=wbuf):
                    for kc in range(KC):
                        kb.mm(ps[:, 0:n], wt[:, kc * 512 + col0: kc * 512 + col0 + 128], hT[:, kc * MT: kc * MT + n],
                              [wbuf, hTb[kc]], [pb], start=(kc == 0), stop=(kc == KC - 1))

                def gemmA(ps, pb, tb, col0, ncol, wt=wt, wbuf=wbuf):
                    for kc in range(KC):
                        kb.mm(ps[:, 0:ncol], hT[:, kc * MT + tb * 128: kc * MT + (tb + 1) * 128],
                              wt[:, kc * 512 + col0: kc * 512 + col0 + ncol],
                              [wbuf, hTb[kc]], [pb], start=(kc == 0), stop=(kc == KC - 1))

                if blk < 8:
                    h = blk
                    s = cnt["u"] % 2
                    cnt["u"] += 1
                    pq, pf, pbk = psum[3 * s], psum[3 * s + 1], psum[3 * s + 2]
                    bq, bf_, bbk = psb[3 * s], psb[3 * s + 1], psb[3 * s + 2]
                    gemmB(pq, bq, 0)
                    gemmB(pf, bf_, 128)
                    gemmB(pbk, bbk, 256)
                    (qs, qsb), (sf, sfb), (sbw, sbwb) = ev[s]
                    (lf, lfb), (lbk, lbkb), (kf, kfb), (kbk, kbkb), (pfw, pfwb), (pbw, pbwb), (peb, pebb), \
                        (df, dfb), (db, dbb), (e1, e1b), (e3, e3b), (d2, d2b), (e6, e6b) = sh
                    kb.act(qs[:, 0:n], pq[:, 0:n], AF.Silu, [bq], [qsb])
                    kb.act(sf[:, 0:n], pf[:, 0:n], AF.Sigmoid, [bf_], [sfb])
                    kb.act(sbw[:, 0:n], pbk[:, 0:n], AF.Sigmoid, [bbk], [sbwb])
                    if is_x:
                        pg, bg = psum[6], psb[6]
                        gemmB(pg, bg, 384)
                        gt, gb = stg["g"].next()
                        kb.act(gt[:, 0:n], pg[:, 0:n], AF.Silu, [bg], [gb])
                        kb.dma(SP, dr["GT"].ap()[h, :, t0:t0 + n], gt[:, 0:n], [gb], [])
                    flush()
                    nch = n // 64
                    c0 = g0 // 64
                    for di, (sg, sgb, lg, lgb, kk, kkb) in enumerate(((sf, sfb, lf, lfb, kf, kfb),
                                                                     (sbw, sbwb, lbk, lbkb, kbk, kbkb))):
                        hd = di * 8 + h
                        kb.act(lg[:, 0:n], sg[:, 0:n], AF.Ln, [sgb, b_c], [lgb],
                               scale=oml[:, hd:hd + 1], bias=lbv[:, hd:hd + 1])
                        kb.ts(kk[:, 0:n], sg[:, 0:n], noml[:, hd:hd + 1], oml[:, hd:hd + 1], ALU.mult, ALU.add,
                              [sgb, b_c], [kkb])
                    v3 = lambda t_, off: fap(t_, off, [[64, nch], [1, 64]])
                    bc3 = lambda t_, off: fap(t_, off, [[64, nch], [0, 64]])
                    kb.scan(pfw[:, 0:n], smask[:, 0:n], lf[:, 0:n], [b_c, lfb], [pfwb])
                    kb.tt(v3(df, 0), v3(pfw, 0), bc3(pfw, 32), ALU.subtract, [pfwb], [dfb])
                    kb.tt(v3(d2, 0), v3(pfw, 0), bc3(pfw, 63), ALU.subtract, [pfwb], [d2b])
                    kb.scan(pbw[:, 0:n], smask[:, 0:n], lbk[:, 0:n], [b_c, lbkb], [pbwb])
                    kb.tt(peb[:, 0:n], pbw[:, 0:n], lbk[:, 0:n], ALU.subtract, [pbwb, lbkb], [pebb])
                    kb.tt(v3(db, 0), v3(peb, 0), bc3(peb, 31), ALU.subtract, [pebb], [dbb])
                    kb.act(e1[:, 0:n], df[:, 0:n], AF.Exp, [dfb], [e1b])
                    kb.act(df[:, 0:n], df[:, 0:n], AF.Exp, [dfb], [dfb], scale=-1.0)
                    kb.act(e3[:, 0:n], db[:, 0:n], AF.Exp, [dbb], [e3b], scale=-1.0)
                    kb.act(db[:, 0:n], db[:, 0:n], AF.Exp, [dbb], [dbb])
                    kb.act(d2[:, 0:n], d2[:, 0:n], AF.Exp, [d2b], [d2b], scale=-1.0)
                    kb.act(e6[:, 0:n], peb[:, 0:n], AF.Exp, [pebb], [e6b])

                    def cs(k, hd, c):
                        return fap(csc, (k * 16 + hd) * NCH + c, [[1, nch]])
                    hf, hb = h, 8 + h
                    kb.act(cs(0, hf, c0), fap(pfw, 32, [[64, nch]]), AF.Exp, [pfwb], [b_csc])
                    kb.act(cs(2, hf, c0), fap(pfw, 63, [[64, nch]]), AF.Exp, [pfwb], [b_csc])
                    kb.tt(cs(0, hb, c0), fap(pbw, 63, [[64, nch]]), fap(peb, 31, [[64, nch]]), ALU.subtract,
                          [pbwb, pebb], [b_csc])
                    kb.act(cs(0, hb, c0), cs(0, hb, c0), AF.Exp, [b_csc], [b_csc])
                    kb.act(cs(2, hb, c0), fap(pbw, 63, [[64, nch]]), AF.Exp, [pbwb], [b_csc])
                    outs = {}
                    for nm, a_, ab, b_, bb, dst in (("qdf", qs, qsb, e1, e1b, "QDF"), ("kdf", kf, kfb, df, dfb, "KDF"),
                                                    ("qdb", qs, qsb, e3, e3b, "QDB"), ("kdb", kbk, kbkb, db, dbb, "KDB"),
                                                    ("kof", kf, kfb, d2, d2b, None), ("kob", kbk, kbkb, e6, e6b, None)):
                        ot, ob = stg[nm].next()
                        kb.tt(ot[:, 0:n], a_[:, 0:n], b_[:, 0:n], ALU.mult, [ab, bb], [ob])
                        if dst is not None:
                            kb.dma(SP, dr[dst].ap()[h, :, g0:g0 + n], ot[:, 0:n], [ob], [])
                        outs[nm] = (ot, ob)

                    def tail(h=h, n=n, nblk=nblk, g0=g0, kof=outs["kof"], kob=outs["kob"]):
                        pt, ptb = psum[7], psb[7]
                        ptv = pt.bitcast(BF16)
                        kt, ktb = stg_kt.next()
                        for di, (ot, ob) in enumerate((kof, kob)):
                            for tb in range(nblk):
                                kb.tr(ptv[:, (di * nblk + tb) * 128:(di * nblk + tb + 1) * 128],
                                      ot[:, tb * 128:(tb + 1) * 128], ident_b[:], [ob, b_const], [ptb])
                        kb.cp(kt[:, 0:2 * n], ptv[:, 0:2 * n], [ptb], [ktb], eng=ACT)
                        for di, nm in enumerate(("KDFt", "KDBt")):
                            kb.dma(SP, dr[nm].ap()[h, :, (g0 // 128) * 128:(g0 // 128 + nblk) * 128],
                                   kt[:, di * n:(di + 1) * n], [ktb], [])
                    pend.append((1, tail))
                elif blk in (8, 9):
                    vt, vb = stg_v.next()
                    for tb in range(nblk):
                        s = cnt["s"] % 6
                        cnt["s"] += 1
                        gemmA(psum[s], psb[s], tb, 0, 512)
                        kb.cp(fap(vt, tb * 128, [[nblk * 128, 4], [1, 128]]), fap(psum[s], 0, [[128, 4], [1, 128]]),
                              [psb[s]], [vb], eng=ACT)
                        if tb == 0:
                            flush()
                    for h4 in range(4):
                        h = (blk - 8) * 4 + h4
                        kb.dma(SP, dr["VS"].ap()[h, :, (g0 // 128) * 128:(g0 // 128 + nblk) * 128],
                               vt[:, h4 * nblk * 128:(h4 + 1) * nblk * 128], [vb], [])
                elif blk in (10, 11, 12):
                    nh = 4 if blk < 12 else 2
                    for f in range(nh):
                        qi = cnt["q"]
                        cnt["q"] += 1
                        pq, bq = psum[qi % 3], psb[qi % 3]
                        pss, bss = psum[3 + qi % 2], psb[3 + qi % 2]
                        prx, brx = psum[5 + qi % 2], psb[5 + qi % 2]
                        (sq, sqb) = ev[qi % 2][0]
                        (rs, rsb) = ev[qi % 2][1]
                        (t1, t1b), (t2, t2b) = sh[0], sh[1]
                        gemmB(pq, bq, f * 128)
                        gcol = 1 if blk < 12 else 2
                        kb.act(sq[:, 0:n], pq[:, 0:n], AF.Square, [bq], [sqb])
                        flush()
                        ot, ob = stg_q.next()
                        xg, xgb = xg_t.next()
                        if blk < 12:
                            dst = dr["AQT"].ap()[(blk - 10) * 4 + f, :, t0:t0 + n]
                        else:
                            dst = dr["AKT"].ap()[f, :, g0:g0 + n]

                        def tail1(n=n, pq=pq, bq=bq, pss=pss, bss=bss, sq=sq, sqb=sqb, rs=rs, rsb=rsb, xg=xg, xgb=xgb,
                                  ot=ot, ob=ob, gcol=gcol, is_x=is_x, dst=dst):
                            kb.mm(pss[:, 0:n], ones_f[:], sq[:, 0:n], [b_const, sqb], [bss])
                            kb.act(rs[:, 0:n], pss[:, 0:n], AF.Ln, [bss], [rsb], scale=1.0 / 128, bias=EPS)
                            kb.act(rs[:, 0:n], rs[:, 0:n], AF.Exp, [rsb], [rsb], scale=-0.5)
                            if is_x:
                                kb.stt(xg[:, 0:n], pq[:, 0:n], v128[:, gcol:gcol + 1], rs[:, 0:n], ALU.mult, ALU.mult,
                                       [bq, b_c, rsb], [xgb])
                            else:
                                kb.stt(ot[:, 0:n], pq[:, 0:n], v128[:, gcol:gcol + 1], rs[:, 0:n], ALU.mult, ALU.mult,
                                       [bq, b_c, rsb], [ob])
                                kb.dma(SP, dst, ot[:, 0:n], [ob], [])

                        def tail2(n=n, prx=prx, brx=brx, xg=xg, xgb=xgb, ot=ot, ob=ob, t1=t1, t1b=t1b, t2=t2, t2b=t2b,
                                  dst=dst):
                            kb.mm(prx[:, 0:n], rt_b[:], xg[:, 0:n], [b_c, xgb], [brx])
                            kb.tt(t1[:, 0:n], xg[:, 0:n], ropeC[:, 0:n], ALU.mult, [xgb, b_rope], [t1b])
                            kb.tt(t2[:, 0:n], prx[:, 0:n], ropeS[:, 0:n], ALU.mult, [brx, b_rope], [t2b])
                            kb.tt(ot[:, 0:n], t1[:, 0:n], t2[:, 0:n], ALU.add, [t1b, t2b], [ob])
                            kb.dma(SP, dst, ot[:, 0:n], [ob], [])
                        pend.append((1, tail1))
                        if is_x:
                            pend.append((2, tail2))
                    if blk == 12:
                        vt, vb = stg_av.next()
                        for tb in range(nblk):
                            s = 3 + cnt["s"] % 4
                            cnt["s"] += 1
                            gemmA(psum[s], psb[s], tb, 256, 256)
                            kb.cp(fap(vt, tb * 128, [[nblk * 128, 2], [1, 128]]),
                                  fap(psum[s], 0, [[128, 2], [1, 128]]), [psb[s]], [vb], eng=ACT)
                            flush()
                        for kv in range(2):
                            kb.dma(SP, dr["AVS"].ap()[kv, :, (g0 // 128) * 128:(g0 // 128 + nblk) * 128],
                                   vt[:, kv * nblk * 128:(kv + 1) * nblk * 128], [vb], [])
                else:
                    for f in range(4):
                        s = 3 + cnt["s"] % 4
                        cnt["s"] += 1
                        gemmB(psum[s], psb[s], f * 128)
                        ot, ob = stg_q.next()
                        kb.act(ot[:, 0:n], psum[s][:, 0:n], AF.Sigmoid, [psb[s]], [ob])
                        flush()
                        fb = (blk - 13) * 4 + f
                        nm = "SGR" if fb < 16 else "SGA"
                        kb.dma(SP, dr[nm].ap()[fb % 16, :, t0:t0 + n], ot[:, 0:n], [ob], [])
                if ti + 1 < len(tiles):
                    if bi == k1:
                        nxt = stage_a1(ti + 1)
                    if bi == k2:
                        stage_a2(ti + 1, nxt)
            flush(all_=True)
        kb.dma(SP, dr["CSC"].ap(), csc[:], [b_csc], [])
    P.barrier()


def phase2a(kb, ges, G):
    nc = kb.nc
    P = kb.P
    psum, psb = G["psum"], G["psb"]
    scr_b = G["scr_b"]
    ones_f, b_const = G["ones_f"], G["b_const"]
    dr = kb.dr
    W = NPAIR * 128
    with ExitStack() as es:
        mk_f = kb.sb(es, "mk_f2", 512, F32)
        mk = kb.sb(es, "mk2", 512, BF16)
        v128 = kb.sb(es, "v128_2", 3, F32)
        b_c = Buf("p2const")
        kb.dma(SP, mk_f[:], dr["masks"].ap(), [], [b_c])
        kb.cp(mk[:], mk_f[:], [b_c], [b_c])
        kb.dma(SP, v128[:], dr["vec128"].ap(), [], [b_c])
        names = ("QDF", "KDF", "QDB", "KDB", "KDFt", "KDBt")
        sets = []
        for s in range(2):
            d = {nm: (kb.sb(es, "h%s%d" % (nm, s), TL, BF16), Buf()) for nm in names}
            d["VX"] = (kb.sb(es, "hVX%d" % s, 2 * W, BF16), Buf())
            d["csc"] = (kb.sb(es, "hcsc%d" % s, 3 * 2 * NCH, F32), Buf())
            kb.memset(d["VX"][0][:], 0.0, [d["VX"][1]])
            sets.append(d)
        GTt = kb.sb(es, "hGT", T, BF16)
        GTb = Buf()
        oacc = kb.sb(es, "oacc", T, F32)
        oab = [Buf("oacc%d" % g) for g in range(8)]
        yst = kb.sb(es, "yst", T, BF16)
        yb = Buf("yst")
        S32 = [kb.sbs(es, "S32_%d" % di, 128, F32, 3) for di in range(2)]
        S16 = [kb.sbs(es, "S16_%d" % di, 128, BF16, 2) for di in range(2)]
        IT = [kb.sbs(es, "IT_%d" % di, 128, BF16, 3) for di in range(2)]
        ntm = [(kb.sb(es, "ntm%d" % i, 512, F32), Buf()) for i in range(3)]
        CS = 3 * 16 * NCH

        def load_head(h, st):
            for nm in names:
                t_, b_ = st[nm]
                kb.dma(SP, t_[:], dr[nm].ap()[h], [], [b_])
            t_, b_ = st["VX"]
            for half in range(2):
                kb.dma(SP, fap(t_, half * 128, [[256, NPAIR], [1, 128]], p0=half * 64, npart=64),
                       bass.AP(dr["VS"], (h * 128 + half * 64) * W, [[W, 64], [128, NPAIR], [1, 128]]), [], [b_])
            t_, b_ = st["csc"]
            kb.dma(SP, fap(t_, 0, [[2 * NCH, 3], [NCH, 2], [1, NCH]]),
                   bass.AP(dr["CSC"], h * NCH, [[CS, 128], [16 * NCH, 3], [8 * NCH, 2], [1, NCH]]), [], [b_])

        load_head(0, sets[0])
        for h in range(8):
            st = sets[h % 2]
            if h + 1 < 8:
                load_head(h + 1, sets[(h + 1) % 2])
            kb.dma(SP, GTt[:], dr["GT"].ap()[h], [], [GTb])
            csc_t, csc_b = st["csc"]
            VX_t, VX_b = st["VX"]
            order = [list(range(NPAIR)), [1, 0] + list(range(NPAIR - 1, 1, -1))]
            cur = [None, None]
            for di in range(2):
                s0, s0b = S32[di].next()
                kb.memset(s0[:], 0.0, [s0b])
                cur[di] = (s0, s0b)
            written = [False] * 8
            ucnt = [0, 0]
            its = [[None, None] for _ in range(NPAIR)]
            ups_l = [[None, None] for _ in range(NPAIR)]

            def part_a(step):
                for di in range(2):
                    p = order[di][step]
                    QD_t, QD_b = st["QDF" if di == 0 else "QDB"]
                    KD_t, KD_b = st["KDF" if di == 0 else "KDB"]
                    if p >= 2:
                        ips, ipb = psum[2 + di], psb[2 + di]
                        kb.mm(ips[:, 0:128], KD_t[:, p * 128:(p + 1) * 128], QD_t[:, p * 128:(p + 1) * 128],
                              [KD_b, QD_b], [ipb])
                        it_t, it_b = IT[di].next()
                        kb.tt(it_t[:], ips[:, 0:128], mk[:, di * 128:(di + 1) * 128], ALU.mult, [ipb, b_c], [it_b])
                        its[step][di] = (it_t, it_b)
                for di in range(2):
                    p = order[di][step]
                    Kt_t, Kt_b = st["KDFt" if di == 0 else "KDBt"]
                    ui = 4 + 2 * di + ucnt[di] % 2
                    ucnt[di] += 1
                    kb.mm(psum[ui][:, 0:256], Kt_t[:, p * 128:(p + 1) * 128], VX_t[:, p * 256:(p + 1) * 256],
                          [Kt_b, VX_b], [psb[ui]])
                    ups_l[step][di] = ui

            def part_b(step):
                for di in range(2):
                    p = order[di][step]
                    QD_t, QD_b = st["QDF" if di == 0 else "QDB"]
                    is_x = p >= 2
                    oT, oTb = psum[di], psb[di]
                    ui = ups_l[step][di]
                    if is_x:
                        g = (p - 2) // 4
                        slot = (p - 2) % 4
                        it_t, it_b = its[step][di]
                    chunks = (2 * p, 2 * p + 1) if di == 0 else (2 * p + 1, 2 * p)
                    for ci, ch in enumerate(chunks):
                        s_t, s_b = cur[di]
                        half = ch % 2
                        if is_x:
                            oc = slot * 128 + half * 64
                            kb.mm(oT[:, oc:oc + 64], VX_t[:, p * 256 + half * 128: p * 256 + (half + 1) * 128],
                                  it_t[:, half * 64:(half + 1) * 64], [VX_b, it_b], [oTb], start=True, stop=False)
                            s16, s16b = S16[di].next()
                            kb.act(s16[:], s_t[:], AF.Identity, [s_b, csc_b], [s16b],
                                   scale=csc_t[:, (0 * 2 + di) * NCH + ch:(0 * 2 + di) * NCH + ch + 1])
                            kb.mm(oT[:, oc:oc + 64], s16[:], QD_t[:, ch * 64:(ch + 1) * 64], [s16b, QD_b], [oTb],
                                  start=False, stop=True)
                        n_t, n_b = S32[di].next()
                        kb.stt(n_t[:], s_t[:], csc_t[:, (2 * 2 + di) * NCH + ch:(2 * 2 + di) * NCH + ch + 1],
                               psum[ui][:, half * 128:(half + 1) * 128], ALU.mult, ALU.add,
                               [s_b, csc_b, psb[ui]], [n_b])
                        cur[di] = (n_t, n_b)
                    if is_x and ((di == 0 and slot == 3) or (di == 1 and slot == 0)):
                        if not written[g]:
                            kb.cp(oacc[:, g * 512:(g + 1) * 512], oT[:], [oTb], [oab[g]], eng=ACT)
                            written[g] = True
                        else:
                            kb.tt(oacc[:, g * 512:(g + 1) * 512], oT[:], oacc[:, g * 512:(g + 1) * 512], ALU.add,
                                  [oTb, oab[g]], [oab[g]])

            part_a(0)
            for step in range(NPAIR):
                if step + 1 < NPAIR:
                    part_a(step + 1)
                part_b(step)
            for g in range(8):
                (sq, sqb), (rs, rsb), (y1, y1b) = ntm
                pss, pssb = psum[2 + g % 2], psb[2 + g % 2]
                sl = slice(g * 512, (g + 1) * 512)
                kb.act(sq[:], oacc[:, sl], AF.Square, [oab[g]], [sqb])
                kb.mm(pss[:], ones_f[:], sq[:], [b_const, sqb], [pssb])
                kb.act(rs[:], pss[:], AF.Ln, [pssb], [rsb], scale=1.0 / 128, bias=EPS)
                kb.act(rs[:], rs[:], AF.Exp, [rsb], [rsb], scale=-0.5)
                kb.stt(y1[:], oacc[:, sl], v128[:, 0:1], rs[:], ALU.mult, ALU.mult, [oab[g], b_c, rsb], [y1b])
                kb.tt(yst[:, sl], y1[:], GTt[:, sl], ALU.mult, [y1b, GTb], [yb])
            kb.dma(SP, dr["YREC"].ap()[h], yst[:], [yb], [])
    P.barrier()


def phase2b(kb, ges, G):
    nc = kb.nc
    P = kb.P
    psum, psb = G["psum"], G["psb"]
    scr_b = G["scr_b"]
    dr = kb.dr
    W = NPAIR * 128
    NQB = T // 128
    SCALE = 128 ** -0.5
    with ExitStack() as es:
        mk_f = kb.sb(es, "mk_f3", 512, F32)
        mk = kb.sb(es, "mk3", 512, BF16)
        ones_b = kb.sb(es, "ones_b", 128, BF16)
        esink = kb.sb(es, "esink", 8, F32)
        b_c = Buf("p2bconst")
        kb.dma(SP, mk_f[:], dr["masks"].ap(), [], [b_c])
        kb.cp(mk[:], mk_f[:], [b_c], [b_c])
        kb.memset(ones_b[:], 1.0, [b_c])
        kb.dma(SP, esink[:], dr["sink_bc"].ap(), [], [b_c])
        kb.act(esink[:], esink[:], AF.Exp, [b_c], [b_c])
        ak = kb.sb(es, "ak", TL, BF16)
        av = kb.sb(es, "av", W, BF16)
        aq = kb.sb(es, "aq", 4 * T, BF16)
        ya = kb.sb(es, "ya", 4 * T, BF16)
        b_in = Buf("attin")
        b_ya = Buf("ya")
        pT = kb.sbs(es, "pT", 512, BF16, 6)
        den = kb.sbs(es, "den", 512, F32, 2)
        for kv in range(2):
            kb.dma(SP, ak[:], dr["AKT"].ap()[kv], [scr_b["AKT"]], [b_in])
            kb.dma(SP, av[:], dr["AVS"].ap()[kv], [scr_b["AVS"]], [b_in])
            for h4 in range(4):
                kb.dma(SP, aq[:, h4 * T:(h4 + 1) * T], dr["AQT"].ap()[kv * 4 + h4], [scr_b["AQT"]], [b_in])
            items = []
            for qb in range(NQB):
                kblocks = [(0, None), (1, None)]
                if qb > 0:
                    kblocks.append((2 + qb - 1, 2))
                kblocks.append((2 + qb, None))
                if qb < NQB - 1:
                    kblocks.append((2 + qb + 1, 3))
                for i, (kblk, mi) in enumerate(kblocks):
                    items.append((qb, kblk, mi, i == 0, i == len(kblocks) - 1))
            pts = [None] * len(items)

            def score(ix):
                qb, kblk, mi, first, last = items[ix]
                s_ps, s_b = psum[ix % 4], psb[ix % 4]
                kb.mm(s_ps[:], ak[:, kblk * 128:(kblk + 1) * 128], fap(aq, qb * 128, [[T, 4], [1, 128]]),
                      [b_in], [s_b])
                p_t, p_b = pT.next()
                kb.act(p_t[:], s_ps[:], AF.Exp, [s_b], [p_b], scale=SCALE)
                if mi is not None:
                    kb.tt(fap(p_t, 0, [[128, 4], [1, 128]]), fap(p_t, 0, [[128, 4], [1, 128]]),
                          fap(mk, mi * 128, [[0, 4], [1, 128]]), ALU.mult, [p_b, b_c], [p_b])
                pts[ix] = (p_t, p_b)

            LOOK = 3
            for ix in range(min(LOOK, len(items))):
                score(ix)
            for ix in range(len(items)):
                if ix + LOOK < len(items):
                    score(ix + LOOK)
                qb, kblk, mi, first, last = items[ix]
                o_ps, o_b = psum[4 + qb % 2], psb[4 + qb % 2]
                d_ps, d_b = psum[6 + qb % 2], psb[6 + qb % 2]
                p_t, p_b = pts[ix]
                kb.mm(o_ps[:], av[:, kblk * 128:(kblk + 1) * 128], p_t[:], [b_in, p_b], [o_b], start=first, stop=last)
                kb.mm(d_ps[:], ones_b[:], p_t[:], [b_c, p_b], [d_b], start=first, stop=last)
                if last:
                    dn_t, dn_b = den.next()
                    kb.tt(fap(dn_t, 0, [[128, 4], [1, 128]]), fap(d_ps, 0, [[128, 4], [1, 128]]),
                          fap(esink, kv * 4, [[1, 4], [0, 128]]), ALU.add, [d_b, b_c], [dn_b])
                    P.add(DVE, lambda e, o=dn_t: e.reciprocal(out=o[:], in_=o[:]), [dn_b], [dn_b])
                    kb.tt(fap(ya, qb * 128, [[T, 4], [1, 128]]), fap(o_ps, 0, [[128, 4], [1, 128]]),
                          fap(dn_t, 0, [[128, 4], [1, 128]]), ALU.mult, [o_b, dn_b], [b_ya])
            for h4 in range(4):
                kb.dma(SP, dr["YATT"].ap()[kv * 4 + h4], ya[:, h4 * T:(h4 + 1) * T], [b_ya], [scr_b["YATT"]])
    P.barrier()


MT3 = 512


def phase3(kb, ges, G):
    nc = kb.nc
    P = kb.P
    psum, psb = G["psum"], G["psb"]
    ident_b, b_const = G["ident_b"], G["b_const"]
    out_d = G["out_d"]
    dr = kb.dr
    n = MT3
    nblk = n // 128
    with ExitStack() as es:
        mods = kb.sb(es, "mods3", 6 * KC, F32)
        gate1 = kb.sb(es, "gate1", D, F32)
        b_c = Buf("p3const")
        kb.dma(SP, mods[:], dr["MODS"].ap()[:, 0:6 * KC], [], [b_c])
        kb.dma(SP, gate1[:], dr["MODS"].ap()[:, 6 * KC:6 * KC + D], [], [b_c])
        yr = kb.sb(es, "yr", 8 * n, BF16)
        ya = kb.sb(es, "ya3", 8 * n, BF16)
        b_yr, b_ya = Buf(), Buf()
        sgr = kb.sbs(es, "sgr", 4 * n, BF16, 2)
        sga = kb.sbs(es, "sga", 4 * n, BF16, 2)
        zT = kb.sb(es, "zT", 16 * n, BF16)
        zb = [Buf() for _ in range(16)]
        xm = [(kb.sb(es, "xm%d" % i, D, F32), Buf()) for i in range(nblk)]
        w8 = kb.sbs(es, "w8_", 8 * 512, BF16, 4)
        w16 = kb.sbs(es, "w16_", KC * 512, BF16, 2)
        tmp = kb.sbs(es, "t3_", n, F32, 4)
        xn = kb.sbs(es, "xn3_", D, BF16, nblk)
        stat = kb.sbs(es, "stat3_", 4, F32, 2)
        h2s = kb.sbs(es, "h2s", KC * 128, BF16, 2)
        unit = 0
        pend = []
        for t0 in range(0, T, n):
            kb.dma(SP, fap(yr, 0, [[n, 8], [1, n]]), bass.AP(dr["YREC"], t0, [[T, 128], [128 * T, 8], [1, n]]),
                   [], [b_yr])
            kb.dma(SP, fap(ya, 0, [[n, 8], [1, n]]), bass.AP(dr["YATT"], t0, [[T, 128], [128 * T, 8], [1, n]]),
                   [], [b_ya])
            for tb in range(nblk):
                kb.dma(SP, xm[tb][0][:], dr["x"].ap()[t0 + tb * 128:t0 + (tb + 1) * 128, :], [], [xm[tb][1]])
            for nb in range(4):
                wr, wrb = w8.next()
                wa, wab = w8.next()
                kb.dma(POOL, wr[:], dr["w_rec_t"].ap()[nb], [], [wrb])
                kb.dma(POOL, wa[:], dr["w_att_t"].ap()[nb], [], [wab])
                gr, grb = sgr.next()
                ga, gab = sga.next()
                kb.dma(SP, fap(gr, 0, [[n, 4], [1, n]]),
                       bass.AP(dr["SGR"], nb * 4 * 128 * T + t0, [[T, 128], [128 * T, 4], [1, n]]), [], [grb])
                kb.dma(SP, fap(ga, 0, [[n, 4], [1, n]]),
                       bass.AP(dr["SGA"], nb * 4 * 128 * T + t0, [[T, 128], [128 * T, 4], [1, n]]), [], [gab])
                for f in range(4):
                    fb = nb * 4 + f
                    s = unit % 3
                    unit += 1
                    pa, pab = psum[2 * s], psb[2 * s]
                    pb_, pbb = psum[2 * s + 1], psb[2 * s + 1]
                    for kc in range(8):
                        kb.mm(pa[:, 0:n], wr[:, kc * 512 + f * 128: kc * 512 + (f + 1) * 128], yr[:, kc * n:(kc + 1) * n],
                              [wrb, b_yr], [pab], start=(kc == 0), stop=(kc == 7))
                    for kc in range(8):
                        kb.mm(pb_[:, 0:n], wa[:, kc * 512 + f * 128: kc * 512 + (f + 1) * 128], ya[:, kc * n:(kc + 1) * n],
                              [wab, b_ya], [pbb], start=(kc == 0), stop=(kc == 7))
                    t1, t1b = tmp.next()
                    t2, t2b = tmp.next()
                    kb.tt(t1[:], pa[:, 0:n], gr[:, f * n:(f + 1) * n], ALU.mult, [pab, grb], [t1b])
                    kb.tt(t2[:], pb_[:, 0:n], ga[:, f * n:(f + 1) * n], ALU.mult, [pbb, gab], [t2b])
                    kb.tt(zT[:, fb * n:(fb + 1) * n], t1[:], t2[:], ALU.add, [t1b, t2b], [zb[fb]])
                if nb == 0:
                    for fn in pend:
                        fn()
                    pend = []
            for nb2 in range(4):
                wo, wob = w16.next()
                kb.dma(POOL, wo[:], dr["w_out_t"].ap()[nb2], [], [wob])
                for tb in range(nblk):
                    s = 6 + unit % 2
                    unit += 1
                    pm, pmb = psum[s], psb[s]
                    for fb in range(16):
                        kb.mm(pm[:], zT[:, fb * n + tb * 128: fb * n + (tb + 1) * 128], wo[:, fb * 512:(fb + 1) * 512],
                              [zb[fb], wob], [pmb], start=(fb == 0), stop=(fb == 15))
                    t1, t1b = tmp.next()
                    cs = slice(nb2 * 512, (nb2 + 1) * 512)
                    kb.tt(t1[:], pm[:], gate1[:, cs], ALU.mult, [pmb, b_c], [t1b])
                    kb.tt(xm[tb][0][:, cs], t1[:], xm[tb][0][:, cs], ALU.add, [t1b, xm[tb][1]], [xm[tb][1]])
            for tb in range(nblk):
                xt, xb_ = xm[tb]
                r0 = t0 + tb * 128
                kb.dma(SP, out_d.ap()[r0:r0 + 128, :], xt[:], [xb_], [])
                st, stb = stat.next()
                xnt, xnb = xn.next()
                kb.act(xnt[:], xt[:], AF.Square, [xb_], [xnb, stb], accum=st[:, 0:1])
                kb.ts(st[:, 1:2], st[:, 0:1], 1.0 / D, EPS, ALU.mult, ALU.add, [stb], [stb])
                kb.act(st[:, 2:3], st[:, 1:2], AF.Ln, [stb], [stb])
                kb.act(st[:, 3:4], st[:, 2:3], AF.Exp, [stb], [stb], scale=-0.5)
                kb.act(xnt[:], xt[:], AF.Identity, [xb_, stb], [xnb], scale=st[:, 3:4])

                def part2(xnt=xnt, xnb=xnb, r0=r0):
                    h2, h2b = h2s.next()
                    for half in range(2):
                        pt, ptb = psum[6 + half], psb[6 + half]
                        ptv = pt.bitcast(BF16)
                        for j in range(8):
                            kc = half * 8 + j
                            kb.tr(ptv[:, j * 128:(j + 1) * 128], xnt[:, kc * 128:(kc + 1) * 128], ident_b[:],
                                  [xnb, b_const], [ptb])
                        for j in range(8):
                            kc = half * 8 + j
                            kb.ts(h2[:, kc * 128:(kc + 1) * 128], ptv[:, j * 128:(j + 1) * 128],
                                  mods[:, 64 + kc:65 + kc], mods[:, 80 + kc:81 + kc], ALU.mult, ALU.add,
                                  [ptb, b_c], [h2b])
                    kb.dma(SP, bass.AP(dr["H2T"], r0, [[T, 128], [128 * T, KC], [1, 128]]),
                           fap(h2, 0, [[128, KC], [1, 128]]), [h2b], [])
                pend.append(part2)
        for fn in pend:
            fn()
    P.barrier()


MT4 = 512


def phase4(kb, ges, G):
    nc = kb.nc
    P = kb.P
    psum, psb = G["psum"], G["psb"]
    scr_b = G["scr_b"]
    out_d, out_b = G["out_d"], G["out_b"]
    dr = kb.dr
    n = MT4
    nblk = n // 128
    HW = n + 2
    with ExitStack() as es:
        gate2 = kb.sb(es, "gate2", D, F32)
        cw = kb.sb(es, "cw", 3 * NFB, F32)
        cb = kb.sb(es, "cb", NFB, F32)
        b_c = Buf("p4const")
        kb.dma(SP, gate2[:], dr["MODS"].ap()[:, 6 * KC + D:6 * KC + 2 * D], [scr_b["MODS"]], [b_c])
        kb.dma(SP, cw[:], dr["convwT"].ap(), [], [b_c])
        kb.dma(SP, cb[:], dr["convbT"].ap(), [], [b_c])
        h2r = kb.sbs(es, "h2r", KC * HW, BF16, 2)
        acc = [[(kb.sb(es, "acc%d_%d" % (j, i), D, F32), Buf()) for i in range(nblk)] for j in range(2)]
        actT = kb.sbs(es, "actT", 11 * n, BF16, 2)
        wup = kb.sbs(es, "wup", KC * 256, BF16, 3)
        wdn = kb.sbs(es, "wdn", 11 * 512, BF16, 2)
        A_sb = kb.sbs(es, "A_sb", HW, F32, 2)
        c_sb = kb.sbs(es, "c_sb", n, F32, 2)
        s_sb = kb.sbs(es, "s_sb", n, F32, 2)
        xmr = kb.sbs(es, "xmr", D, F32, 2)
        unit = 0
        prev_epi = None
        for t0 in range(0, T, n):
            h2, h2b = h2r.next()
            lo = 1 if t0 == 0 else 0
            hi = HW - 1 if t0 + n == T else HW
            if lo:
                kb.memset(fap(h2, 0, [[HW, KC]]), 0.0, [h2b])
            if hi < HW:
                kb.memset(fap(h2, HW - 1, [[HW, KC]]), 0.0, [h2b])
            kb.dma(SP, fap(h2, lo, [[HW, KC], [1, hi - lo]]),
                   bass.AP(dr["H2T"], t0 - 1 + lo, [[T, 128], [128 * T, KC], [1, hi - lo]]), [scr_b["H2T"]], [h2b])
            for g in range(4):
                if g == 1 and prev_epi is not None:
                    prev_epi()
                    prev_epi = None
                at, atb = actT.next()
                for fi in range(11):
                    fb = g * 11 + fi
                    wu, wub = wup.next()
                    kb.dma(POOL, wu[:], dr["w_up_t"].ap()[fb], [], [wub])
                    s = unit % 2
                    unit += 1
                    pa, pab = psum[3 * s], psb[3 * s]
                    ph, phb = psum[3 * s + 1], psb[3 * s + 1]
                    pu, pub = psum[3 * s + 2], psb[3 * s + 2]
                    for kc in range(KC):
                        kb.mm(pa[:], wu[:, kc * 256: kc * 256 + 128], h2[:, kc * HW + 1: kc * HW + 1 + n],
                              [wub, h2b], [pab], start=(kc == 0), stop=(kc == KC - 1))
                    for kc in range(KC):
                        kb.mm(ph[:, 0:2], wu[:, kc * 256: kc * 256 + 128], fap(h2, kc * HW, [[HW - 1, 2]]),
                              [wub, h2b], [phb], start=(kc == 0), stop=(kc == KC - 1))
                    for kc in range(KC):
                        kb.mm(pu[:], wu[:, kc * 256 + 128: kc * 256 + 256], h2[:, kc * HW + 1: kc * HW + 1 + n],
                              [wub, h2b], [pub], start=(kc == 0), stop=(kc == KC - 1))
                    A, Ab = A_sb.next()
                    c, cbb = c_sb.next()
                    sl, slb = s_sb.next()
                    kb.cp(A[:, 1:1 + n], pa[:], [pab], [Ab], eng=ACT)
                    kb.cp(fap(A, 0, [[HW - 1, 2]]), ph[:, 0:2], [phb], [Ab], eng=ACT)
                    kb.ts(c[:], A[:, 0:n], cw[:, fb:fb + 1], cb[:, fb:fb + 1], ALU.mult, ALU.add, [Ab, b_c], [cbb])
                    kb.stt(c[:], A[:, 1:1 + n], cw[:, NFB + fb:NFB + fb + 1], c[:], ALU.mult, ALU.add,
                           [Ab, b_c, cbb], [cbb])
                    kb.stt(c[:], A[:, 2:2 + n], cw[:, 2 * NFB + fb:2 * NFB + fb + 1], c[:], ALU.mult, ALU.add,
                           [Ab, b_c, cbb], [cbb])
                    kb.act(sl[:], c[:], AF.Silu, [cbb], [slb])
                    kb.tt(at[:, fi * n:(fi + 1) * n], pu[:], sl[:], ALU.mult, [pub, slb], [atb])
                for nb in range(4):
                    wd, wdb = wdn.next()
                    kb.dma(POOL, wd[:], dr["w_down_t"].ap()[g, nb], [], [wdb])
                    for tb in range(nblk):
                        s = 6 + unit % 2
                        unit += 1
                        pd, pdb = psum[s], psb[s]
                        for fi in range(11):
                            kb.mm(pd[:], at[:, fi * n + tb * 128: fi * n + (tb + 1) * 128], wd[:, fi * 512:(fi + 1) * 512],
                                  [atb, wdb], [pdb], start=(fi == 0), stop=(fi == 10))
                        a_t, a_b = acc[(t0 // n) % 2][tb]
                        cs = slice(nb * 512, (nb + 1) * 512)
                        if g == 0:
                            kb.cp(a_t[:, cs], pd[:], [pdb], [a_b], eng=ACT)
                        else:
                            kb.tt(a_t[:, cs], pd[:], a_t[:, cs], ALU.add, [pdb, a_b], [a_b])
            def epi(t0=t0):
                for tb in range(nblk):
                    a_t, a_b = acc[(t0 // n) % 2][tb]
                    r0 = t0 + tb * 128
                    xr, xrb = xmr.next()
                    kb.dma(SP, xr[:], out_d.ap()[r0:r0 + 128, :], [], [xrb])
                    kb.tt(a_t[:], a_t[:], gate2[:], ALU.mult, [a_b, b_c], [a_b])
                    kb.tt(xr[:], a_t[:], xr[:], ALU.add, [a_b, xrb], [xrb])
                    kb.dma(SP, out_d.ap()[r0:r0 + 128, :], xr[:], [xrb], [])
            prev_epi = epi
        if prev_epi is not None:
            prev_epi()


def tile_w(w, nb=512):
    K, N = w.shape
    return np.ascontiguousarray(w.reshape(K // 128, 128, N // nb, nb).transpose(2, 1, 0, 3)).reshape(
        N // nb, 128, (K // 128) * nb)


def featT(v):
    return np.ascontiguousarray(v.reshape(-1, 128).T)


def host_consts():
    ident = np.eye(128, dtype=np.float32)
    R = np.zeros((128, 128), np.float32)
    for m in range(128):
        if (m % 64) < 32:
            R[m, m + 32] = -1.0
        else:
            R[m, m - 32] = 1.0
    RT = np.ascontiguousarray(R.T)
    s = np.arange(128)[:, None]
    c = np.arange(128)[None, :]
    same = (s // 64) == (c // 64)
    mask_f = (same & (s <= c)).astype(np.float32)
    mask_b = (same & (s >= c)).astype(np.float32)
    am_prev = (s >= c).astype(np.float32)
    am_next = (s <= c).astype(np.float32)
    masks = np.concatenate([mask_f, mask_b, am_prev, am_next], axis=1)
    t = np.arange(T)
    rows = (t // 64).astype(np.float32)
    cols = (t % 64).astype(np.float32)
    inv_freq = (10000.0 ** (-np.arange(32, dtype=np.float32) / 32)).astype(np.float32)
    C = np.zeros((128, T), np.float32)
    S = np.zeros((128, T), np.float32)
    for d in range(128):
        pos = rows if d < 64 else cols
        ang = (pos * inv_freq[d % 32]).astype(np.float32)
        C[d] = np.cos(ang)
        S[d] = np.sin(ang)
    return ident, RT, masks, C, S


_CACHE = {}


def prep(x, c, ctx, c_ctx, w_ada, b_ada, norm1_g, w_in, hgrn_lower_bounds, hgrn_norm_g, q_norm_g,
         k_norm_g, attn_sink, w_rec_proj, w_att_proj, w_out, norm2_g, w_up, conv_w, conv_b, w_down):
    f = lambda a: np.asarray(a, dtype=np.float32)
    x, c, ctx, c_ctx = f(x), f(c), f(ctx), f(c_ctx)
    B = x.shape[0]
    ident, RT, masks, C, S = host_consts()
    w_in0 = f(w_in)[0]
    cols = []
    for h in range(8):
        for seg in (0, 1, 2, 4):
            cols += list(range(seg * 1024 + h * 128, seg * 1024 + (h + 1) * 128))
    cols += list(range(3072, 4096)) + list(range(5120, 6144)) + list(range(6144, 6656)) + list(range(6656, 10752))
    w_in_t = tile_w(w_in0[:, cols])
    w_ada_t = tile_w(f(w_ada)[0])
    b_ada0 = f(b_ada)[0]
    b_adaT = featT(b_ada0)
    b_gate = np.concatenate([b_ada0[2 * D:3 * D], b_ada0[5 * D:6 * D]])
    b_gate_bc = np.ascontiguousarray(np.broadcast_to(b_gate[None, :], (128, 2 * D)))
    w_up0 = f(w_up)[0]
    ucols = []
    for fb in range(NFB):
        ucols += list(range(fb * 128, (fb + 1) * 128)) + list(range(DFF + fb * 128, DFF + (fb + 1) * 128))
    w_up_t = tile_w(w_up0[:, ucols], 256)
    w_down0 = f(w_down)[0]
    w_down_t = np.ascontiguousarray(w_down0.reshape(4, 11, 128, 4, 512).transpose(0, 3, 2, 1, 4)).reshape(
        4, 4, 128, 11 * 512)
    lbraw = f(hgrn_lower_bounds)
    lbrawT = np.ascontiguousarray(lbraw.reshape(2, 2, 8, 128).transpose(3, 0, 1, 2)).reshape(128, 32)
    vec128 = np.stack([f(hgrn_norm_g)[0], f(q_norm_g)[0], f(k_norm_g)[0]], axis=1)
    sink_bc = np.ascontiguousarray(np.broadcast_to(f(attn_sink)[0][None, :], (128, 8)))
    cw = f(conv_w)[0]
    convwT = np.ascontiguousarray(cw.reshape(3, NFB, 128).transpose(2, 0, 1)).reshape(128, 3 * NFB)
    convbT = featT(f(conv_b)[0])
    shared = {
        "w_ada_t": w_ada_t, "b_adaT": b_adaT, "b_gate_bc": b_gate_bc, "g1T": featT(f(norm1_g)[0]),
        "g2T": featT(f(norm2_g)[0]), "w_in_t": w_in_t, "lbrawT": lbrawT, "vec128": np.ascontiguousarray(vec128),
        "sink_bc": sink_bc, "w_rec_t": tile_w(f(w_rec_proj)[0]), "w_att_t": tile_w(f(w_att_proj)[0]),
        "w_out_t": tile_w(f(w_out)[0]), "w_up_t": w_up_t, "convwT": convwT, "convbT": convbT,
        "w_down_t": w_down_t, "ident": ident, "ropeRT": RT, "masks": masks, "ropeC": C, "ropeS": S,
    }
    in_maps = []
    for b in range(B):
        cv = np.stack([c[b], c_ctx], axis=0)
        cvecT = np.ascontiguousarray(cv.reshape(2, KC, 128).transpose(2, 0, 1)).reshape(128, 32)
        m = dict(shared)
        m["x"] = np.ascontiguousarray(x[b])
        m["ctx"] = np.ascontiguousarray(ctx[b])
        m["cvecT"] = cvecT
        in_maps.append(m)
    return in_maps


def kernel(**inputs):
    in_maps = prep(**inputs)
    if "nc" not in _CACHE:
        _CACHE["nc"] = build_program()
    nc = _CACHE["nc"]
    res = run_bass_kernel_spmd(nc, in_maps, core_ids=list(range(len(in_maps))))
    return np.stack([np.asarray(r["out"], dtype=np.float32) for r in res.results], axis=0)
```

```python
import numpy as np
from contextlib import ExitStack
import concourse.bass as bass
import concourse.mybir as mybir
from concourse.bass_utils import run_bass_kernel_spmd

F32 = mybir.dt.float32
BF16 = mybir.dt.bfloat16
AF = mybir.ActivationFunctionType
ALU = mybir.AluOpType

PE, ACT, DVE, POOL, SP = "pe", "act", "dve", "pool", "sp"
ENGS = (PE, ACT, DVE, POOL, SP)
DMA_ENGS = (SP, ACT, POOL)
N_DMA_SEMS = 16

T = 4096
L = 256
TL = T + L
D = 2048
KC = 16
DFF = 5632
NFB = 44
EPS = 1e-6
NCH = TL // 64
NPAIR = TL // 128

DEBUG_OUT = []
STOP_AFTER = 99


class Buf:
    __slots__ = ("name", "w", "r", "excl")

    def __init__(self, name="", excl=False):
        self.name = name
        self.w = None
        self.r = {}
        self.excl = excl


class Op:
    __slots__ = ("eng", "fn", "dma", "deps", "marked", "cnt", "sem_i", "barrier")

    def __init__(self, eng, fn, dma):
        self.eng = eng
        self.fn = fn
        self.dma = dma
        self.deps = []
        self.marked = False
        self.cnt = 0
        self.sem_i = -1
        self.barrier = 0


class Prog:
    def __init__(self):
        self.ops = []
        self.nbar = 0

    def add(self, eng, fn, reads=(), writes=(), dma=False):
        i = len(self.ops)
        op = Op(eng, fn, dma)
        writes = [b for b in writes if b is not None] + [b for b in reads if b is not None and b.excl]
        reads = [b for b in reads if b is not None and not b.excl]
        deps = set()
        for b in reads:
            if b.w is not None:
                deps.add(b.w)
        for b in writes:
            if b.w is not None:
                deps.add(b.w)
            for r in b.r.values():
                if isinstance(r, list):
                    deps.update(r)
                else:
                    deps.add(r)
        for b in reads:
            if dma:
                b.r.setdefault("dma", []).append(i)
            else:
                b.r[eng] = i
        for b in writes:
            b.w = i
            b.r = {}
        ops = self.ops
        for d in deps:
            p = ops[d]
            if p.eng == PE and eng == PE and not p.dma and not dma:
                continue
            op.deps.append(d)
            p.marked = True
        ops.append(op)
        return i

    def barrier(self):
        self.nbar += 1
        for e in ENGS:
            op = Op(e, None, False)
            op.barrier = self.nbar
            self.ops.append(op)

    def emit(self, nc, es):
        ops = self.ops
        eng_sem = {e: es.enter_context(nc.semaphore("s_" + e)) for e in ENGS}
        bar_sem = es.enter_context(nc.semaphore("s_bar"))
        dma_sems = {e: [es.enter_context(nc.semaphore("d_%s%d" % (e, k))) for k in range(N_DMA_SEMS)]
                    for e in DMA_ENGS}
        cnt = {e: 0 for e in ENGS}
        dcnt = {e: [0] * N_DMA_SEMS for e in DMA_ENGS}
        drr = {e: 0 for e in DMA_ENGS}
        for op in ops:
            if op.barrier:
                continue
            if op.dma:
                k = drr[op.eng]
                drr[op.eng] = (k + 1) % N_DMA_SEMS
                dcnt[op.eng][k] += 16
                op.sem_i = k
                op.cnt = dcnt[op.eng][k]
            elif op.marked:
                cnt[op.eng] += 1
                op.cnt = cnt[op.eng]
        per_eng = {e: [] for e in ENGS}
        for op in ops:
            per_eng[op.eng].append(op)

        def run(engobj, ename):
            known = {}
            issued = [0] * N_DMA_SEMS
            for op in per_eng[ename]:
                if op.barrier:
                    if ename in dma_sems:
                        for k in range(N_DMA_SEMS):
                            v = issued[k]
                            if v > 0 and known.get((ename, k), 0) < v:
                                engobj.wait_ge(dma_sems[ename][k], v)
                                known[(ename, k)] = v
                    engobj.drain().then_inc(bar_sem, 1)
                    engobj.wait_ge(bar_sem, len(ENGS) * op.barrier)
                    continue
                need = {}
                for d in op.deps:
                    p = ops[d]
                    key = (p.eng, p.sem_i) if p.dma else (p.eng, -1)
                    if need.get(key, 0) < p.cnt:
                        need[key] = p.cnt
                if op.dma and op.cnt > 16:
                    key = (ename, op.sem_i)
                    if need.get(key, 0) < op.cnt - 16:
                        need[key] = op.cnt - 16
                for key, v in need.items():
                    if known.get(key, 0) >= v:
                        continue
                    known[key] = v
                    sem = dma_sems[key[0]][key[1]] if key[1] >= 0 else eng_sem[key[0]]
                    engobj.wait_ge(sem, v)
                ins = op.fn(engobj)
                if op.dma:
                    ins.then_inc(dma_sems[ename][op.sem_i], 16)
                    issued[op.sem_i] = op.cnt
                elif op.marked:
                    ins.then_inc(eng_sem[ename], 1)
            if ename in dma_sems:
                for k in range(N_DMA_SEMS):
                    v = issued[k]
                    if v > 0 and known.get((ename, k), 0) < v:
                        engobj.wait_ge(dma_sems[ename][k], v)

        with nc.Block() as block:
            @block.tensor
            def _(e):
                run(e, PE)

            @block.scalar
            def _(e):
                run(e, ACT)

            @block.vector
            def _(e):
                run(e, DVE)

            @block.gpsimd
            def _(e):
                run(e, POOL)

            @block.sync
            def _(e):
                run(e, SP)


class Rot:
    def __init__(self, items):
        self.items = items
        self.i = 0

    def next(self):
        it = self.items[self.i % len(self.items)]
        self.i += 1
        return it


class KB:
    def __init__(self):
        self.nc = bass.Bass("TRN2", target_bir_lowering=False)
        self.P = Prog()
        self.dr = {}
        self.drb = {}

    def din(self, name, shape, dt=F32):
        t = self.nc.dram_tensor(name, list(shape), dt, kind="ExternalInput")
        self.dr[name] = t
        self.drb[name] = Buf(name)
        return t

    def dscr(self, name, shape, dt):
        kind = "ExternalOutput" if name in DEBUG_OUT else "Internal"
        t = self.nc.dram_tensor(name, list(shape), dt, kind=kind)
        self.dr[name] = t
        return t

    def sb(self, es, name, free, dt):
        return es.enter_context(self.nc.sbuf_tensor(name, [128, free], dt))

    def sbs(self, es, name, free, dt, n):
        return Rot([(self.sb(es, "%s%d" % (name, i), free, dt), Buf("%s%d" % (name, i))) for i in range(n)])

    def mm(self, out, lhsT, rhs, R, W, start=True, stop=True):
        self.P.add(PE, lambda e: e.matmul(out, lhsT=lhsT, rhs=rhs, start=start, stop=stop), R, W)

    def tr(self, out, in_, ident, R, W):
        self.P.add(PE, lambda e: e.transpose(out=out, in_=in_, identity=ident), R, W)

    def act(self, out, in_, func, R, W, scale=1.0, bias=0.0, accum=None):
        if accum is None:
            self.P.add(ACT, lambda e: e.activation(out=out, in_=in_, func=func, bias=bias, scale=scale), R, W)
        else:
            self.P.add(ACT, lambda e: e.activation(out=out, in_=in_, func=func, bias=bias, scale=scale,
                                                   accum_out=accum), R, W)

    def ts(self, out, in0, s1, s2, op0, op1, R, W, eng=DVE):
        self.P.add(eng, lambda e: e.tensor_scalar(out=out, in0=in0, scalar1=s1, scalar2=s2, op0=op0, op1=op1), R, W)

    def tt(self, out, in0, in1, op, R, W, eng=DVE):
        self.P.add(eng, lambda e: e.tensor_tensor(out=out, in0=in0, in1=in1, op=op), R, W)

    def stt(self, out, in0, scalar, in1, op0, op1, R, W):
        self.P.add(DVE, lambda e: e.scalar_tensor_tensor(out=out, in0=in0, scalar=scalar, in1=in1, op0=op0, op1=op1),
                   R, W)

    def scan(self, out, d0, d1, R, W):
        self.P.add(DVE, lambda e: e.tensor_tensor_scan(out=out, data0=d0, data1=d1, initial=0.0,
                                                       op0=ALU.mult, op1=ALU.add), R, W)

    def cp(self, out, in_, R, W, eng=DVE):
        if eng == ACT:
            self.P.add(ACT, lambda e: e.activation(out=out, in_=in_, func=AF.Copy), R, W)
        else:
            self.P.add(eng, lambda e: e.tensor_copy(out=out, in_=in_), R, W)

    def memset(self, ap, val, W, eng=DVE):
        self.P.add(eng, lambda e: e.memset(ap, val), (), W)

    def dma(self, q, out, in_, R, W):
        self.P.add(q, lambda e: e.dma_start(out=out, in_=in_), R, W, dma=True)


def fap(t, col, dims, p0=0, npart=128):
    F = t.shape[1]
    return bass.AP(t, p0 * F + col, [[F, npart]] + [list(d) for d in dims])


MT = 512


def build_program():
    kb = KB()
    nc = kb.nc
    P = kb.P
    x_d = kb.din("x", [T, D])
    ctx_d = kb.din("ctx", [L, D])
    cvec_d = kb.din("cvecT", [128, 32])
    wada_d = kb.din("w_ada_t", [24, 128, KC * 512])
    badaT_d = kb.din("b_adaT", [128, 96])
    bgate_d = kb.din("b_gate_bc", [128, 2 * D])
    g1_d = kb.din("g1T", [128, KC])
    g2_d = kb.din("g2T", [128, KC])
    win_d = kb.din("w_in_t", [21, 128, KC * 512])
    lbraw_d = kb.din("lbrawT", [128, 32])
    vec128_d = kb.din("vec128", [128, 3])
    sink_d = kb.din("sink_bc", [128, 8])
    wrec_d = kb.din("w_rec_t", [4, 128, 8 * 512])
    watt_d = kb.din("w_att_t", [4, 128, 8 * 512])
    wout_d = kb.din("w_out_t", [4, 128, KC * 512])
    wup_d = kb.din("w_up_t", [NFB, 128, KC * 256])
    convw_d = kb.din("convwT", [128, 3 * NFB])
    convb_d = kb.din("convbT", [128, NFB])
    wdown_d = kb.din("w_down_t", [4, 4, 128, 11 * 512])
    ident_d = kb.din("ident", [128, 128])
    rt_d = kb.din("ropeRT", [128, 128])
    mask_d = kb.din("masks", [128, 4 * 128])
    cos_d = kb.din("ropeC", [128, T])
    sin_d = kb.din("ropeS", [128, T])
    out_d = nc.dram_tensor("out", [T, D], F32, kind="ExternalOutput")
    out_b = None

    QDF = kb.dscr("QDF", [8, 128, TL], BF16)
    KDF = kb.dscr("KDF", [8, 128, TL], BF16)
    QDB = kb.dscr("QDB", [8, 128, TL], BF16)
    KDB = kb.dscr("KDB", [8, 128, TL], BF16)
    KDFt = kb.dscr("KDFt", [8, 128, NPAIR * 128], BF16)
    KDBt = kb.dscr("KDBt", [8, 128, NPAIR * 128], BF16)
    VS = kb.dscr("VS", [8, 128, NPAIR * 128], BF16)
    GT = kb.dscr("GT", [8, 128, T], BF16)
    AQT = kb.dscr("AQT", [8, 128, T], BF16)
    AKT = kb.dscr("AKT", [2, 128, TL], BF16)
    AVS = kb.dscr("AVS", [2, 128, NPAIR * 128], BF16)
    SGR = kb.dscr("SGR", [16, 128, T], BF16)
    SGA = kb.dscr("SGA", [16, 128, T], BF16)
    YREC = kb.dscr("YREC", [8, 128, T], BF16)
    YATT = kb.dscr("YATT", [8, 128, T], BF16)
    H2T = kb.dscr("H2T", [KC, 128, T], BF16)
    kb.dscr("WRB", [4, 128, 8 * 512], BF16)
    kb.dscr("WAB", [4, 128, 8 * 512], BF16)
    kb.dscr("WOB", [4, 128, KC * 512], BF16)
    CSC = kb.dscr("CSC", [128, 3 * 16 * NCH], F32)
    MODS = kb.dscr("MODS", [128, 6 * KC + 2 * D], F32)
    scr_b = {n: None for n in ("QDF", "KDF", "QDB", "KDB", "KDFt", "KDBt", "VS", "GT", "AQT", "AKT", "AVS",
                                 "SGR", "SGA", "YREC", "YATT", "H2T", "CSC", "MODS")}

    with ExitStack() as ges:
        ident_f = kb.sb(ges, "ident_f", 128, F32)
        ident_b = kb.sb(ges, "ident_b", 128, BF16)
        ones_f = kb.sb(ges, "ones_f", 128, F32)
        b_const = Buf("const")

        psum = [ges.enter_context(nc.psum_tensor("ps%d" % i, [128, 512], F32)) for i in range(8)]
        psb = [Buf("ps%d" % i, excl=True) for i in range(8)]

        kb.dma(SP, ident_f[:], ident_d.ap(), [], [b_const])
        kb.cp(ident_b[:], ident_f[:], [b_const], [b_const])
        kb.memset(ones_f[:], 1.0, [b_const])

        with ExitStack() as es:
            cv_f = kb.sb(es, "cv_f", 32, F32)
            csil = kb.sb(es, "csil", 32, BF16)
            crep = kb.sb(es, "crep", KC * 128, BF16)
            badaT = kb.sb(es, "badaT", 96, F32)
            bgate = kb.sb(es, "bgate", 2 * D, F32)
            g1s = kb.sb(es, "g1s", KC, F32)
            g2s = kb.sb(es, "g2s", KC, F32)
            modT = kb.sb(es, "modT", 192, F32)
            mods = kb.sb(es, "mods", 6 * KC + 2 * D, F32)
            b0 = Buf("p0")
            b_mods = Buf("mods")
            wb = kb.sbs(es, "wb0_", KC * 512, BF16, 3)
            kb.dma(SP, cv_f[:], cvec_d.ap(), [], [b0])
            kb.dma(SP, badaT[:], badaT_d.ap(), [], [b0])
            kb.dma(SP, bgate[:], bgate_d.ap(), [], [b0])
            kb.dma(SP, g1s[:], g1_d.ap(), [], [b0])
            kb.dma(SP, g2s[:], g2_d.ap(), [], [b0])
            kb.act(csil[:], cv_f[:], AF.Silu, [b0], [b0])
            kb.cp(fap(crep, 0, [[128, KC], [1, 128]]), fap(csil, 0, [[1, KC], [0, 128]]), [b0], [b0])
            ps_mod = psum[0]
            gi = 0
            for cb in range(24):
                j = cb // 4
                wt, wbuf = wb.next()
                kb.dma(POOL, wt[:], wada_d.ap()[cb], [], [wbuf])
                if j in (2, 5):
                    ps = psum[1 + gi % 2]
                    pb = psb[1 + gi % 2]
                    gi += 1
                    for kc in range(KC):
                        kb.mm(ps[:], crep[:, kc * 128:(kc + 1) * 128], wt[:, kc * 512:(kc + 1) * 512],
                              [b0, wbuf], [pb], start=(kc == 0), stop=(kc == KC - 1))
                    gcol = ((0 if j == 2 else 1) * D) + (cb % 4) * 512
                    kb.tt(mods[:, 6 * KC + gcol: 6 * KC + gcol + 512], ps[:], bgate[:, gcol:gcol + 512], ALU.add,
                          [pb, b0], [b_mods])
                else:
                    for f in range(4):
                        blk = cb * 4 + f
                        for kc in range(KC):
                            kb.mm(ps_mod[:, blk * 2: blk * 2 + 2],
                                  wt[:, kc * 512 + f * 128: kc * 512 + (f + 1) * 128],
                                  fap(csil, kc, [[16, 2]]),
                                  [b0, wbuf], [psb[0]], start=(kc == 0), stop=(kc == KC - 1))
            kb.tt(fap(modT, 0, [[2, 96], [1, 2]]), fap(ps_mod, 0, [[2, 96], [1, 2]]),
                  fap(badaT, 0, [[1, 96], [0, 2]]), ALU.add, [psb[0], b0], [b0])

            def modv(j, v):
                return fap(modT, (j * 16) * 2 + v, [[2, KC]])
            kb.stt(mods[:, 0:16], modv(1, 0), 1.0, g1s[:], ALU.add, ALU.mult, [b0], [b_mods])
            kb.cp(mods[:, 16:32], modv(0, 0), [b0], [b_mods])
            kb.stt(mods[:, 32:48], modv(1, 1), 1.0, g1s[:], ALU.add, ALU.mult, [b0], [b_mods])
            kb.cp(mods[:, 48:64], modv(0, 1), [b0], [b_mods])
            kb.stt(mods[:, 64:80], modv(4, 0), 1.0, g2s[:], ALU.add, ALU.mult, [b0], [b_mods])
            kb.cp(mods[:, 80:96], modv(3, 0), [b0], [b_mods])
            kb.dma(SP, MODS.ap(), mods[:], [b_mods], [scr_b["MODS"]])
        P.barrier()
        G = locals()
        if STOP_AFTER >= 1:
            phase1(kb, ges, G)
        if STOP_AFTER >= 2:
            phase2a(kb, ges, G)
        if STOP_AFTER >= 3:
            phase2b(kb, ges, G)
        if STOP_AFTER >= 4:
            phase3(kb, ges, G)
        if STOP_AFTER >= 5:
            phase4(kb, ges, G)
        P.emit(nc, ges)
    return nc


def phase1(kb, ges, G):
    nc = kb.nc
    P = kb.P
    psum, psb = G["psum"], G["psb"]
    ident_b, ones_f, b_const = G["ident_b"], G["ones_f"], G["b_const"]
    dr = kb.dr
    with ExitStack() as es:
        mods = kb.sb(es, "mods1", 6 * KC, F32)
        lbr = kb.sb(es, "lbr", 32, F32)
        lbv = kb.sb(es, "lbv", 16, F32)
        oml = kb.sb(es, "oml", 16, F32)
        noml = kb.sb(es, "noml", 16, F32)
        v128 = kb.sb(es, "v128", 3, F32)
        rt_f = kb.sb(es, "rt_f", 128, F32)
        rt_b = kb.sb(es, "rt_b", 128, BF16)
        smask = kb.sb(es, "smask", MT, F32)
        csc = kb.sb(es, "csc", 3 * 16 * NCH, F32)
        b_c = Buf("p1const")
        b_csc = Buf("csc")
        kb.dma(SP, mods[:], dr["MODS"].ap()[:, 0:6 * KC], [], [b_c])
        kb.dma(SP, lbr[:], dr["lbrawT"].ap(), [], [b_c])
        kb.dma(SP, v128[:], dr["vec128"].ap(), [], [b_c])
        kb.dma(SP, rt_f[:], dr["ropeRT"].ap(), [], [b_c])
        kb.cp(rt_b[:], rt_f[:], [b_c], [b_c])
        kb.tt(lbv[:], lbr[:, 0:16], lbr[:, 16:32], ALU.subtract, [b_c], [b_c])
        kb.act(lbv[:], lbv[:], AF.Sigmoid, [b_c], [b_c])
        kb.ts(oml[:], lbv[:], -1.0, 1.0, ALU.mult, ALU.add, [b_c], [b_c])
        kb.ts(noml[:], lbv[:], 1.0, -1.0, ALU.mult, ALU.add, [b_c], [b_c])
        kb.memset(smask[:], 1.0, [b_c])
        kb.memset(fap(smask, 0, [[64, MT // 64]]), 0.0, [b_c])
        kb.memset(csc[:], 1.0, [b_csc])

        hTs = [(kb.sb(es, "hT%d" % i, KC * MT, BF16), [Buf("hT%d_%d" % (i, k)) for k in range(KC)]) for i in range(2)]
        xs = kb.sbs(es, "xs", D, F32, 2)
        xn = kb.sbs(es, "xn", D, BF16, MT // 128)
        stat = kb.sbs(es, "stat", 4, F32, 2)
        wb = kb.sbs(es, "wb1_", KC * 512, BF16, 2)
        ev = [[(kb.sb(es, "ev%d_%d" % (s, i), MT, F32), Buf()) for i in range(3)] for s in range(2)]
        sh = [(kb.sb(es, "sh%d" % i, MT, F32), Buf()) for i in range(13)]
        stg = {nm: kb.sbs(es, "st_" + nm, MT, BF16, 2) for nm in ("qdf", "kdf", "qdb", "kdb", "g", "kof", "kob")}
        stg_kt = kb.sbs(es, "st_kt", 2 * MT, BF16, 2)
        stg_v = kb.sbs(es, "st_v", 4 * MT, BF16, 1)
        stg_q = kb.sbs(es, "st_q", MT, BF16, 3)
        stg_av = kb.sbs(es, "st_av", 2 * MT, BF16, 1)
        ropeC = kb.sb(es, "ropeC_sb", MT, F32)
        ropeS = kb.sb(es, "ropeS_sb", MT, F32)
        b_rope = Buf("rope")
        xg_t = kb.sbs(es, "xg", MT, BF16, 3)

        tiles = [("c", 0, L)] + [("x", t0, MT) for t0 in range(0, T, MT)]
        pend = []
        cnt = {"u": 0, "q": 0, "s": 0, "r": 0}

        def flush(all_=False):
            keep = []
            for dly, fn in pend:
                if all_ or dly <= 1:
                    fn()
                else:
                    keep.append((dly - 1, fn))
            pend[:] = keep

        def stage_a1(ti):
            kind, t0, n = tiles[ti]
            src = dr["x"] if kind == "x" else dr["ctx"]
            res = []
            for tb in range(n // 128):
                xt, xb_ = xs.next()
                xnt, xnb = xn.next()
                st, stb = stat.next()
                kb.dma(SP, xt[:], src.ap()[t0 + tb * 128: t0 + (tb + 1) * 128, :], [], [xb_])
                kb.act(xnt[:], xt[:], AF.Square, [xb_], [xnb, stb], accum=st[:, 0:1])
                kb.ts(st[:, 1:2], st[:, 0:1], 1.0 / D, EPS, ALU.mult, ALU.add, [stb], [stb])
                kb.act(st[:, 2:3], st[:, 1:2], AF.Ln, [stb], [stb])
                kb.act(st[:, 3:4], st[:, 2:3], AF.Exp, [stb], [stb], scale=-0.5)
                kb.act(xnt[:], xt[:], AF.Identity, [xb_, stb], [xnb], scale=st[:, 3:4])
                res.append((xnt, xnb))
            return res

        def stage_a2(ti, xns):
            kind, t0, n = tiles[ti]
            hT, hTb = hTs[ti % 2]
            a_off = 0 if kind == "x" else 32
            for tb, (xnt, xnb) in enumerate(xns):
                for half in range(2):
                    pt, ptb = psum[6 + half], psb[6 + half]
                    ptv = pt.bitcast(BF16)
                    for j in range(8):
                        kc = half * 8 + j
                        kb.tr(ptv[:, j * 128:(j + 1) * 128], xnt[:, kc * 128:(kc + 1) * 128], ident_b[:],
                              [xnb, b_const], [ptb])
                    for j in range(8):
                        kc = half * 8 + j
                        kb.ts(hT[:, kc * MT + tb * 128: kc * MT + (tb + 1) * 128], ptv[:, j * 128:(j + 1) * 128],
                              mods[:, a_off + kc: a_off + kc + 1], mods[:, a_off + 16 + kc: a_off + 17 + kc],
                              ALU.mult, ALU.add, [ptb, b_c], [hTb[kc]])

        xns0 = stage_a1(0)
        stage_a2(0, xns0)
        for ti, (kind, t0, n) in enumerate(tiles):
            is_x = kind == "x"
            g0 = (L + t0) if is_x else 0
            nblk = n // 128
            hT, hTb = hTs[ti % 2]
            if is_x:
                kb.dma(SP, ropeC[:, 0:n], dr["ropeC"].ap()[:, t0:t0 + n], [], [b_rope])
                kb.dma(SP, ropeS[:, 0:n], dr["ropeS"].ap()[:, t0:t0 + n], [], [b_rope])
            if is_x:
                blocks = []
                for i in range(8):
                    blocks += [i, 13 + i]
                blocks += [8, 9, 10, 11, 12]
            else:
                blocks = list(range(8)) + [8, 9, 12]
            tail_delay = 5 if is_x else 1
            k1 = 12 if is_x else 2
            k2 = 16 if is_x else 6
            nxt = None
            for bi, blk in enumerate(blocks):
                wt, wbuf = wb.next()
                kb.dma(POOL, wt[:], dr["w_in_t"].ap()[blk], [], [wbuf])

                def gemmB(ps, pb, col0, wt=wt, wbuf=wbuf):
                    for kc in range(KC):
                        kb.mm(ps[:, 0:n], wt[:, kc * 512 + col0: kc * 512 + col0 + 128], hT[:, kc * MT: kc * MT + n],
                              [wbuf, hTb[kc]], [pb], start=(kc == 0), stop=(kc == KC - 1))

                def gemmA(ps, pb, tb, col0, ncol, wt=wt, wbuf=wbuf):
                    for kc in range(KC):
                        kb.mm(ps[:, 0:ncol], hT[:, kc * MT + tb * 128: kc * MT + (tb + 1) * 128],
                              wt[:, kc * 512 + col0: kc * 512 + col0 + ncol],
                              [wbuf, hTb[kc]], [pb], start=(kc == 0), stop=(kc == KC - 1))

                if blk < 8:
                    h = blk
                    s = 0 if is_x else cnt["u"] % 2
                    cnt["u"] += 1
                    pq, pf, pbk = psum[3 * s], psum[3 * s + 1], psum[3 * s + 2]
                    bq, bf_, bbk = psb[3 * s], psb[3 * s + 1], psb[3 * s + 2]
                    gemmB(pq, bq, 0)
                    gemmB(pf, bf_, 128)
                    gemmB(pbk, bbk, 256)
                    (qs, qsb), (sf, sfb), (sbw, sbwb) = ev[s]
                    (lf, lfb), (lbk, lbkb), (kf, kfb), (kbk, kbkb), (pfw, pfwb), (pbw, pbwb), (peb, pebb), \
                        (df, dfb), (db, dbb), (e1, e1b), (e3, e3b), (d2, d2b), (e6, e6b) = sh
                    kb.act(qs[:, 0:n], pq[:, 0:n], AF.Silu, [bq], [qsb])
                    kb.act(sf[:, 0:n], pf[:, 0:n], AF.Sigmoid, [bf_], [sfb])
                    kb.act(sbw[:, 0:n], pbk[:, 0:n], AF.Sigmoid, [bbk], [sbwb])
                    if is_x:
                        pg, bg = psum[6], psb[6]
                        gemmB(pg, bg, 384)
                        gt, gb = stg["g"].next()
                        kb.act(gt[:, 0:n], pg[:, 0:n], AF.Silu, [bg], [gb])
                        kb.dma(SP, dr["GT"].ap()[h, :, t0:t0 + n], gt[:, 0:n], [gb], [])
                    flush()
                    nch = n // 64
                    c0 = g0 // 64
                    for di, (sg, sgb, lg, lgb, kk, kkb) in enumerate(((sf, sfb, lf, lfb, kf, kfb),
                                                                     (sbw, sbwb, lbk, lbkb, kbk, kbkb))):
                        hd = di * 8 + h
                        kb.act(lg[:, 0:n], sg[:, 0:n], AF.Ln, [sgb, b_c], [lgb],
                               scale=oml[:, hd:hd + 1], bias=lbv[:, hd:hd + 1])
                        kb.ts(kk[:, 0:n], sg[:, 0:n], noml[:, hd:hd + 1], oml[:, hd:hd + 1], ALU.mult, ALU.add,
                              [sgb, b_c], [kkb])
                    v3 = lambda t_, off: fap(t_, off, [[64, nch], [1, 64]])
                    bc3 = lambda t_, off: fap(t_, off, [[64, nch], [0, 64]])
                    kb.scan(pfw[:, 0:n], smask[:, 0:n], lf[:, 0:n], [b_c, lfb], [pfwb])
                    kb.tt(v3(df, 0), v3(pfw, 0), bc3(pfw, 32), ALU.subtract, [pfwb], [dfb])
                    kb.tt(v3(d2, 0), v3(pfw, 0), bc3(pfw, 63), ALU.subtract, [pfwb], [d2b])
                    kb.scan(pbw[:, 0:n], smask[:, 0:n], lbk[:, 0:n], [b_c, lbkb], [pbwb])
                    kb.tt(peb[:, 0:n], pbw[:, 0:n], lbk[:, 0:n], ALU.subtract, [pbwb, lbkb], [pebb])
                    kb.tt(v3(db, 0), v3(peb, 0), bc3(peb, 31), ALU.subtract, [pebb], [dbb])
                    kb.act(e1[:, 0:n], df[:, 0:n], AF.Exp, [dfb], [e1b])
                    kb.act(df[:, 0:n], df[:, 0:n], AF.Exp, [dfb], [dfb], scale=-1.0)
                    kb.act(e3[:, 0:n], db[:, 0:n], AF.Exp, [dbb], [e3b], scale=-1.0)
                    kb.act(db[:, 0:n], db[:, 0:n], AF.Exp, [dbb], [dbb])
                    kb.act(d2[:, 0:n], d2[:, 0:n], AF.Exp, [d2b], [d2b], scale=-1.0)
                    kb.act(e6[:, 0:n], peb[:, 0:n], AF.Exp, [pebb], [e6b])

                    def cs(k, hd, c):
                        return fap(csc, (k * 16 + hd) * NCH + c, [[1, nch]])
                    hf, hb = h, 8 + h
                    kb.act(cs(0, hf, c0), fap(pfw, 32, [[64, nch]]), AF.Exp, [pfwb], [b_csc])
                    kb.act(cs(2, hf, c0), fap(pfw, 63, [[64, nch]]), AF.Exp, [pfwb], [b_csc])
                    kb.tt(cs(0, hb, c0), fap(pbw, 63, [[64, nch]]), fap(peb, 31, [[64, nch]]), ALU.subtract,
                          [pbwb, pebb], [b_csc])
                    kb.act(cs(0, hb, c0), cs(0, hb, c0), AF.Exp, [b_csc], [b_csc])
                    kb.act(cs(2, hb, c0), fap(pbw, 63, [[64, nch]]), AF.Exp, [pbwb], [b_csc])
                    outs = {}
                    for nm, a_, ab, b_, bb, dst in (("qdf", qs, qsb, e1, e1b, "QDF"), ("kdf", kf, kfb, df, dfb, "KDF"),
                                                    ("qdb", qs, qsb, e3, e3b, "QDB"), ("kdb", kbk, kbkb, db, dbb, "KDB"),
                                                    ("kof", kf, kfb, d2, d2b, None), ("kob", kbk, kbkb, e6, e6b, None)):
                        ot, ob = stg[nm].next()
                        kb.tt(ot[:, 0:n], a_[:, 0:n], b_[:, 0:n], ALU.mult, [ab, bb], [ob])
                        if dst is not None:
                            kb.dma(SP, dr[dst].ap()[h, :, g0:g0 + n], ot[:, 0:n], [ob], [])
                        outs[nm] = (ot, ob)

                    def tail(h=h, n=n, nblk=nblk, g0=g0, kof=outs["kof"], kob=outs["kob"]):
                        pt, ptb = psum[7], psb[7]
                        ptv = pt.bitcast(BF16)
                        kt, ktb = stg_kt.next()
                        for di, (ot, ob) in enumerate((kof, kob)):
                            for tb in range(nblk):
                                kb.tr(ptv[:, (di * nblk + tb) * 128:(di * nblk + tb + 1) * 128],
                                      ot[:, tb * 128:(tb + 1) * 128], ident_b[:], [ob, b_const], [ptb])
                        kb.cp(kt[:, 0:2 * n], ptv[:, 0:2 * n], [ptb], [ktb], eng=ACT)
                        for di, nm in enumerate(("KDFt", "KDBt")):
                            kb.dma(SP, dr[nm].ap()[h, :, (g0 // 128) * 128:(g0 // 128 + nblk) * 128],
                                   kt[:, di * n:(di + 1) * n], [ktb], [])
                    pend.append((tail_delay, tail))
                elif blk in (8, 9):
                    vt, vb = stg_v.next()
                    for tb in range(nblk):
                        s = cnt["s"] % 6
                        cnt["s"] += 1
                        gemmA(psum[s], psb[s], tb, 0, 512)
                        kb.cp(fap(vt, tb * 128, [[nblk * 128, 4], [1, 128]]), fap(psum[s], 0, [[128, 4], [1, 128]]),
                              [psb[s]], [vb], eng=ACT)
                        if tb == 0:
                            flush()
                    for h4 in range(4):
                        h = (blk - 8) * 4 + h4
                        kb.dma(SP, dr["VS"].ap()[h, :, (g0 // 128) * 128:(g0 // 128 + nblk) * 128],
                               vt[:, h4 * nblk * 128:(h4 + 1) * nblk * 128], [vb], [])
                elif blk in (10, 11, 12):
                    nh = 4 if blk < 12 else 2
                    for f in range(nh):
                        qi = cnt["q"]
                        cnt["q"] += 1
                        pq, bq = psum[qi % 3], psb[qi % 3]
                        pss, bss = psum[3 + qi % 2], psb[3 + qi % 2]
                        prx, brx = psum[5 + qi % 2], psb[5 + qi % 2]
                        (sq, sqb) = ev[qi % 2][0]
                        (rs, rsb) = ev[qi % 2][1]
                        (t1, t1b), (t2, t2b) = sh[0], sh[1]
                        gemmB(pq, bq, f * 128)
                        gcol = 1 if blk < 12 else 2
                        kb.act(sq[:, 0:n], pq[:, 0:n], AF.Square, [bq], [sqb])
                        flush()
                        ot, ob = stg_q.next()
                        xg, xgb = xg_t.next()
                        if blk < 12:
                            dst = dr["AQT"].ap()[(blk - 10) * 4 + f, :, t0:t0 + n]
                        else:
                            dst = dr["AKT"].ap()[f, :, g0:g0 + n]

                        def tail1(n=n, pq=pq, bq=bq, pss=pss, bss=bss, sq=sq, sqb=sqb, rs=rs, rsb=rsb, xg=xg, xgb=xgb,
                                  ot=ot, ob=ob, gcol=gcol, is_x=is_x, dst=dst):
                            kb.mm(pss[:, 0:n], ones_f[:], sq[:, 0:n], [b_const, sqb], [bss])
                            kb.act(rs[:, 0:n], pss[:, 0:n], AF.Ln, [bss], [rsb], scale=1.0 / 128, bias=EPS)
                            kb.act(rs[:, 0:n], rs[:, 0:n], AF.Exp, [rsb], [rsb], scale=-0.5)
                            if is_x:
                                kb.stt(xg[:, 0:n], pq[:, 0:n], v128[:, gcol:gcol + 1], rs[:, 0:n], ALU.mult, ALU.mult,
                                       [bq, b_c, rsb], [xgb])
                            else:
                                kb.stt(ot[:, 0:n], pq[:, 0:n], v128[:, gcol:gcol + 1], rs[:, 0:n], ALU.mult, ALU.mult,
                                       [bq, b_c, rsb], [ob])
                                kb.dma(SP, dst, ot[:, 0:n], [ob], [])

                        def tail2(n=n, prx=prx, brx=brx, xg=xg, xgb=xgb, ot=ot, ob=ob, t1=t1, t1b=t1b, t2=t2, t2b=t2b,
                                  dst=dst):
                            kb.mm(prx[:, 0:n], rt_b[:], xg[:, 0:n], [b_c, xgb], [brx])
                            kb.tt(t1[:, 0:n], xg[:, 0:n], ropeC[:, 0:n], ALU.mult, [xgb, b_rope], [t1b])
                            kb.tt(t2[:, 0:n], prx[:, 0:n], ropeS[:, 0:n], ALU.mult, [brx, b_rope], [t2b])
                            kb.tt(ot[:, 0:n], t1[:, 0:n], t2[:, 0:n], ALU.add, [t1b, t2b], [ob])
                            kb.dma(SP, dst, ot[:, 0:n], [ob], [])
                        pend.append((1, tail1))
                        if is_x:
                            pend.append((2, tail2))
                    if blk == 12:
                        vt, vb = stg_av.next()
                        for tb in range(nblk):
                            s = 3 + cnt["s"] % 4
                            cnt["s"] += 1
                            gemmA(psum[s], psb[s], tb, 256, 256)
                            kb.cp(fap(vt, tb * 128, [[nblk * 128, 2], [1, 128]]),
                                  fap(psum[s], 0, [[128, 2], [1, 128]]), [psb[s]], [vb], eng=ACT)
                            flush()
                        for kv in range(2):
                            kb.dma(SP, dr["AVS"].ap()[kv, :, (g0 // 128) * 128:(g0 // 128 + nblk) * 128],
                                   vt[:, kv * nblk * 128:(kv + 1) * nblk * 128], [vb], [])
                else:
                    for f in range(4):
                        s = 3 + cnt["s"] % 3
                        cnt["s"] += 1
                        gemmB(psum[s], psb[s], f * 128)
                        ot, ob = stg_q.next()
                        kb.act(ot[:, 0:n], psum[s][:, 0:n], AF.Sigmoid, [psb[s]], [ob])
                        flush()
                        fb = (blk - 13) * 4 + f
                        nm = "SGR" if fb < 16 else "SGA"
                        kb.dma(SP, dr[nm].ap()[fb % 16, :, t0:t0 + n], ot[:, 0:n], [ob], [])
                if ti + 1 < len(tiles):
                    if bi == k1:
                        nxt = stage_a1(ti + 1)
                    if bi == k2:
                        stage_a2(ti + 1, nxt)
            flush(all_=True)
        kb.dma(SP, dr["CSC"].ap(), csc[:], [b_csc], [])
    P.barrier()


def phase2a(kb, ges, G):
    nc = kb.nc
    P = kb.P
    psum, psb = G["psum"], G["psb"]
    scr_b = G["scr_b"]
    ones_f, b_const = G["ones_f"], G["b_const"]
    dr = kb.dr
    W = NPAIR * 128
    with ExitStack() as es:
        mk_f = kb.sb(es, "mk_f2", 512, F32)
        mk = kb.sb(es, "mk2", 512, BF16)
        v128 = kb.sb(es, "v128_2", 3, F32)
        b_c = Buf("p2const")
        kb.dma(SP, mk_f[:], dr["masks"].ap(), [], [b_c])
        kb.cp(mk[:], mk_f[:], [b_c], [b_c])
        kb.dma(SP, v128[:], dr["vec128"].ap(), [], [b_c])
        for nb in range(4):
            kb.dma(POOL, dr["WRB"].ap()[nb], dr["w_rec_t"].ap()[nb], [], [])
            kb.dma(POOL, dr["WAB"].ap()[nb], dr["w_att_t"].ap()[nb], [], [])
            kb.dma(POOL, dr["WOB"].ap()[nb], dr["w_out_t"].ap()[nb], [], [])
        names = ("QDF", "KDF", "QDB", "KDB", "KDFt", "KDBt")
        sets = []
        for s in range(2):
            d = {nm: (kb.sb(es, "h%s%d" % (nm, s), TL, BF16), Buf()) for nm in names}
            d["VX"] = (kb.sb(es, "hVX%d" % s, 2 * W, BF16), Buf())
            d["csc"] = (kb.sb(es, "hcsc%d" % s, 3 * 2 * NCH, F32), Buf())
            kb.memset(d["VX"][0][:], 0.0, [d["VX"][1]])
            sets.append(d)
        GTt = kb.sb(es, "hGT", T, BF16)
        GTb = Buf()
        oacc = kb.sb(es, "oacc", T, F32)
        oab = [Buf("oacc%d" % g) for g in range(8)]
        yst = kb.sb(es, "yst", T, BF16)
        yb = Buf("yst")
        S32 = [kb.sbs(es, "S32_%d" % di, 128, F32, 3) for di in range(2)]
        S16 = [kb.sbs(es, "S16_%d" % di, 128, BF16, 2) for di in range(2)]
        IT = [kb.sbs(es, "IT_%d" % di, 128, BF16, 3) for di in range(2)]
        ntm = [(kb.sb(es, "ntm%d" % i, 512, F32), Buf()) for i in range(3)]
        CS = 3 * 16 * NCH

        def load_head(h, st):
            for nm in names:
                t_, b_ = st[nm]
                kb.dma(SP, t_[:], dr[nm].ap()[h], [], [b_])
            t_, b_ = st["VX"]
            for half in range(2):
                kb.dma(SP, fap(t_, half * 128, [[256, NPAIR], [1, 128]], p0=half * 64, npart=64),
                       bass.AP(dr["VS"], (h * 128 + half * 64) * W, [[W, 64], [128, NPAIR], [1, 128]]), [], [b_])
            t_, b_ = st["csc"]
            kb.dma(SP, fap(t_, 0, [[2 * NCH, 3], [NCH, 2], [1, NCH]]),
                   bass.AP(dr["CSC"], h * NCH, [[CS, 128], [16 * NCH, 3], [8 * NCH, 2], [1, NCH]]), [], [b_])

        load_head(0, sets[0])
        for h in range(8):
            st = sets[h % 2]
            if h + 1 < 8:
                load_head(h + 1, sets[(h + 1) % 2])
            kb.dma(SP, GTt[:], dr["GT"].ap()[h], [], [GTb])
            csc_t, csc_b = st["csc"]
            VX_t, VX_b = st["VX"]
            order = [list(range(NPAIR)), [1, 0] + list(range(NPAIR - 1, 1, -1))]
            cur = [None, None]
            for di in range(2):
                s0, s0b = S32[di].next()
                kb.memset(s0[:], 0.0, [s0b])
                cur[di] = (s0, s0b)
            written = [False] * 8
            ucnt = [0, 0]
            its = [[None, None] for _ in range(NPAIR)]
            ups_l = [[None, None] for _ in range(NPAIR)]

            def part_a(step):
                for di in range(2):
                    p = order[di][step]
                    QD_t, QD_b = st["QDF" if di == 0 else "QDB"]
                    KD_t, KD_b = st["KDF" if di == 0 else "KDB"]
                    if p >= 2:
                        ips, ipb = psum[2 + di], psb[2 + di]
                        kb.mm(ips[:, 0:128], KD_t[:, p * 128:(p + 1) * 128], QD_t[:, p * 128:(p + 1) * 128],
                              [KD_b, QD_b], [ipb])
                        it_t, it_b = IT[di].next()
                        kb.tt(it_t[:], ips[:, 0:128], mk[:, di * 128:(di + 1) * 128], ALU.mult, [ipb, b_c], [it_b])
                        its[step][di] = (it_t, it_b)
                for di in range(2):
                    p = order[di][step]
                    Kt_t, Kt_b = st["KDFt" if di == 0 else "KDBt"]
                    ui = 4 + 2 * di + ucnt[di] % 2
                    ucnt[di] += 1
                    kb.mm(psum[ui][:, 0:256], Kt_t[:, p * 128:(p + 1) * 128], VX_t[:, p * 256:(p + 1) * 256],
                          [Kt_b, VX_b], [psb[ui]])
                    ups_l[step][di] = ui

            def part_b(step):
                info = []
                for di in range(2):
                    p = order[di][step]
                    is_x = p >= 2
                    g = (p - 2) // 4 if is_x else None
                    slot = (p - 2) % 4 if is_x else None
                    chunks = (2 * p, 2 * p + 1) if di == 0 else (2 * p + 1, 2 * p)
                    info.append((p, is_x, g, slot, chunks))
                for ci in range(2):
                    for di in range(2):
                        p, is_x, g, slot, chunks = info[di]
                        ch = chunks[ci]
                        QD_t, QD_b = st["QDF" if di == 0 else "QDB"]
                        oT, oTb = psum[di], psb[di]
                        ui = ups_l[step][di]
                        s_t, s_b = cur[di]
                        half = ch % 2
                        if is_x:
                            it_t, it_b = its[step][di]
                            oc = slot * 128 + half * 64
                            kb.mm(oT[:, oc:oc + 64], VX_t[:, p * 256 + half * 128: p * 256 + (half + 1) * 128],
                                  it_t[:, half * 64:(half + 1) * 64], [VX_b, it_b], [oTb], start=True, stop=False)
                            s16, s16b = S16[di].next()
                            kb.act(s16[:], s_t[:], AF.Identity, [s_b, csc_b], [s16b],
                                   scale=csc_t[:, (0 * 2 + di) * NCH + ch:(0 * 2 + di) * NCH + ch + 1])
                            kb.mm(oT[:, oc:oc + 64], s16[:], QD_t[:, ch * 64:(ch + 1) * 64], [s16b, QD_b], [oTb],
                                  start=False, stop=True)
                        n_t, n_b = S32[di].next()
                        kb.stt(n_t[:], s_t[:], csc_t[:, (2 * 2 + di) * NCH + ch:(2 * 2 + di) * NCH + ch + 1],
                               psum[ui][:, half * 128:(half + 1) * 128], ALU.mult, ALU.add,
                               [s_b, csc_b, psb[ui]], [n_b])
                        cur[di] = (n_t, n_b)
                for di in range(2):
                    p, is_x, g, slot, chunks = info[di]
                    oT, oTb = psum[di], psb[di]
                    if is_x and ((di == 0 and slot == 3) or (di == 1 and slot == 0)):
                        if not written[g]:
                            kb.cp(oacc[:, g * 512:(g + 1) * 512], oT[:], [oTb], [oab[g]], eng=ACT)
                            written[g] = True
                        else:
                            kb.tt(oacc[:, g * 512:(g + 1) * 512], oT[:], oacc[:, g * 512:(g + 1) * 512], ALU.add,
                                  [oTb, oab[g]], [oab[g]])

            part_a(0)
            for step in range(NPAIR):
                if step + 1 < NPAIR:
                    part_a(step + 1)
                part_b(step)
            for g in range(8):
                (sq, sqb), (rs, rsb), (y1, y1b) = ntm
                pss, pssb = psum[2 + g % 2], psb[2 + g % 2]
                sl = slice(g * 512, (g + 1) * 512)
                kb.act(sq[:], oacc[:, sl], AF.Square, [oab[g]], [sqb])
                kb.mm(pss[:], ones_f[:], sq[:], [b_const, sqb], [pssb])
                kb.act(rs[:], pss[:], AF.Ln, [pssb], [rsb], scale=1.0 / 128, bias=EPS)
                kb.act(rs[:], rs[:], AF.Exp, [rsb], [rsb], scale=-0.5)
                kb.stt(y1[:], oacc[:, sl], v128[:, 0:1], rs[:], ALU.mult, ALU.mult, [oab[g], b_c, rsb], [y1b])
                kb.tt(yst[:, sl], y1[:], GTt[:, sl], ALU.mult, [y1b, GTb], [yb])
            kb.dma(SP, dr["YREC"].ap()[h], yst[:], [yb], [])
    P.barrier()


def phase2b(kb, ges, G):
    nc = kb.nc
    P = kb.P
    psum, psb = G["psum"], G["psb"]
    scr_b = G["scr_b"]
    dr = kb.dr
    W = NPAIR * 128
    NQB = T // 128
    SCALE = 128 ** -0.5
    with ExitStack() as es:
        mk_f = kb.sb(es, "mk_f3", 512, F32)
        mk = kb.sb(es, "mk3", 512, BF16)
        ones_b = kb.sb(es, "ones_b", 128, BF16)
        esink = kb.sb(es, "esink", 8, F32)
        b_c = Buf("p2bconst")
        kb.dma(SP, mk_f[:], dr["masks"].ap(), [], [b_c])
        kb.cp(mk[:], mk_f[:], [b_c], [b_c])
        kb.memset(ones_b[:], 1.0, [b_c])
        kb.dma(SP, esink[:], dr["sink_bc"].ap(), [], [b_c])
        kb.act(esink[:], esink[:], AF.Exp, [b_c], [b_c])
        ak = kb.sb(es, "ak", TL, BF16)
        av = kb.sb(es, "av", W, BF16)
        aq = kb.sb(es, "aq", 4 * T, BF16)
        ya = kb.sb(es, "ya", 4 * T, BF16)
        b_in = Buf("attin")
        b_ya = Buf("ya")
        pT = kb.sbs(es, "pT", 512, BF16, 6)
        den = kb.sbs(es, "den", 512, F32, 2)
        for kv in range(2):
            kb.dma(SP, ak[:], dr["AKT"].ap()[kv], [scr_b["AKT"]], [b_in])
            kb.dma(SP, av[:], dr["AVS"].ap()[kv], [scr_b["AVS"]], [b_in])
            for h4 in range(4):
                kb.dma(SP, aq[:, h4 * T:(h4 + 1) * T], dr["AQT"].ap()[kv * 4 + h4], [scr_b["AQT"]], [b_in])
            items = []
            for qb in range(NQB):
                kblocks = [(0, None), (1, None)]
                if qb > 0:
                    kblocks.append((2 + qb - 1, 2))
                kblocks.append((2 + qb, None))
                if qb < NQB - 1:
                    kblocks.append((2 + qb + 1, 3))
                for i, (kblk, mi) in enumerate(kblocks):
                    items.append((qb, kblk, mi, i == 0, i == len(kblocks) - 1))
            pts = [None] * len(items)

            def score(ix):
                qb, kblk, mi, first, last = items[ix]
                s_ps, s_b = psum[ix % 4], psb[ix % 4]
                kb.mm(s_ps[:], ak[:, kblk * 128:(kblk + 1) * 128], fap(aq, qb * 128, [[T, 4], [1, 128]]),
                      [b_in], [s_b])
                p_t, p_b = pT.next()
                kb.act(p_t[:], s_ps[:], AF.Exp, [s_b], [p_b], scale=SCALE)
                if mi is not None:
                    kb.tt(fap(p_t, 0, [[128, 4], [1, 128]]), fap(p_t, 0, [[128, 4], [1, 128]]),
                          fap(mk, mi * 128, [[0, 4], [1, 128]]), ALU.mult, [p_b, b_c], [p_b])
                pts[ix] = (p_t, p_b)

            LOOK = 3
            for ix in range(min(LOOK, len(items))):
                score(ix)
            for ix in range(len(items)):
                if ix + LOOK < len(items):
                    score(ix + LOOK)
                qb, kblk, mi, first, last = items[ix]
                o_ps, o_b = psum[4 + qb % 2], psb[4 + qb % 2]
                d_ps, d_b = psum[6 + qb % 2], psb[6 + qb % 2]
                p_t, p_b = pts[ix]
                kb.mm(o_ps[:], av[:, kblk * 128:(kblk + 1) * 128], p_t[:], [b_in, p_b], [o_b], start=first, stop=last)
                kb.mm(d_ps[:], ones_b[:], p_t[:], [b_c, p_b], [d_b], start=first, stop=last)
                if last:
                    dn_t, dn_b = den.next()
                    kb.tt(fap(dn_t, 0, [[128, 4], [1, 128]]), fap(d_ps, 0, [[128, 4], [1, 128]]),
                          fap(esink, kv * 4, [[1, 4], [0, 128]]), ALU.add, [d_b, b_c], [dn_b])
                    P.add(DVE, lambda e, o=dn_t: e.reciprocal(out=o[:], in_=o[:]), [dn_b], [dn_b])
                    kb.tt(fap(ya, qb * 128, [[T, 4], [1, 128]]), fap(o_ps, 0, [[128, 4], [1, 128]]),
                          fap(dn_t, 0, [[128, 4], [1, 128]]), ALU.mult, [o_b, dn_b], [b_ya])
            for h4 in range(4):
                kb.dma(SP, dr["YATT"].ap()[kv * 4 + h4], ya[:, h4 * T:(h4 + 1) * T], [b_ya], [scr_b["YATT"]])
    P.barrier()


MT3 = 512


def phase3(kb, ges, G):
    nc = kb.nc
    P = kb.P
    psum, psb = G["psum"], G["psb"]
    ident_b, b_const = G["ident_b"], G["b_const"]
    out_d = G["out_d"]
    dr = kb.dr
    n = MT3
    nblk = n // 128
    with ExitStack() as es:
        mods = kb.sb(es, "mods3", 6 * KC, F32)
        gate1 = kb.sb(es, "gate1", D, F32)
        b_c = Buf("p3const")
        kb.dma(SP, mods[:], dr["MODS"].ap()[:, 0:6 * KC], [], [b_c])
        kb.dma(SP, gate1[:], dr["MODS"].ap()[:, 6 * KC:6 * KC + D], [], [b_c])
        yr = kb.sb(es, "yr", 8 * n, BF16)
        ya = kb.sb(es, "ya3", 8 * n, BF16)
        b_yr, b_ya = Buf(), Buf()
        sgr = kb.sbs(es, "sgr", 4 * n, BF16, 2)
        sga = kb.sbs(es, "sga", 4 * n, BF16, 2)
        zT = kb.sb(es, "zT", 16 * n, BF16)
        zb = [Buf() for _ in range(16)]
        xm = [(kb.sb(es, "xm%d" % i, D, F32), Buf()) for i in range(nblk)]
        w8 = kb.sbs(es, "w8_", 8 * 512, BF16, 4)
        w16 = kb.sbs(es, "w16_", KC * 512, BF16, 2)
        tmp = kb.sbs(es, "t3_", n, F32, 4)
        xn = kb.sbs(es, "xn3_", D, BF16, nblk)
        stat = kb.sbs(es, "stat3_", 4, F32, 2)
        h2s = kb.sbs(es, "h2s", KC * 128, BF16, 2)
        unit = 0
        pend = []
        for t0 in range(0, T, n):
            kb.dma(SP, fap(yr, 0, [[n, 8], [1, n]]), bass.AP(dr["YREC"], t0, [[T, 128], [128 * T, 8], [1, n]]),
                   [], [b_yr])
            kb.dma(SP, fap(ya, 0, [[n, 8], [1, n]]), bass.AP(dr["YATT"], t0, [[T, 128], [128 * T, 8], [1, n]]),
                   [], [b_ya])
            for tb in range(nblk):
                kb.dma(SP, xm[tb][0][:], dr["x"].ap()[t0 + tb * 128:t0 + (tb + 1) * 128, :], [], [xm[tb][1]])
            for nb in range(4):
                wr, wrb = w8.next()
                wa, wab = w8.next()
                kb.dma(POOL, wr[:], dr["WRB"].ap()[nb], [], [wrb])
                kb.dma(POOL, wa[:], dr["WAB"].ap()[nb], [], [wab])
                gr, grb = sgr.next()
                ga, gab = sga.next()
                kb.dma(SP, fap(gr, 0, [[n, 4], [1, n]]),
                       bass.AP(dr["SGR"], nb * 4 * 128 * T + t0, [[T, 128], [128 * T, 4], [1, n]]), [], [grb])
                kb.dma(SP, fap(ga, 0, [[n, 4], [1, n]]),
                       bass.AP(dr["SGA"], nb * 4 * 128 * T + t0, [[T, 128], [128 * T, 4], [1, n]]), [], [gab])
                for f in range(4):
                    fb = nb * 4 + f
                    s = unit % 3
                    unit += 1
                    pa, pab = psum[2 * s], psb[2 * s]
                    pb_, pbb = psum[2 * s + 1], psb[2 * s + 1]
                    for kc in range(8):
                        kb.mm(pa[:, 0:n], wr[:, kc * 512 + f * 128: kc * 512 + (f + 1) * 128], yr[:, kc * n:(kc + 1) * n],
                              [wrb, b_yr], [pab], start=(kc == 0), stop=(kc == 7))
                    for kc in range(8):
                        kb.mm(pb_[:, 0:n], wa[:, kc * 512 + f * 128: kc * 512 + (f + 1) * 128], ya[:, kc * n:(kc + 1) * n],
                              [wab, b_ya], [pbb], start=(kc == 0), stop=(kc == 7))
                    t1, t1b = tmp.next()
                    t2, t2b = tmp.next()
                    kb.tt(t1[:], pa[:, 0:n], gr[:, f * n:(f + 1) * n], ALU.mult, [pab, grb], [t1b])
                    kb.tt(t2[:], pb_[:, 0:n], ga[:, f * n:(f + 1) * n], ALU.mult, [pbb, gab], [t2b])
                    kb.tt(zT[:, fb * n:(fb + 1) * n], t1[:], t2[:], ALU.add, [t1b, t2b], [zb[fb]])
                if nb == 0:
                    for fn in pend:
                        fn()
                    pend = []
            for nb2 in range(4):
                wo, wob = w16.next()
                kb.dma(POOL, wo[:], dr["WOB"].ap()[nb2], [], [wob])
                for tb in range(nblk):
                    s = 6 + unit % 2
                    unit += 1
                    pm, pmb = psum[s], psb[s]
                    for fb in range(16):
                        kb.mm(pm[:], zT[:, fb * n + tb * 128: fb * n + (tb + 1) * 128], wo[:, fb * 512:(fb + 1) * 512],
                              [zb[fb], wob], [pmb], start=(fb == 0), stop=(fb == 15))
                    t1, t1b = tmp.next()
                    cs = slice(nb2 * 512, (nb2 + 1) * 512)
                    kb.tt(t1[:], pm[:], gate1[:, cs], ALU.mult, [pmb, b_c], [t1b])
                    kb.tt(xm[tb][0][:, cs], t1[:], xm[tb][0][:, cs], ALU.add, [t1b, xm[tb][1]], [xm[tb][1]])
            for tb in range(nblk):
                xt, xb_ = xm[tb]
                r0 = t0 + tb * 128
                kb.dma(SP, out_d.ap()[r0:r0 + 128, :], xt[:], [xb_], [])
                st, stb = stat.next()
                xnt, xnb = xn.next()
                kb.act(xnt[:], xt[:], AF.Square, [xb_], [xnb, stb], accum=st[:, 0:1])
                kb.ts(st[:, 1:2], st[:, 0:1], 1.0 / D, EPS, ALU.mult, ALU.add, [stb], [stb])
                kb.act(st[:, 2:3], st[:, 1:2], AF.Ln, [stb], [stb])
                kb.act(st[:, 3:4], st[:, 2:3], AF.Exp, [stb], [stb], scale=-0.5)
                kb.act(xnt[:], xt[:], AF.Identity, [xb_, stb], [xnb], scale=st[:, 3:4])

                def part2(xnt=xnt, xnb=xnb, r0=r0):
                    h2, h2b = h2s.next()
                    for half in range(2):
                        pt, ptb = psum[6 + half], psb[6 + half]
                        ptv = pt.bitcast(BF16)
                        for j in range(8):
                            kc = half * 8 + j
                            kb.tr(ptv[:, j * 128:(j + 1) * 128], xnt[:, kc * 128:(kc + 1) * 128], ident_b[:],
                                  [xnb, b_const], [ptb])
                        for j in range(8):
                            kc = half * 8 + j
                            kb.ts(h2[:, kc * 128:(kc + 1) * 128], ptv[:, j * 128:(j + 1) * 128],
                                  mods[:, 64 + kc:65 + kc], mods[:, 80 + kc:81 + kc], ALU.mult, ALU.add,
                                  [ptb, b_c], [h2b])
                    kb.dma(SP, bass.AP(dr["H2T"], r0, [[T, 128], [128 * T, KC], [1, 128]]),
                           fap(h2, 0, [[128, KC], [1, 128]]), [h2b], [])
                pend.append(part2)
        for fn in pend:
            fn()
    P.barrier()


MT4 = 512


def phase4(kb, ges, G):
    nc = kb.nc
    P = kb.P
    psum, psb = G["psum"], G["psb"]
    scr_b = G["scr_b"]
    out_d, out_b = G["out_d"], G["out_b"]
    dr = kb.dr
    n = MT4
    nblk = n // 128
    HW = n + 2
    with ExitStack() as es:
        gate2 = kb.sb(es, "gate2", D, F32)
        cw = kb.sb(es, "cw", 3 * NFB, F32)
        cb = kb.sb(es, "cb", NFB, F32)
        b_c = Buf("p4const")
        kb.dma(SP, gate2[:], dr["MODS"].ap()[:, 6 * KC + D:6 * KC + 2 * D], [scr_b["MODS"]], [b_c])
        kb.dma(SP, cw[:], dr["convwT"].ap(), [], [b_c])
        kb.dma(SP, cb[:], dr["convbT"].ap(), [], [b_c])
        h2r = kb.sbs(es, "h2r", KC * HW, BF16, 2)
        acc = [[(kb.sb(es, "acc%d_%d" % (j, i), D, F32), Buf()) for i in range(nblk)] for j in range(2)]
        actT = kb.sbs(es, "actT", 11 * n, BF16, 2)
        wup = kb.sbs(es, "wup", KC * 256, BF16, 3)
        wdn = kb.sbs(es, "wdn", 11 * 512, BF16, 2)
        A_sb = kb.sbs(es, "A_sb", HW, F32, 2)
        c_sb = kb.sbs(es, "c_sb", n, F32, 2)
        s_sb = kb.sbs(es, "s_sb", n, F32, 2)
        xmr = kb.sbs(es, "xmr", D, F32, 2)
        unit = 0
        prev_epi = None
        for t0 in range(0, T, n):
            h2, h2b = h2r.next()
            lo = 1 if t0 == 0 else 0
            hi = HW - 1 if t0 + n == T else HW
            if lo:
                kb.memset(fap(h2, 0, [[HW, KC]]), 0.0, [h2b])
            if hi < HW:
                kb.memset(fap(h2, HW - 1, [[HW, KC]]), 0.0, [h2b])
            kb.dma(SP, fap(h2, lo, [[HW, KC], [1, hi - lo]]),
                   bass.AP(dr["H2T"], t0 - 1 + lo, [[T, 128], [128 * T, KC], [1, hi - lo]]), [scr_b["H2T"]], [h2b])
            for g in range(4):
                if g == 1 and prev_epi is not None:
                    prev_epi()
                    prev_epi = None
                at, atb = actT.next()
                for fi in range(11):
                    fb = g * 11 + fi
                    wu, wub = wup.next()
                    kb.dma(POOL, wu[:], dr["w_up_t"].ap()[fb], [], [wub])
                    s = unit % 2
                    unit += 1
                    pa, pab = psum[3 * s], psb[3 * s]
                    ph, phb = psum[3 * s + 1], psb[3 * s + 1]
                    pu, pub = psum[3 * s + 2], psb[3 * s + 2]
                    for kc in range(KC):
                        kb.mm(pa[:], wu[:, kc * 256: kc * 256 + 128], h2[:, kc * HW + 1: kc * HW + 1 + n],
                              [wub, h2b], [pab], start=(kc == 0), stop=(kc == KC - 1))
                    for kc in range(KC):
                        kb.mm(ph[:, 0:2], wu[:, kc * 256: kc * 256 + 128], fap(h2, kc * HW, [[HW - 1, 2]]),
                              [wub, h2b], [phb], start=(kc == 0), stop=(kc == KC - 1))
                    for kc in range(KC):
                        kb.mm(pu[:], wu[:, kc * 256 + 128: kc * 256 + 256], h2[:, kc * HW + 1: kc * HW + 1 + n],
                              [wub, h2b], [pub], start=(kc == 0), stop=(kc == KC - 1))
                    A, Ab = A_sb.next()
                    c, cbb = c_sb.next()
                    sl, slb = s_sb.next()
                    kb.cp(A[:, 1:1 + n], pa[:], [pab], [Ab], eng=ACT)
                    kb.cp(fap(A, 0, [[HW - 1, 2]]), ph[:, 0:2], [phb], [Ab], eng=ACT)
                    kb.ts(c[:], A[:, 0:n], cw[:, fb:fb + 1], cb[:, fb:fb + 1], ALU.mult, ALU.add, [Ab, b_c], [cbb])
                    kb.stt(c[:], A[:, 1:1 + n], cw[:, NFB + fb:NFB + fb + 1], c[:], ALU.mult, ALU.add,
                           [Ab, b_c, cbb], [cbb])
                    kb.stt(c[:], A[:, 2:2 + n], cw[:, 2 * NFB + fb:2 * NFB + fb + 1], c[:], ALU.mult, ALU.add,
                           [Ab, b_c, cbb], [cbb])
                    kb.act(sl[:], c[:], AF.Silu, [cbb], [slb])
                    kb.tt(at[:, fi * n:(fi + 1) * n], pu[:], sl[:], ALU.mult, [pub, slb], [atb])
                for nb in range(4):
                    wd, wdb = wdn.next()
                    kb.dma(POOL, wd[:], dr["w_down_t"].ap()[g, nb], [], [wdb])
                    for tb in range(nblk):
                        s = 6 + unit % 2
                        unit += 1
                        pd, pdb = psum[s], psb[s]
                        for fi in range(11):
                            kb.mm(pd[:], at[:, fi * n + tb * 128: fi * n + (tb + 1) * 128], wd[:, fi * 512:(fi + 1) * 512],
                                  [atb, wdb], [pdb], start=(fi == 0), stop=(fi == 10))
                        a_t, a_b = acc[(t0 // n) % 2][tb]
                        cs = slice(nb * 512, (nb + 1) * 512)
                        if g == 0:
                            kb.cp(a_t[:, cs], pd[:], [pdb], [a_b], eng=ACT)
                        else:
                            kb.tt(a_t[:, cs], pd[:], a_t[:, cs], ALU.add, [pdb, a_b], [a_b])
            def epi(t0=t0):
                for tb in range(nblk):
                    a_t, a_b = acc[(t0 // n) % 2][tb]
                    r0 = t0 + tb * 128
                    xr, xrb = xmr.next()
                    kb.dma(SP, xr[:], out_d.ap()[r0:r0 + 128, :], [], [xrb])
                    kb.tt(a_t[:], a_t[:], gate2[:], ALU.mult, [a_b, b_c], [a_b])
                    kb.tt(xr[:], a_t[:], xr[:], ALU.add, [a_b, xrb], [xrb])
                    kb.dma(SP, out_d.ap()[r0:r0 + 128, :], xr[:], [xrb], [])
            prev_epi = epi
        if prev_epi is not None:
            prev_epi()


def tile_w(w, nb=512):
    K, N = w.shape
    return np.ascontiguousarray(w.reshape(K // 128, 128, N // nb, nb).transpose(2, 1, 0, 3)).reshape(
        N // nb, 128, (K // 128) * nb)


def featT(v):
    return np.ascontiguousarray(v.reshape(-1, 128).T)


def host_consts():
    ident = np.eye(128, dtype=np.float32)
    R = np.zeros((128, 128), np.float32)
    for m in range(128):
        if (m % 64) < 32:
            R[m, m + 32] = -1.0
        else:
            R[m, m - 32] = 1.0
    RT = np.ascontiguousarray(R.T)
    s = np.arange(128)[:, None]
    c = np.arange(128)[None, :]
    same = (s // 64) == (c // 64)
    mask_f = (same & (s <= c)).astype(np.float32)
    mask_b = (same & (s >= c)).astype(np.float32)
    am_prev = (s >= c).astype(np.float32)
    am_next = (s <= c).astype(np.float32)
    masks = np.concatenate([mask_f, mask_b, am_prev, am_next], axis=1)
    t = np.arange(T)
    rows = (t // 64).astype(np.float32)
    cols = (t % 64).astype(np.float32)
    inv_freq = (10000.0 ** (-np.arange(32, dtype=np.float32) / 32)).astype(np.float32)
    C = np.zeros((128, T), np.float32)
    S = np.zeros((128, T), np.float32)
    for d in range(128):
        pos = rows if d < 64 else cols
        ang = (pos * inv_freq[d % 32]).astype(np.float32)
        C[d] = np.cos(ang)
        S[d] = np.sin(ang)
    return ident, RT, masks, C, S


_CACHE = {}


def prep(x, c, ctx, c_ctx, w_ada, b_ada, norm1_g, w_in, hgrn_lower_bounds, hgrn_norm_g, q_norm_g,
         k_norm_g, attn_sink, w_rec_proj, w_att_proj, w_out, norm2_g, w_up, conv_w, conv_b, w_down):
    f = lambda a: np.asarray(a, dtype=np.float32)
    x, c, ctx, c_ctx = f(x), f(c), f(ctx), f(c_ctx)
    B = x.shape[0]
    ident, RT, masks, C, S = host_consts()
    w_in0 = f(w_in)[0]
    cols = []
    for h in range(8):
        for seg in (0, 1, 2, 4):
            cols += list(range(seg * 1024 + h * 128, seg * 1024 + (h + 1) * 128))
    cols += list(range(3072, 4096)) + list(range(5120, 6144)) + list(range(6144, 6656)) + list(range(6656, 10752))
    w_in_t = tile_w(w_in0[:, cols])
    w_ada_t = tile_w(f(w_ada)[0])
    b_ada0 = f(b_ada)[0]
    b_adaT = featT(b_ada0)
    b_gate = np.concatenate([b_ada0[2 * D:3 * D], b_ada0[5 * D:6 * D]])
    b_gate_bc = np.ascontiguousarray(np.broadcast_to(b_gate[None, :], (128, 2 * D)))
    w_up0 = f(w_up)[0]
    ucols = []
    for fb in range(NFB):
        ucols += list(range(fb * 128, (fb + 1) * 128)) + list(range(DFF + fb * 128, DFF + (fb + 1) * 128))
    w_up_t = tile_w(w_up0[:, ucols], 256)
    w_down0 = f(w_down)[0]
    w_down_t = np.ascontiguousarray(w_down0.reshape(4, 11, 128, 4, 512).transpose(0, 3, 2, 1, 4)).reshape(
        4, 4, 128, 11 * 512)
    lbraw = f(hgrn_lower_bounds)
    lbrawT = np.ascontiguousarray(lbraw.reshape(2, 2, 8, 128).transpose(3, 0, 1, 2)).reshape(128, 32)
    vec128 = np.stack([f(hgrn_norm_g)[0], f(q_norm_g)[0], f(k_norm_g)[0]], axis=1)
    sink_bc = np.ascontiguousarray(np.broadcast_to(f(attn_sink)[0][None, :], (128, 8)))
    cw = f(conv_w)[0]
    convwT = np.ascontiguousarray(cw.reshape(3, NFB, 128).transpose(2, 0, 1)).reshape(128, 3 * NFB)
    convbT = featT(f(conv_b)[0])
    shared = {
        "w_ada_t": w_ada_t, "b_adaT": b_adaT, "b_gate_bc": b_gate_bc, "g1T": featT(f(norm1_g)[0]),
        "g2T": featT(f(norm2_g)[0]), "w_in_t": w_in_t, "lbrawT": lbrawT, "vec128": np.ascontiguousarray(vec128),
        "sink_bc": sink_bc, "w_rec_t": tile_w(f(w_rec_proj)[0]), "w_att_t": tile_w(f(w_att_proj)[0]),
        "w_out_t": tile_w(f(w_out)[0]), "w_up_t": w_up_t, "convwT": convwT, "convbT": convbT,
        "w_down_t": w_down_t, "ident": ident, "ropeRT": RT, "masks": masks, "ropeC": C, "ropeS": S,
    }
    in_maps = []
    for b in range(B):
        cv = np.stack([c[b], c_ctx], axis=0)
        cvecT = np.ascontiguousarray(cv.reshape(2, KC, 128).transpose(2, 0, 1)).reshape(128, 32)
        m = dict(shared)
        m["x"] = np.ascontiguousarray(x[b])
        m["ctx"] = np.ascontiguousarray(ctx[b])
        m["cvecT"] = cvecT
        in_maps.append(m)
    return in_maps


def kernel(**inputs):
    in_maps = prep(**inputs)
    if "nc" not in _CACHE:
        _CACHE["nc"] = build_program()
    nc = _CACHE["nc"]
    res = run_bass_kernel_spmd(nc, in_maps, core_ids=list(range(len(in_maps))))
    return np.stack([np.asarray(r["out"], dtype=np.float32) for r in res.results], axis=0)
```

```python
import numpy as np
from contextlib import ExitStack
import concourse.bass as bass
import concourse.mybir as mybir
from concourse.bass_utils import run_bass_kernel_spmd

F32 = mybir.dt.float32
BF16 = mybir.dt.bfloat16
AF = mybir.ActivationFunctionType
ALU = mybir.AluOpType

PE, ACT, DVE, POOL, SP = "pe", "act", "dve", "pool", "sp"
ENGS = (PE, ACT, DVE, POOL, SP)
DMA_ENGS = (SP, ACT, POOL)
N_DMA_SEMS = 16

T = 4096
L = 256
TL = T + L
D = 2048
KC = 16
DFF = 5632
NFB = 44
EPS = 1e-6
NCH = TL // 64
NPAIR = TL // 128

DEBUG_OUT = []
STOP_AFTER = 99


class Buf:
    __slots__ = ("name", "w", "r", "excl")

    def __init__(self, name="", excl=False):
        self.name = name
        self.w = None
        self.r = {}
        self.excl = excl


class Op:
    __slots__ = ("eng", "fn", "dma", "deps", "marked", "cnt", "sem_i", "barrier")

    def __init__(self, eng, fn, dma):
        self.eng = eng
        self.fn = fn
        self.dma = dma
        self.deps = []
        self.marked = False
        self.cnt = 0
        self.sem_i = -1
        self.barrier = 0


class Prog:
    def __init__(self):
        self.ops = []
        self.nbar = 0

    def add(self, eng, fn, reads=(), writes=(), dma=False):
        i = len(self.ops)
        op = Op(eng, fn, dma)
        writes = [b for b in writes if b is not None] + [b for b in reads if b is not None and b.excl]
        reads = [b for b in reads if b is not None and not b.excl]
        deps = set()
        for b in reads:
            if b.w is not None:
                deps.add(b.w)
        for b in writes:
            if b.w is not None:
                deps.add(b.w)
            for r in b.r.values():
                if isinstance(r, list):
                    deps.update(r)
                else:
                    deps.add(r)
        for b in reads:
            if dma:
                b.r.setdefault("dma", []).append(i)
            else:
                b.r[eng] = i
        for b in writes:
            b.w = i
            b.r = {}
        ops = self.ops
        for d in deps:
            p = ops[d]
            if p.eng == PE and eng == PE and not p.dma and not dma:
                continue
            op.deps.append(d)
            p.marked = True
        ops.append(op)
        return i

    def barrier(self):
        self.nbar += 1
        for e in ENGS:
            op = Op(e, None, False)
            op.barrier = self.nbar
            self.ops.append(op)

    def emit(self, nc, es):
        ops = self.ops
        eng_sem = {e: es.enter_context(nc.semaphore("s_" + e)) for e in ENGS}
        bar_sem = es.enter_context(nc.semaphore("s_bar"))
        dma_sems = {e: [es.enter_context(nc.semaphore("d_%s%d" % (e, k))) for k in range(N_DMA_SEMS)]
                    for e in DMA_ENGS}
        cnt = {e: 0 for e in ENGS}
        dcnt = {e: [0] * N_DMA_SEMS for e in DMA_ENGS}
        drr = {e: 0 for e in DMA_ENGS}
        for op in ops:
            if op.barrier:
                continue
            if op.dma:
                k = drr[op.eng]
                drr[op.eng] = (k + 1) % N_DMA_SEMS
                dcnt[op.eng][k] += 16
                op.sem_i = k
                op.cnt = dcnt[op.eng][k]
            elif op.marked:
                cnt[op.eng] += 1
                op.cnt = cnt[op.eng]
        per_eng = {e: [] for e in ENGS}
        for op in ops:
            per_eng[op.eng].append(op)

        def run(engobj, ename):
            known = {}
            issued = [0] * N_DMA_SEMS
            for op in per_eng[ename]:
                if op.barrier:
                    if ename in dma_sems:
                        for k in range(N_DMA_SEMS):
                            v = issued[k]
                            if v > 0 and known.get((ename, k), 0) < v:
                                engobj.wait_ge(dma_sems[ename][k], v)
                                known[(ename, k)] = v
                    engobj.drain().then_inc(bar_sem, 1)
                    engobj.wait_ge(bar_sem, len(ENGS) * op.barrier)
                    continue
                need = {}
                for d in op.deps:
                    p = ops[d]
                    key = (p.eng, p.sem_i) if p.dma else (p.eng, -1)
                    if need.get(key, 0) < p.cnt:
                        need[key] = p.cnt
                if op.dma and op.cnt > 16:
                    key = (ename, op.sem_i)
                    if need.get(key, 0) < op.cnt - 16:
                        need[key] = op.cnt - 16
                for key, v in need.items():
                    if known.get(key, 0) >= v:
                        continue
                    known[key] = v
                    sem = dma_sems[key[0]][key[1]] if key[1] >= 0 else eng_sem[key[0]]
                    engobj.wait_ge(sem, v)
                ins = op.fn(engobj)
                if op.dma:
                    ins.then_inc(dma_sems[ename][op.sem_i], 16)
                    issued[op.sem_i] = op.cnt
                elif op.marked:
                    ins.then_inc(eng_sem[ename], 1)
            if ename in dma_sems:
                for k in range(N_DMA_SEMS):
                    v = issued[k]
                    if v > 0 and known.get((ename, k), 0) < v:
                        engobj.wait_ge(dma_sems[ename][k], v)

        with nc.Block() as block:
            @block.tensor
            def _(e):
                run(e, PE)

            @block.scalar
            def _(e):
                run(e, ACT)

            @block.vector
            def _(e):
                run(e, DVE)

            @block.gpsimd
            def _(e):
                run(e, POOL)

            @block.sync
            def _(e):
                run(e, SP)


class Rot:
    def __init__(self, items):
        self.items = items
        self.i = 0

    def next(self):
        it = self.items[self.i % len(self.items)]
        self.i += 1
        return it


class KB:
    def __init__(self):
        self.nc = bass.Bass("TRN2", target_bir_lowering=False)
        self.P = Prog()
        self.dr = {}
        self.drb = {}

    def din(self, name, shape, dt=F32):
        t = self.nc.dram_tensor(name, list(shape), dt, kind="ExternalInput")
        self.dr[name] = t
        self.drb[name] = Buf(name)
        return t

    def dscr(self, name, shape, dt):
        kind = "ExternalOutput" if name in DEBUG_OUT else "Internal"
        t = self.nc.dram_tensor(name, list(shape), dt, kind=kind)
        self.dr[name] = t
        return t

    def sb(self, es, name, free, dt):
        return es.enter_context(self.nc.sbuf_tensor(name, [128, free], dt))

    def sbs(self, es, name, free, dt, n):
        return Rot([(self.sb(es, "%s%d" % (name, i), free, dt), Buf("%s%d" % (name, i))) for i in range(n)])

    def mm(self, out, lhsT, rhs, R, W, start=True, stop=True):
        self.P.add(PE, lambda e: e.matmul(out, lhsT=lhsT, rhs=rhs, start=start, stop=stop), R, W)

    def tr(self, out, in_, ident, R, W):
        self.P.add(PE, lambda e: e.transpose(out=out, in_=in_, identity=ident), R, W)

    def act(self, out, in_, func, R, W, scale=1.0, bias=0.0, accum=None):
        if accum is None:
            self.P.add(ACT, lambda e: e.activation(out=out, in_=in_, func=func, bias=bias, scale=scale), R, W)
        else:
            self.P.add(ACT, lambda e: e.activation(out=out, in_=in_, func=func, bias=bias, scale=scale,
                                                   accum_out=accum), R, W)

    def ts(self, out, in0, s1, s2, op0, op1, R, W, eng=DVE):
        self.P.add(eng, lambda e: e.tensor_scalar(out=out, in0=in0, scalar1=s1, scalar2=s2, op0=op0, op1=op1), R, W)

    def tt(self, out, in0, in1, op, R, W, eng=DVE):
        self.P.add(eng, lambda e: e.tensor_tensor(out=out, in0=in0, in1=in1, op=op), R, W)

    def stt(self, out, in0, scalar, in1, op0, op1, R, W):
        self.P.add(DVE, lambda e: e.scalar_tensor_tensor(out=out, in0=in0, scalar=scalar, in1=in1, op0=op0, op1=op1),
                   R, W)

    def scan(self, out, d0, d1, R, W):
        self.P.add(DVE, lambda e: e.tensor_tensor_scan(out=out, data0=d0, data1=d1, initial=0.0,
                                                       op0=ALU.mult, op1=ALU.add), R, W)

    def cp(self, out, in_, R, W, eng=DVE):
        if eng == ACT:
            self.P.add(ACT, lambda e: e.activation(out=out, in_=in_, func=AF.Copy), R, W)
        else:
            self.P.add(eng, lambda e: e.tensor_copy(out=out, in_=in_), R, W)

    def memset(self, ap, val, W, eng=DVE):
        self.P.add(eng, lambda e: e.memset(ap, val), (), W)

    def dma(self, q, out, in_, R, W):
        self.P.add(q, lambda e: e.dma_start(out=out, in_=in_), R, W, dma=True)


def fap(t, col, dims, p0=0, npart=128):
    F = t.shape[1]
    return bass.AP(t, p0 * F + col, [[F, npart]] + [list(d) for d in dims])


MT = 512


def build_program():
    kb = KB()
    nc = kb.nc
    P = kb.P
    x_d = kb.din("x", [T, D])
    ctx_d = kb.din("ctx", [L, D])
    cvec_d = kb.din("cvecT", [128, 32])
    wada_d = kb.din("w_ada_t", [24, 128, KC * 512])
    badaT_d = kb.din("b_adaT", [128, 96])
    bgate_d = kb.din("b_gate_bc", [128, 2 * D])
    g1_d = kb.din("g1T", [128, KC])
    g2_d = kb.din("g2T", [128, KC])
    win_d = kb.din("w_in_t", [21, 128, KC * 512])
    lbraw_d = kb.din("lbrawT", [128, 32])
    vec128_d = kb.din("vec128", [128, 3])
    sink_d = kb.din("sink_bc", [128, 8])
    wrec_d = kb.din("w_rec_t", [4, 128, 8 * 512])
    watt_d = kb.din("w_att_t", [4, 128, 8 * 512])
    wout_d = kb.din("w_out_t", [4, 128, KC * 512])
    wup_d = kb.din("w_up_t", [NFB, 128, KC * 256])
    convw_d = kb.din("convwT", [128, 3 * NFB])
    convb_d = kb.din("convbT", [128, NFB])
    wdown_d = kb.din("w_down_t", [4, 4, 128, 11 * 512])
    ident_d = kb.din("ident", [128, 128])
    rt_d = kb.din("ropeRT", [128, 128])
    mask_d = kb.din("masks", [128, 4 * 128])
    cos_d = kb.din("ropeC", [128, T])
    sin_d = kb.din("ropeS", [128, T])
    out_d = nc.dram_tensor("out", [T, D], F32, kind="ExternalOutput")
    out_b = None

    QDF = kb.dscr("QDF", [8, 128, TL], BF16)
    KDF = kb.dscr("KDF", [8, 128, TL], BF16)
    QDB = kb.dscr("QDB", [8, 128, TL], BF16)
    KDB = kb.dscr("KDB", [8, 128, TL], BF16)
    KDFt = kb.dscr("KDFt", [8, 128, NPAIR * 128], BF16)
    KDBt = kb.dscr("KDBt", [8, 128, NPAIR * 128], BF16)
    VS = kb.dscr("VS", [8, 128, NPAIR * 128], BF16)
    GT = kb.dscr("GT", [8, 128, T], BF16)
    AQT = kb.dscr("AQT", [8, 128, T], BF16)
    AKT = kb.dscr("AKT", [2, 128, TL], BF16)
    AVS = kb.dscr("AVS", [2, 128, NPAIR * 128], BF16)
    SGR = kb.dscr("SGR", [16, 128, T], BF16)
    SGA = kb.dscr("SGA", [16, 128, T], BF16)
    YREC = kb.dscr("YREC", [8, 128, T], BF16)
    YATT = kb.dscr("YATT", [8, 128, T], BF16)
    H2T = kb.dscr("H2T", [KC, 128, T], BF16)
    kb.dscr("WRB", [4, 128, 8 * 512], BF16)
    kb.dscr("WAB", [4, 128, 8 * 512], BF16)
    kb.dscr("WOB", [4, 128, KC * 512], BF16)
    CSC = kb.dscr("CSC", [128, 3 * 16 * NCH], F32)
    MODS = kb.dscr("MODS", [128, 6 * KC + 2 * D], F32)
    scr_b = {n: None for n in ("QDF", "KDF", "QDB", "KDB", "KDFt", "KDBt", "VS", "GT", "AQT", "AKT", "AVS",
                                 "SGR", "SGA", "YREC", "YATT", "H2T", "CSC", "MODS")}

    with ExitStack() as ges:
        ident_f = kb.sb(ges, "ident_f", 128, F32)
        ident_b = kb.sb(ges, "ident_b", 128, BF16)
        ones_f = kb.sb(ges, "ones_f", 128, F32)
        b_const = Buf("const")

        psum = [ges.enter_context(nc.psum_tensor("ps%d" % i, [128, 512], F32)) for i in range(8)]
        psb = [Buf("ps%d" % i, excl=True) for i in range(8)]

        kb.dma(SP, ident_f[:], ident_d.ap(), [], [b_const])
        kb.cp(ident_b[:], ident_f[:], [b_const], [b_const])
        kb.memset(ones_f[:], 1.0, [b_const])

        with ExitStack() as es:
            cv_f = kb.sb(es, "cv_f", 32, F32)
            csil = kb.sb(es, "csil", 32, BF16)
            crep = kb.sb(es, "crep", KC * 128, BF16)
            badaT = kb.sb(es, "badaT", 96, F32)
            bgate = kb.sb(es, "bgate", 2 * D, F32)
            g1s = kb.sb(es, "g1s", KC, F32)
            g2s = kb.sb(es, "g2s", KC, F32)
            modT = kb.sb(es, "modT", 192, F32)
            mods = kb.sb(es, "mods", 6 * KC + 2 * D, F32)
            b0 = Buf("p0")
            b_mods = Buf("mods")
            wb = kb.sbs(es, "wb0_", KC * 512, BF16, 3)
            kb.dma(SP, cv_f[:], cvec_d.ap(), [], [b0])
            kb.dma(SP, badaT[:], badaT_d.ap(), [], [b0])
            kb.dma(SP, bgate[:], bgate_d.ap(), [], [b0])
            kb.dma(SP, g1s[:], g1_d.ap(), [], [b0])
            kb.dma(SP, g2s[:], g2_d.ap(), [], [b0])
            kb.act(csil[:], cv_f[:], AF.Silu, [b0], [b0])
            kb.cp(fap(crep, 0, [[128, KC], [1, 128]]), fap(csil, 0, [[1, KC], [0, 128]]), [b0], [b0])
            ps_mod = psum[0]
            gi = 0
            for cb in range(24):
                j = cb // 4
                wt, wbuf = wb.next()
                kb.dma(POOL, wt[:], wada_d.ap()[cb], [], [wbuf])
                if j in (2, 5):
                    ps = psum[1 + gi % 2]
                    pb = psb[1 + gi % 2]
                    gi += 1
                    for kc in range(KC):
                        kb.mm(ps[:], crep[:, kc * 128:(kc + 1) * 128], wt[:, kc * 512:(kc + 1) * 512],
                              [b0, wbuf], [pb], start=(kc == 0), stop=(kc == KC - 1))
                    gcol = ((0 if j == 2 else 1) * D) + (cb % 4) * 512
                    kb.tt(mods[:, 6 * KC + gcol: 6 * KC + gcol + 512], ps[:], bgate[:, gcol:gcol + 512], ALU.add,
                          [pb, b0], [b_mods])
                else:
                    for f in range(4):
                        blk = cb * 4 + f
                        for kc in range(KC):
                            kb.mm(ps_mod[:, blk * 2: blk * 2 + 2],
                                  wt[:, kc * 512 + f * 128: kc * 512 + (f + 1) * 128],
                                  fap(csil, kc, [[16, 2]]),
                                  [b0, wbuf], [psb[0]], start=(kc == 0), stop=(kc == KC - 1))
            kb.tt(fap(modT, 0, [[2, 96], [1, 2]]), fap(ps_mod, 0, [[2, 96], [1, 2]]),
                  fap(badaT, 0, [[1, 96], [0, 2]]), ALU.add, [psb[0], b0], [b0])

            def modv(j, v):
                return fap(modT, (j * 16) * 2 + v, [[2, KC]])
            kb.stt(mods[:, 0:16], modv(1, 0), 1.0, g1s[:], ALU.add, ALU.mult, [b0], [b_mods])
            kb.cp(mods[:, 16:32], modv(0, 0), [b0], [b_mods])
            kb.stt(mods[:, 32:48], modv(1, 1), 1.0, g1s[:], ALU.add, ALU.mult, [b0], [b_mods])
            kb.cp(mods[:, 48:64], modv(0, 1), [b0], [b_mods])
            kb.stt(mods[:, 64:80], modv(4, 0), 1.0, g2s[:], ALU.add, ALU.mult, [b0], [b_mods])
            kb.cp(mods[:, 80:96], modv(3, 0), [b0], [b_mods])
            kb.dma(SP, MODS.ap(), mods[:], [b_mods], [scr_b["MODS"]])
        P.barrier()
        G = locals()
        if STOP_AFTER >= 1:
            phase1(kb, ges, G)
        if STOP_AFTER >= 2:
            phase2a(kb, ges, G)
        if STOP_AFTER >= 3:
            phase2b(kb, ges, G)
        if STOP_AFTER >= 4:
            phase3(kb, ges, G)
        if STOP_AFTER >= 5:
            phase4(kb, ges, G)
        P.emit(nc, ges)
    return nc


def phase1(kb, ges, G):
    nc = kb.nc
    P = kb.P
    psum, psb = G["psum"], G["psb"]
    ident_b, ones_f, b_const = G["ident_b"], G["ones_f"], G["b_const"]
    dr = kb.dr
    with ExitStack() as es:
        mods = kb.sb(es, "mods1", 6 * KC, F32)
        lbr = kb.sb(es, "lbr", 32, F32)
        lbv = kb.sb(es, "lbv", 16, F32)
        oml = kb.sb(es, "oml", 16, F32)
        noml = kb.sb(es, "noml", 16, F32)
        v128 = kb.sb(es, "v128", 3, F32)
        rt_f = kb.sb(es, "rt_f", 128, F32)
        rt_b = kb.sb(es, "rt_b", 128, BF16)
        smask = kb.sb(es, "smask", MT, F32)
        csc = kb.sb(es, "csc", 3 * 16 * NCH, F32)
        b_c = Buf("p1const")
        b_csc = Buf("csc")
        kb.dma(SP, mods[:], dr["MODS"].ap()[:, 0:6 * KC], [], [b_c])
        kb.dma(SP, lbr[:], dr["lbrawT"].ap(), [], [b_c])
        kb.dma(SP, v128[:], dr["vec128"].ap(), [], [b_c])
        kb.dma(SP, rt_f[:], dr["ropeRT"].ap(), [], [b_c])
        kb.cp(rt_b[:], rt_f[:], [b_c], [b_c])
        kb.tt(lbv[:], lbr[:, 0:16], lbr[:, 16:32], ALU.subtract, [b_c], [b_c])
        kb.act(lbv[:], lbv[:], AF.Sigmoid, [b_c], [b_c])
        kb.ts(oml[:], lbv[:], -1.0, 1.0, ALU.mult, ALU.add, [b_c], [b_c])
        kb.ts(noml[:], lbv[:], 1.0, -1.0, ALU.mult, ALU.add, [b_c], [b_c])
        kb.memset(smask[:], 1.0, [b_c])
        kb.memset(fap(smask, 0, [[64, MT // 64]]), 0.0, [b_c])
        kb.memset(csc[:], 1.0, [b_csc])

        hTs = [(kb.sb(es, "hT%d" % i, KC * MT, BF16), [Buf("hT%d_%d" % (i, k)) for k in range(KC)]) for i in range(2)]
        xs = kb.sbs(es, "xs", D, F32, 2)
        xn = kb.sbs(es, "xn", D, BF16, MT // 128)
        stat = kb.sbs(es, "stat", 4, F32, 2)
        wb = kb.sbs(es, "wb1_", KC * 512, BF16, 2)
        ev = [[(kb.sb(es, "ev%d_%d" % (s, i), MT, F32), Buf()) for i in range(3)] for s in range(2)]
        sh = [(kb.sb(es, "sh%d" % i, MT, F32), Buf()) for i in range(13)]
        stg = {nm: kb.sbs(es, "st_" + nm, MT, BF16, 2) for nm in ("qdf", "kdf", "qdb", "kdb", "g", "kof", "kob")}
        stg_kt = kb.sbs(es, "st_kt", 2 * MT, BF16, 2)
        stg_v = kb.sbs(es, "st_v", 4 * MT, BF16, 1)
        stg_q = kb.sbs(es, "st_q", MT, BF16, 3)
        stg_av = kb.sbs(es, "st_av", 2 * MT, BF16, 1)
        ropeC = kb.sb(es, "ropeC_sb", MT, F32)
        ropeS = kb.sb(es, "ropeS_sb", MT, F32)
        b_rope = Buf("rope")
        xg_t = kb.sbs(es, "xg", MT, BF16, 3)

        tiles = [("c", 0, L)] + [("x", t0, MT) for t0 in range(0, T, MT)]
        pend = []
        cnt = {"u": 0, "q": 0, "s": 0, "r": 0}

        def flush(all_=False):
            keep = []
            for dly, fn in pend:
                if all_ or dly <= 1:
                    fn()
                else:
                    keep.append((dly - 1, fn))
            pend[:] = keep

        def stage_a0(ti, tb):
            kind, t0, n = tiles[ti]
            src = dr["x"] if kind == "x" else dr["ctx"]
            xt, xb_ = xs.next()
            kb.dma(SP, xt[:], src.ap()[t0 + tb * 128: t0 + (tb + 1) * 128, :], [], [xb_])
            return (xt, xb_)

        def stage_a1(ti, loaded=None):
            kind, t0, n = tiles[ti]
            res = []
            for tb in range(n // 128):
                if loaded is None:
                    xt, xb_ = stage_a0(ti, tb)
                else:
                    xt, xb_ = loaded[tb]
                xnt, xnb = xn.next()
                st, stb = stat.next()
                kb.act(xnt[:], xt[:], AF.Square, [xb_], [xnb, stb], accum=st[:, 0:1])
                kb.ts(st[:, 1:2], st[:, 0:1], 1.0 / D, EPS, ALU.mult, ALU.add, [stb], [stb])
                kb.act(st[:, 2:3], st[:, 1:2], AF.Ln, [stb], [stb])
                kb.act(st[:, 3:4], st[:, 2:3], AF.Exp, [stb], [stb], scale=-0.5)
                kb.act(xnt[:], xt[:], AF.Identity, [xb_, stb], [xnb], scale=st[:, 3:4])
                res.append((xnt, xnb))
            return res

        def stage_a1_one(ti, ld):
            xt, xb_ = ld
            xnt, xnb = xn.next()
            st, stb = stat.next()
            kb.act(xnt[:], xt[:], AF.Square, [xb_], [xnb, stb], accum=st[:, 0:1])
            kb.ts(st[:, 1:2], st[:, 0:1], 1.0 / D, EPS, ALU.mult, ALU.add, [stb], [stb])
            kb.act(st[:, 2:3], st[:, 1:2], AF.Ln, [stb], [stb])
            kb.act(st[:, 3:4], st[:, 2:3], AF.Exp, [stb], [stb], scale=-0.5)
            kb.act(xnt[:], xt[:], AF.Identity, [xb_, stb], [xnb], scale=st[:, 3:4])
            return (xnt, xnb)

        def stage_a2(ti, xns):
            kind, t0, n = tiles[ti]
            hT, hTb = hTs[ti % 2]
            a_off = 0 if kind == "x" else 32
            for tb, (xnt, xnb) in enumerate(xns):
                for half in range(2):
                    pt, ptb = psum[6 + half], psb[6 + half]
                    ptv = pt.bitcast(BF16)
                    for j in range(8):
                        kc = half * 8 + j
                        kb.tr(ptv[:, j * 128:(j + 1) * 128], xnt[:, kc * 128:(kc + 1) * 128], ident_b[:],
                              [xnb, b_const], [ptb])
                    for j in range(8):
                        kc = half * 8 + j
                        kb.ts(hT[:, kc * MT + tb * 128: kc * MT + (tb + 1) * 128], ptv[:, j * 128:(j + 1) * 128],
                              mods[:, a_off + kc: a_off + kc + 1], mods[:, a_off + 16 + kc: a_off + 17 + kc],
                              ALU.mult, ALU.add, [ptb, b_c], [hTb[kc]])

        xns0 = stage_a1(0)
        stage_a2(0, xns0)
        for ti, (kind, t0, n) in enumerate(tiles):
            is_x = kind == "x"
            g0 = (L + t0) if is_x else 0
            nblk = n // 128
            hT, hTb = hTs[ti % 2]
            if is_x:
                kb.dma(SP, ropeC[:, 0:n], dr["ropeC"].ap()[:, t0:t0 + n], [], [b_rope])
                kb.dma(SP, ropeS[:, 0:n], dr["ropeS"].ap()[:, t0:t0 + n], [], [b_rope])
            if is_x:
                blocks = []
                for i in range(8):
                    blocks += [i, 13 + i]
                blocks += [8, 9, 10, 11, 12]
            else:
                blocks = list(range(8)) + [8, 9, 12]
            tail_delay = 5 if is_x else 1
            k1 = 8 if is_x else 1
            k2 = 16 if is_x else 8
            nxt = None
            for bi, blk in enumerate(blocks):
                wt, wbuf = wb.next()
                kb.dma(POOL, wt[:], dr["w_in_t"].ap()[blk], [], [wbuf])

                def gemmB(ps, pb, col0, wt=wt, wbuf=wbuf):
                    for kc in range(KC):
                        kb.mm(ps[:, 0:n], wt[:, kc * 512 + col0: kc * 512 + col0 + 128], hT[:, kc * MT: kc * MT + n],
                              [wbuf, hTb[kc]], [pb], start=(kc == 0), stop=(kc == KC - 1))

                def gemmA(ps, pb, tb, col0, ncol, wt=wt, wbuf=wbuf):
                    for kc in range(KC):
                        kb.mm(ps[:, 0:ncol], hT[:, kc * MT + tb * 128: kc * MT + (tb + 1) * 128],
                              wt[:, kc * 512 + col0: kc * 512 + col0 + ncol],
                              [wbuf, hTb[kc]], [pb], start=(kc == 0), stop=(kc == KC - 1))

                if blk < 8:
                    h = blk
                    s = 0 if is_x else cnt["u"] % 2
                    cnt["u"] += 1
                    pq, pf, pbk = psum[3 * s], psum[3 * s + 1], psum[3 * s + 2]
                    bq, bf_, bbk = psb[3 * s], psb[3 * s + 1], psb[3 * s + 2]
                    gemmB(pq, bq, 0)
                    gemmB(pf, bf_, 128)
                    gemmB(pbk, bbk, 256)
                    (qs, qsb), (sf, sfb), (sbw, sbwb) = ev[s]
                    (lf, lfb), (lbk, lbkb), (kf, kfb), (kbk, kbkb), (pfw, pfwb), (pbw, pbwb), (peb, pebb), \
                        (df, dfb), (db, dbb), (e1, e1b), (e3, e3b), (d2, d2b), (e6, e6b) = sh
                    kb.act(qs[:, 0:n], pq[:, 0:n], AF.Silu, [bq], [qsb])
                    kb.act(sf[:, 0:n], pf[:, 0:n], AF.Sigmoid, [bf_], [sfb])
                    kb.act(sbw[:, 0:n], pbk[:, 0:n], AF.Sigmoid, [bbk], [sbwb])
                    if is_x:
                        pg, bg = psum[6], psb[6]
                        gemmB(pg, bg, 384)
                        gt, gb = stg["g"].next()
                        kb.act(gt[:, 0:n], pg[:, 0:n], AF.Silu, [bg], [gb])
                        kb.dma(SP, dr["GT"].ap()[h, :, t0:t0 + n], gt[:, 0:n], [gb], [])
                    flush()
                    nch = n // 64
                    c0 = g0 // 64
                    for di, (sg, sgb, lg, lgb, kk, kkb) in enumerate(((sf, sfb, lf, lfb, kf, kfb),
                                                                     (sbw, sbwb, lbk, lbkb, kbk, kbkb))):
                        hd = di * 8 + h
                        kb.act(lg[:, 0:n], sg[:, 0:n], AF.Ln, [sgb, b_c], [lgb],
                               scale=oml[:, hd:hd + 1], bias=lbv[:, hd:hd + 1])
                        kb.ts(kk[:, 0:n], sg[:, 0:n], noml[:, hd:hd + 1], oml[:, hd:hd + 1], ALU.mult, ALU.add,
                              [sgb, b_c], [kkb])
                    v3 = lambda t_, off: fap(t_, off, [[64, nch], [1, 64]])
                    bc3 = lambda t_, off: fap(t_, off, [[64, nch], [0, 64]])
                    kb.scan(pfw[:, 0:n], smask[:, 0:n], lf[:, 0:n], [b_c, lfb], [pfwb])
                    kb.tt(v3(df, 0), v3(pfw, 0), bc3(pfw, 32), ALU.subtract, [pfwb], [dfb])
                    kb.tt(v3(d2, 0), v3(pfw, 0), bc3(pfw, 63), ALU.subtract, [pfwb], [d2b])
                    kb.scan(pbw[:, 0:n], smask[:, 0:n], lbk[:, 0:n], [b_c, lbkb], [pbwb])
                    kb.tt(peb[:, 0:n], pbw[:, 0:n], lbk[:, 0:n], ALU.subtract, [pbwb, lbkb], [pebb])
                    kb.tt(v3(db, 0), v3(peb, 0), bc3(peb, 31), ALU.subtract, [pebb], [dbb])
                    kb.act(e1[:, 0:n], df[:, 0:n], AF.Exp, [dfb], [e1b])
                    kb.act(df[:, 0:n], df[:, 0:n], AF.Exp, [dfb], [dfb], scale=-1.0)
                    kb.act(e3[:, 0:n], db[:, 0:n], AF.Exp, [dbb], [e3b], scale=-1.0)
                    kb.act(db[:, 0:n], db[:, 0:n], AF.Exp, [dbb], [dbb])
                    kb.act(d2[:, 0:n], d2[:, 0:n], AF.Exp, [d2b], [d2b], scale=-1.0)
                    kb.act(e6[:, 0:n], peb[:, 0:n], AF.Exp, [pebb], [e6b])

                    def cs(k, hd, c):
                        return fap(csc, (k * 16 + hd) * NCH + c, [[1, nch]])
                    hf, hb = h, 8 + h
                    kb.act(cs(0, hf, c0), fap(pfw, 32, [[64, nch]]), AF.Exp, [pfwb], [b_csc])
                    kb.act(cs(2, hf, c0), fap(pfw, 63, [[64, nch]]), AF.Exp, [pfwb], [b_csc])
                    kb.tt(cs(0, hb, c0), fap(pbw, 63, [[64, nch]]), fap(peb, 31, [[64, nch]]), ALU.subtract,
                          [pbwb, pebb], [b_csc])
                    kb.act(cs(0, hb, c0), cs(0, hb, c0), AF.Exp, [b_csc], [b_csc])
                    kb.act(cs(2, hb, c0), fap(pbw, 63, [[64, nch]]), AF.Exp, [pbwb], [b_csc])
                    outs = {}
                    for nm, a_, ab, b_, bb, dst in (("qdf", qs, qsb, e1, e1b, "QDF"), ("kdf", kf, kfb, df, dfb, "KDF"),
                                                    ("qdb", qs, qsb, e3, e3b, "QDB"), ("kdb", kbk, kbkb, db, dbb, "KDB"),
                                                    ("kof", kf, kfb, d2, d2b, None), ("kob", kbk, kbkb, e6, e6b, None)):
                        ot, ob = stg[nm].next()
                        kb.tt(ot[:, 0:n], a_[:, 0:n], b_[:, 0:n], ALU.mult, [ab, bb], [ob])
                        if dst is not None:
                            kb.dma(SP, dr[dst].ap()[h, :, g0:g0 + n], ot[:, 0:n], [ob], [])
                        outs[nm] = (ot, ob)

                    def tail(h=h, n=n, nblk=nblk, g0=g0, kof=outs["kof"], kob=outs["kob"]):
                        pt, ptb = psum[7], psb[7]
                        ptv = pt.bitcast(BF16)
                        kt, ktb = stg_kt.next()
                        for di, (ot, ob) in enumerate((kof, kob)):
                            for tb in range(nblk):
                                kb.tr(ptv[:, (di * nblk + tb) * 128:(di * nblk + tb + 1) * 128],
                                      ot[:, tb * 128:(tb + 1) * 128], ident_b[:], [ob, b_const], [ptb])
                        kb.cp(kt[:, 0:2 * n], ptv[:, 0:2 * n], [ptb], [ktb], eng=ACT)
                        for di, nm in enumerate(("KDFt", "KDBt")):
                            kb.dma(SP, dr[nm].ap()[h, :, (g0 // 128) * 128:(g0 // 128 + nblk) * 128],
                                   kt[:, di * n:(di + 1) * n], [ktb], [])
                    pend.append((tail_delay, tail))
                elif blk in (8, 9):
                    vt, vb = stg_v.next()
                    for tb in range(nblk):
                        s = cnt["s"] % 6
                        cnt["s"] += 1
                        gemmA(psum[s], psb[s], tb, 0, 512)
                        kb.cp(fap(vt, tb * 128, [[nblk * 128, 4], [1, 128]]), fap(psum[s], 0, [[128, 4], [1, 128]]),
                              [psb[s]], [vb], eng=ACT)
                        if tb == 0:
                            flush()
                    for h4 in range(4):
                        h = (blk - 8) * 4 + h4
                        kb.dma(SP, dr["VS"].ap()[h, :, (g0 // 128) * 128:(g0 // 128 + nblk) * 128],
                               vt[:, h4 * nblk * 128:(h4 + 1) * nblk * 128], [vb], [])
                elif blk in (10, 11, 12):
                    nh = 4 if blk < 12 else 2
                    for f in range(nh):
                        qi = cnt["q"]
                        cnt["q"] += 1
                        pq, bq = psum[qi % 3], psb[qi % 3]
                        pss, bss = psum[3 + qi % 2], psb[3 + qi % 2]
                        prx, brx = psum[5 + qi % 2], psb[5 + qi % 2]
                        (sq, sqb) = ev[qi % 2][0]
                        (rs, rsb) = ev[qi % 2][1]
                        (t1, t1b), (t2, t2b) = sh[0], sh[1]
                        gemmB(pq, bq, f * 128)
                        gcol = 1 if blk < 12 else 2
                        kb.act(sq[:, 0:n], pq[:, 0:n], AF.Square, [bq], [sqb])
                        flush()
                        ot, ob = stg_q.next()
                        xg, xgb = xg_t.next()
                        if blk < 12:
                            dst = dr["AQT"].ap()[(blk - 10) * 4 + f, :, t0:t0 + n]
                        else:
                            dst = dr["AKT"].ap()[f, :, g0:g0 + n]

                        def tail1(n=n, pq=pq, bq=bq, pss=pss, bss=bss, sq=sq, sqb=sqb, rs=rs, rsb=rsb, xg=xg, xgb=xgb,
                                  ot=ot, ob=ob, gcol=gcol, is_x=is_x, dst=dst):
                            kb.mm(pss[:, 0:n], ones_f[:], sq[:, 0:n], [b_const, sqb], [bss])
                            kb.act(rs[:, 0:n], pss[:, 0:n], AF.Ln, [bss], [rsb], scale=1.0 / 128, bias=EPS)
                            kb.act(rs[:, 0:n], rs[:, 0:n], AF.Exp, [rsb], [rsb], scale=-0.5)
                            if is_x:
                                kb.stt(xg[:, 0:n], pq[:, 0:n], v128[:, gcol:gcol + 1], rs[:, 0:n], ALU.mult, ALU.mult,
                                       [bq, b_c, rsb], [xgb])
                            else:
                                kb.stt(ot[:, 0:n], pq[:, 0:n], v128[:, gcol:gcol + 1], rs[:, 0:n], ALU.mult, ALU.mult,
                                       [bq, b_c, rsb], [ob])
                                kb.dma(SP, dst, ot[:, 0:n], [ob], [])

                        def tail2(n=n, prx=prx, brx=brx, xg=xg, xgb=xgb, ot=ot, ob=ob, t1=t1, t1b=t1b, t2=t2, t2b=t2b,
                                  dst=dst):
                            kb.mm(prx[:, 0:n], rt_b[:], xg[:, 0:n], [b_c, xgb], [brx])
                            kb.tt(t1[:, 0:n], xg[:, 0:n], ropeC[:, 0:n], ALU.mult, [xgb, b_rope], [t1b])
                            kb.tt(t2[:, 0:n], prx[:, 0:n], ropeS[:, 0:n], ALU.mult, [brx, b_rope], [t2b])
                            kb.tt(ot[:, 0:n], t1[:, 0:n], t2[:, 0:n], ALU.add, [t1b, t2b], [ob])
                            kb.dma(SP, dst, ot[:, 0:n], [ob], [])
                        pend.append((1, tail1))
                        if is_x:
                            pend.append((2, tail2))
                    if blk == 12:
                        vt, vb = stg_av.next()
                        for tb in range(nblk):
                            s = 3 + cnt["s"] % 4
                            cnt["s"] += 1
                            gemmA(psum[s], psb[s], tb, 256, 256)
                            kb.cp(fap(vt, tb * 128, [[nblk * 128, 2], [1, 128]]),
                                  fap(psum[s], 0, [[128, 2], [1, 128]]), [psb[s]], [vb], eng=ACT)
                            flush()
                        for kv in range(2):
                            kb.dma(SP, dr["AVS"].ap()[kv, :, (g0 // 128) * 128:(g0 // 128 + nblk) * 128],
                                   vt[:, kv * nblk * 128:(kv + 1) * nblk * 128], [vb], [])
                else:
                    for f in range(4):
                        s = 3 + cnt["s"] % 3
                        cnt["s"] += 1
                        gemmB(psum[s], psb[s], f * 128)
                        ot, ob = stg_q.next()
                        kb.act(ot[:, 0:n], psum[s][:, 0:n], AF.Sigmoid, [psb[s]], [ob])
                        flush()
                        fb = (blk - 13) * 4 + f
                        nm = "SGR" if fb < 16 else "SGA"
                        kb.dma(SP, dr[nm].ap()[fb % 16, :, t0:t0 + n], ot[:, 0:n], [ob], [])
                if ti + 1 < len(tiles):
                    rel = bi - k1
                    nb_next = tiles[ti + 1][2] // 128
                    if rel == 0:
                        nxt_ld, nxt = [], []
                    if 2 <= rel < nb_next + 2:
                        tbn = rel - 2
                        one = stage_a1_one(ti + 1, nxt_ld[tbn])
                        nxt.append(one)
                    if 0 <= rel < nb_next:
                        nxt_ld.append(stage_a0(ti + 1, rel))
                    if bi == k2:
                        stage_a2(ti + 1, nxt)
            flush(all_=True)
        kb.dma(SP, dr["CSC"].ap(), csc[:], [b_csc], [])
    P.barrier()


def phase2a(kb, ges, G):
    nc = kb.nc
    P = kb.P
    psum, psb = G["psum"], G["psb"]
    scr_b = G["scr_b"]
    ones_f, b_const = G["ones_f"], G["b_const"]
    dr = kb.dr
    W = NPAIR * 128
    with ExitStack() as es:
        mk_f = kb.sb(es, "mk_f2", 512, F32)
        mk = kb.sb(es, "mk2", 512, BF16)
        v128 = kb.sb(es, "v128_2", 3, F32)
        b_c = Buf("p2const")
        kb.dma(SP, mk_f[:], dr["masks"].ap(), [], [b_c])
        kb.cp(mk[:], mk_f[:], [b_c], [b_c])
        kb.dma(SP, v128[:], dr["vec128"].ap(), [], [b_c])
        for nb in range(4):
            kb.dma(POOL, dr["WRB"].ap()[nb], dr["w_rec_t"].ap()[nb], [], [])
            kb.dma(POOL, dr["WAB"].ap()[nb], dr["w_att_t"].ap()[nb], [], [])
            kb.dma(POOL, dr["WOB"].ap()[nb], dr["w_out_t"].ap()[nb], [], [])
        names = ("QDF", "KDF", "QDB", "KDB", "KDFt", "KDBt")
        sets = []
        for s in range(2):
            d = {nm: (kb.sb(es, "h%s%d" % (nm, s), TL, BF16), Buf()) for nm in names}
            d["VX"] = (kb.sb(es, "hVX%d" % s, 2 * W, BF16), Buf())
            d["csc"] = (kb.sb(es, "hcsc%d" % s, 3 * 2 * NCH, F32), Buf())
            kb.memset(d["VX"][0][:], 0.0, [d["VX"][1]])
            sets.append(d)
        GTt = kb.sb(es, "hGT", T, BF16)
        GTb = Buf()
        oacc = kb.sb(es, "oacc", T, F32)
        oab = [Buf("oacc%d" % g) for g in range(8)]
        yst = kb.sb(es, "yst", T, BF16)
        yb = Buf("yst")
        S32 = [kb.sbs(es, "S32_%d" % di, 128, F32, 3) for di in range(2)]
        S16 = [kb.sbs(es, "S16_%d" % di, 128, BF16, 2) for di in range(2)]
        IT = [kb.sbs(es, "IT_%d" % di, 128, BF16, 3) for di in range(2)]
        ntm = [(kb.sb(es, "ntm%d" % i, 512, F32), Buf()) for i in range(3)]
        CS = 3 * 16 * NCH

        def load_head(h, st):
            for nm in names:
                t_, b_ = st[nm]
                kb.dma(SP, t_[:], dr[nm].ap()[h], [], [b_])
            t_, b_ = st["VX"]
            for half in range(2):
                kb.dma(SP, fap(t_, half * 128, [[256, NPAIR], [1, 128]], p0=half * 64, npart=64),
                       bass.AP(dr["VS"], (h * 128 + half * 64) * W, [[W, 64], [128, NPAIR], [1, 128]]), [], [b_])
            t_, b_ = st["csc"]
            kb.dma(SP, fap(t_, 0, [[2 * NCH, 3], [NCH, 2], [1, NCH]]),
                   bass.AP(dr["CSC"], h * NCH, [[CS, 128], [16 * NCH, 3], [8 * NCH, 2], [1, NCH]]), [], [b_])

        load_head(0, sets[0])
        for h in range(8):
            st = sets[h % 2]
            if h + 1 < 8:
                load_head(h + 1, sets[(h + 1) % 2])
            kb.dma(SP, GTt[:], dr["GT"].ap()[h], [], [GTb])
            csc_t, csc_b = st["csc"]
            VX_t, VX_b = st["VX"]
            order = [list(range(NPAIR)), [1, 0] + list(range(NPAIR - 1, 1, -1))]
            cur = [None, None]
            for di in range(2):
                s0, s0b = S32[di].next()
                kb.memset(s0[:], 0.0, [s0b])
                cur[di] = (s0, s0b)
            written = [False] * 8
            ucnt = [0, 0]
            its = [[None, None] for _ in range(NPAIR)]
            ups_l = [[None, None] for _ in range(NPAIR)]

            def part_a(step):
                for di in range(2):
                    p = order[di][step]
                    QD_t, QD_b = st["QDF" if di == 0 else "QDB"]
                    KD_t, KD_b = st["KDF" if di == 0 else "KDB"]
                    if p >= 2:
                        ips, ipb = psum[2 + di], psb[2 + di]
                        kb.mm(ips[:, 0:128], KD_t[:, p * 128:(p + 1) * 128], QD_t[:, p * 128:(p + 1) * 128],
                              [KD_b, QD_b], [ipb])
                        it_t, it_b = IT[di].next()
                        kb.tt(it_t[:], ips[:, 0:128], mk[:, di * 128:(di + 1) * 128], ALU.mult, [ipb, b_c], [it_b])
                        its[step][di] = (it_t, it_b)
                for di in range(2):
                    p = order[di][step]
                    Kt_t, Kt_b = st["KDFt" if di == 0 else "KDBt"]
                    ui = 4 + 2 * di + ucnt[di] % 2
                    ucnt[di] += 1
                    kb.mm(psum[ui][:, 0:256], Kt_t[:, p * 128:(p + 1) * 128], VX_t[:, p * 256:(p + 1) * 256],
                          [Kt_b, VX_b], [psb[ui]])
                    ups_l[step][di] = ui

            def part_b(step):
                info = []
                for di in range(2):
                    p = order[di][step]
                    is_x = p >= 2
                    g = (p - 2) // 4 if is_x else None
                    slot = (p - 2) % 4 if is_x else None
                    chunks = (2 * p, 2 * p + 1) if di == 0 else (2 * p + 1, 2 * p)
                    info.append((p, is_x, g, slot, chunks))
                for ci in range(2):
                    for di in range(2):
                        p, is_x, g, slot, chunks = info[di]
                        ch = chunks[ci]
                        QD_t, QD_b = st["QDF" if di == 0 else "QDB"]
                        oT, oTb = psum[di], psb[di]
                        ui = ups_l[step][di]
                        s_t, s_b = cur[di]
                        half = ch % 2
                        if is_x:
                            it_t, it_b = its[step][di]
                            oc = slot * 128 + half * 64
                            kb.mm(oT[:, oc:oc + 64], VX_t[:, p * 256 + half * 128: p * 256 + (half + 1) * 128],
                                  it_t[:, half * 64:(half + 1) * 64], [VX_b, it_b], [oTb], start=True, stop=False)
                            s16, s16b = S16[di].next()
                            kb.act(s16[:], s_t[:], AF.Identity, [s_b, csc_b], [s16b],
                                   scale=csc_t[:, (0 * 2 + di) * NCH + ch:(0 * 2 + di) * NCH + ch + 1])
                            kb.mm(oT[:, oc:oc + 64], s16[:], QD_t[:, ch * 64:(ch + 1) * 64], [s16b, QD_b], [oTb],
                                  start=False, stop=True)
                        n_t, n_b = S32[di].next()
                        kb.stt(n_t[:], s_t[:], csc_t[:, (2 * 2 + di) * NCH + ch:(2 * 2 + di) * NCH + ch + 1],
                               psum[ui][:, half * 128:(half + 1) * 128], ALU.mult, ALU.add,
                               [s_b, csc_b, psb[ui]], [n_b])
                        cur[di] = (n_t, n_b)
                for di in range(2):
                    p, is_x, g, slot, chunks = info[di]
                    oT, oTb = psum[di], psb[di]
                    if is_x and ((di == 0 and slot == 3) or (di == 1 and slot == 0)):
                        if not written[g]:
                            kb.cp(oacc[:, g * 512:(g + 1) * 512], oT[:], [oTb], [oab[g]], eng=ACT)
                            written[g] = True
                        else:
                            kb.tt(oacc[:, g * 512:(g + 1) * 512], oT[:], oacc[:, g * 512:(g + 1) * 512], ALU.add,
                                  [oTb, oab[g]], [oab[g]])

            part_a(0)
            for step in range(NPAIR):
                if step + 1 < NPAIR:
                    part_a(step + 1)
                part_b(step)
            for g in range(8):
                (sq, sqb), (rs, rsb), (y1, y1b) = ntm
                pss, pssb = psum[2 + g % 2], psb[2 + g % 2]
                sl = slice(g * 512, (g + 1) * 512)
                kb.act(sq[:], oacc[:, sl], AF.Square, [oab[g]], [sqb])
                kb.mm(pss[:], ones_f[:], sq[:], [b_const, sqb], [pssb])
                kb.act(rs[:], pss[:], AF.Ln, [pssb], [rsb], scale=1.0 / 128, bias=EPS)
                kb.act(rs[:], rs[:], AF.Exp, [rsb], [rsb], scale=-0.5)
                kb.stt(y1[:], oacc[:, sl], v128[:, 0:1], rs[:], ALU.mult, ALU.mult, [oab[g], b_c, rsb], [y1b])
                kb.tt(yst[:, sl], y1[:], GTt[:, sl], ALU.mult, [y1b, GTb], [yb])
            kb.dma(SP, dr["YREC"].ap()[h], yst[:], [yb], [])
    P.barrier()


def phase2b(kb, ges, G):
    nc = kb.nc
    P = kb.P
    psum, psb = G["psum"], G["psb"]
    scr_b = G["scr_b"]
    dr = kb.dr
    W = NPAIR * 128
    NQB = T // 128
    SCALE = 128 ** -0.5
    with ExitStack() as es:
        mk_f = kb.sb(es, "mk_f3", 512, F32)
        mk = kb.sb(es, "mk3", 512, BF16)
        ones_b = kb.sb(es, "ones_b", 128, BF16)
        esink = kb.sb(es, "esink", 8, F32)
        b_c = Buf("p2bconst")
        kb.dma(SP, mk_f[:], dr["masks"].ap(), [], [b_c])
        kb.cp(mk[:], mk_f[:], [b_c], [b_c])
        kb.memset(ones_b[:], 1.0, [b_c])
        kb.dma(SP, esink[:], dr["sink_bc"].ap(), [], [b_c])
        kb.act(esink[:], esink[:], AF.Exp, [b_c], [b_c])
        ak = kb.sb(es, "ak", TL, BF16)
        av = kb.sb(es, "av", W, BF16)
        aq = kb.sb(es, "aq", 4 * T, BF16)
        ya = kb.sb(es, "ya", 4 * T, BF16)
        b_in = Buf("attin")
        b_ya = Buf("ya")
        pT = kb.sbs(es, "pT", 512, BF16, 6)
        den = kb.sbs(es, "den", 512, F32, 2)
        for kv in range(2):
            kb.dma(SP, ak[:], dr["AKT"].ap()[kv], [scr_b["AKT"]], [b_in])
            kb.dma(SP, av[:], dr["AVS"].ap()[kv], [scr_b["AVS"]], [b_in])
            for h4 in range(4):
                kb.dma(SP, aq[:, h4 * T:(h4 + 1) * T], dr["AQT"].ap()[kv * 4 + h4], [scr_b["AQT"]], [b_in])
            items = []
            for qb in range(NQB):
                kblocks = [(0, None), (1, None)]
                if qb > 0:
                    kblocks.append((2 + qb - 1, 2))
                kblocks.append((2 + qb, None))
                if qb < NQB - 1:
                    kblocks.append((2 + qb + 1, 3))
                for i, (kblk, mi) in enumerate(kblocks):
                    items.append((qb, kblk, mi, i == 0, i == len(kblocks) - 1))
            pts = [None] * len(items)

            def score(ix):
                qb, kblk, mi, first, last = items[ix]
                s_ps, s_b = psum[ix % 4], psb[ix % 4]
                kb.mm(s_ps[:], ak[:, kblk * 128:(kblk + 1) * 128], fap(aq, qb * 128, [[T, 4], [1, 128]]),
                      [b_in], [s_b])
                p_t, p_b = pT.next()
                kb.act(p_t[:], s_ps[:], AF.Exp, [s_b], [p_b], scale=SCALE)
                if mi is not None:
                    kb.tt(fap(p_t, 0, [[128, 4], [1, 128]]), fap(p_t, 0, [[128, 4], [1, 128]]),
                          fap(mk, mi * 128, [[0, 4], [1, 128]]), ALU.mult, [p_b, b_c], [p_b])
                pts[ix] = (p_t, p_b)

            LOOK = 3
            for ix in range(min(LOOK, len(items))):
                score(ix)
            for ix in range(len(items)):
                if ix + LOOK < len(items):
                    score(ix + LOOK)
                qb, kblk, mi, first, last = items[ix]
                o_ps, o_b = psum[4 + qb % 2], psb[4 + qb % 2]
                d_ps, d_b = psum[6 + qb % 2], psb[6 + qb % 2]
                p_t, p_b = pts[ix]
                kb.mm(o_ps[:], av[:, kblk * 128:(kblk + 1) * 128], p_t[:], [b_in, p_b], [o_b], start=first, stop=last)
                kb.mm(d_ps[:], ones_b[:], p_t[:], [b_c, p_b], [d_b], start=first, stop=last)
                if last:
                    dn_t, dn_b = den.next()
                    kb.tt(fap(dn_t, 0, [[128, 4], [1, 128]]), fap(d_ps, 0, [[128, 4], [1, 128]]),
                          fap(esink, kv * 4, [[1, 4], [0, 128]]), ALU.add, [d_b, b_c], [dn_b])
                    P.add(DVE, lambda e, o=dn_t: e.reciprocal(out=o[:], in_=o[:]), [dn_b], [dn_b])
                    kb.tt(fap(ya, qb * 128, [[T, 4], [1, 128]]), fap(o_ps, 0, [[128, 4], [1, 128]]),
                          fap(dn_t, 0, [[128, 4], [1, 128]]), ALU.mult, [o_b, dn_b], [b_ya])
            for h4 in range(4):
                kb.dma(SP, dr["YATT"].ap()[kv * 4 + h4], ya[:, h4 * T:(h4 + 1) * T], [b_ya], [scr_b["YATT"]])
    P.barrier()


MT3 = 512


def phase3(kb, ges, G):
    nc = kb.nc
    P = kb.P
    psum, psb = G["psum"], G["psb"]
    ident_b, b_const = G["ident_b"], G["b_const"]
    out_d = G["out_d"]
    dr = kb.dr
    n = MT3
    nblk = n // 128
    with ExitStack() as es:
        mods = kb.sb(es, "mods3", 6 * KC, F32)
        gate1 = kb.sb(es, "gate1", D, F32)
        b_c = Buf("p3const")
        kb.dma(SP, mods[:], dr["MODS"].ap()[:, 0:6 * KC], [], [b_c])
        kb.dma(SP, gate1[:], dr["MODS"].ap()[:, 6 * KC:6 * KC + D], [], [b_c])
        yr = kb.sb(es, "yr", 8 * n, BF16)
        ya = kb.sb(es, "ya3", 8 * n, BF16)
        b_yr, b_ya = Buf(), Buf()
        sgr = kb.sbs(es, "sgr", 4 * n, BF16, 2)
        sga = kb.sbs(es, "sga", 4 * n, BF16, 2)
        zT = kb.sb(es, "zT", 16 * n, BF16)
        zb = [Buf() for _ in range(16)]
        xm = [(kb.sb(es, "xm%d" % i, D, F32), Buf()) for i in range(nblk)]
        w8 = kb.sbs(es, "w8_", 8 * 512, BF16, 4)
        w16 = kb.sbs(es, "w16_", KC * 512, BF16, 2)
        tmp = kb.sbs(es, "t3_", n, F32, 4)
        xn = kb.sbs(es, "xn3_", D, BF16, nblk)
        stat = kb.sbs(es, "stat3_", 4, F32, 2)
        h2s = kb.sbs(es, "h2s", KC * 128, BF16, 2)
        unit = 0
        pend = []
        for t0 in range(0, T, n):
            kb.dma(SP, fap(yr, 0, [[n, 8], [1, n]]), bass.AP(dr["YREC"], t0, [[T, 128], [128 * T, 8], [1, n]]),
                   [], [b_yr])
            kb.dma(SP, fap(ya, 0, [[n, 8], [1, n]]), bass.AP(dr["YATT"], t0, [[T, 128], [128 * T, 8], [1, n]]),
                   [], [b_ya])
            for tb in range(nblk):
                kb.dma(SP, xm[tb][0][:], dr["x"].ap()[t0 + tb * 128:t0 + (tb + 1) * 128, :], [], [xm[tb][1]])
            for nb in range(4):
                wr, wrb = w8.next()
                wa, wab = w8.next()
                kb.dma(POOL, wr[:], dr["WRB"].ap()[nb], [], [wrb])
                kb.dma(POOL, wa[:], dr["WAB"].ap()[nb], [], [wab])
                gr, grb = sgr.next()
                ga, gab = sga.next()
                kb.dma(SP, fap(gr, 0, [[n, 4], [1, n]]),
                       bass.AP(dr["SGR"], nb * 4 * 128 * T + t0, [[T, 128], [128 * T, 4], [1, n]]), [], [grb])
                kb.dma(SP, fap(ga, 0, [[n, 4], [1, n]]),
                       bass.AP(dr["SGA"], nb * 4 * 128 * T + t0, [[T, 128], [128 * T, 4], [1, n]]), [], [gab])
                for f in range(4):
                    fb = nb * 4 + f
                    s = unit % 3
                    unit += 1
                    pa, pab = psum[2 * s], psb[2 * s]
                    pb_, pbb = psum[2 * s + 1], psb[2 * s + 1]
                    for kc in range(8):
                        kb.mm(pa[:, 0:n], wr[:, kc * 512 + f * 128: kc * 512 + (f + 1) * 128], yr[:, kc * n:(kc + 1) * n],
                              [wrb, b_yr], [pab], start=(kc == 0), stop=(kc == 7))
                    for kc in range(8):
                        kb.mm(pb_[:, 0:n], wa[:, kc * 512 + f * 128: kc * 512 + (f + 1) * 128], ya[:, kc * n:(kc + 1) * n],
                              [wab, b_ya], [pbb], start=(kc == 0), stop=(kc == 7))
                    t1, t1b = tmp.next()
                    t2, t2b = tmp.next()
                    kb.tt(t1[:], pa[:, 0:n], gr[:, f * n:(f + 1) * n], ALU.mult, [pab, grb], [t1b])
                    kb.tt(t2[:], pb_[:, 0:n], ga[:, f * n:(f + 1) * n], ALU.mult, [pbb, gab], [t2b])
                    kb.tt(zT[:, fb * n:(fb + 1) * n], t1[:], t2[:], ALU.add, [t1b, t2b], [zb[fb]])
                if nb == 0:
                    for fn in pend:
                        fn()
                    pend = []
            for nb2 in range(4):
                wo, wob = w16.next()
                kb.dma(POOL, wo[:], dr["WOB"].ap()[nb2], [], [wob])
                for tb in range(nblk):
                    s = 6 + unit % 2
                    unit += 1
                    pm, pmb = psum[s], psb[s]
                    for fb in range(16):
                        kb.mm(pm[:], zT[:, fb * n + tb * 128: fb * n + (tb + 1) * 128], wo[:, fb * 512:(fb + 1) * 512],
                              [zb[fb], wob], [pmb], start=(fb == 0), stop=(fb == 15))
                    t1, t1b = tmp.next()
                    cs = slice(nb2 * 512, (nb2 + 1) * 512)
                    kb.tt(t1[:], pm[:], gate1[:, cs], ALU.mult, [pmb, b_c], [t1b])
                    kb.tt(xm[tb][0][:, cs], t1[:], xm[tb][0][:, cs], ALU.add, [t1b, xm[tb][1]], [xm[tb][1]])
            for tb in range(nblk):
                xt, xb_ = xm[tb]
                r0 = t0 + tb * 128
                kb.dma(SP, out_d.ap()[r0:r0 + 128, :], xt[:], [xb_], [])
                st, stb = stat.next()
                xnt, xnb = xn.next()
                kb.act(xnt[:], xt[:], AF.Square, [xb_], [xnb, stb], accum=st[:, 0:1])
                kb.ts(st[:, 1:2], st[:, 0:1], 1.0 / D, EPS, ALU.mult, ALU.add, [stb], [stb])
                kb.act(st[:, 2:3], st[:, 1:2], AF.Ln, [stb], [stb])
                kb.act(st[:, 3:4], st[:, 2:3], AF.Exp, [stb], [stb], scale=-0.5)
                kb.act(xnt[:], xt[:], AF.Identity, [xb_, stb], [xnb], scale=st[:, 3:4])

                def part2(xnt=xnt, xnb=xnb, r0=r0):
                    h2, h2b = h2s.next()
                    for half in range(2):
                        pt, ptb = psum[6 + half], psb[6 + half]
                        ptv = pt.bitcast(BF16)
                        for j in range(8):
                            kc = half * 8 + j
                            kb.tr(ptv[:, j * 128:(j + 1) * 128], xnt[:, kc * 128:(kc + 1) * 128], ident_b[:],
                                  [xnb, b_const], [ptb])
                        for j in range(8):
                            kc = half * 8 + j
                            kb.ts(h2[:, kc * 128:(kc + 1) * 128], ptv[:, j * 128:(j + 1) * 128],
                                  mods[:, 64 + kc:65 + kc], mods[:, 80 + kc:81 + kc], ALU.mult, ALU.add,
                                  [ptb, b_c], [h2b])
                    kb.dma(SP, bass.AP(dr["H2T"], r0, [[T, 128], [128 * T, KC], [1, 128]]),
                           fap(h2, 0, [[128, KC], [1, 128]]), [h2b], [])
                pend.append(part2)
        for fn in pend:
            fn()
    P.barrier()


MT4 = 512


def phase4(kb, ges, G):
    nc = kb.nc
    P = kb.P
    psum, psb = G["psum"], G["psb"]
    scr_b = G["scr_b"]
    out_d, out_b = G["out_d"], G["out_b"]
    dr = kb.dr
    n = MT4
    nblk = n // 128
    HW = n + 2
    with ExitStack() as es:
        gate2 = kb.sb(es, "gate2", D, F32)
        cw = kb.sb(es, "cw", 3 * NFB, F32)
        cb = kb.sb(es, "cb", NFB, F32)
        b_c = Buf("p4const")
        kb.dma(SP, gate2[:], dr["MODS"].ap()[:, 6 * KC + D:6 * KC + 2 * D], [scr_b["MODS"]], [b_c])
        kb.dma(SP, cw[:], dr["convwT"].ap(), [], [b_c])
        kb.dma(SP, cb[:], dr["convbT"].ap(), [], [b_c])
        h2r = kb.sbs(es, "h2r", KC * HW, BF16, 2)
        acc = [[(kb.sb(es, "acc%d_%d" % (j, i), D, F32), Buf()) for i in range(nblk)] for j in range(2)]
        actT = kb.sbs(es, "actT", 11 * n, BF16, 2)
        wup = kb.sbs(es, "wup", KC * 256, BF16, 3)
        wdn = kb.sbs(es, "wdn", 11 * 512, BF16, 2)
        A_sb = kb.sbs(es, "A_sb", HW, F32, 2)
        c_sb = kb.sbs(es, "c_sb", n, F32, 2)
        s_sb = kb.sbs(es, "s_sb", n, F32, 2)
        xmr = kb.sbs(es, "xmr", D, F32, 2)
        unit = 0
        prev_epi = None
        for t0 in range(0, T, n):
            h2, h2b = h2r.next()
            lo = 1 if t0 == 0 else 0
            hi = HW - 1 if t0 + n == T else HW
            if lo:
                kb.memset(fap(h2, 0, [[HW, KC]]), 0.0, [h2b])
            if hi < HW:
                kb.memset(fap(h2, HW - 1, [[HW, KC]]), 0.0, [h2b])
            kb.dma(SP, fap(h2, lo, [[HW, KC], [1, hi - lo]]),
                   bass.AP(dr["H2T"], t0 - 1 + lo, [[T, 128], [128 * T, KC], [1, hi - lo]]), [scr_b["H2T"]], [h2b])
            for g in range(4):
                if g == 1 and prev_epi is not None:
                    prev_epi()
                    prev_epi = None
                at, atb = actT.next()
                for fi in range(11):
                    fb = g * 11 + fi
                    wu, wub = wup.next()
                    kb.dma(POOL, wu[:], dr["w_up_t"].ap()[fb], [], [wub])
                    s = unit % 2
                    unit += 1
                    pa, pab = psum[3 * s], psb[3 * s]
                    ph, phb = psum[3 * s + 1], psb[3 * s + 1]
                    pu, pub = psum[3 * s + 2], psb[3 * s + 2]
                    for kc in range(KC):
                        kb.mm(pa[:], wu[:, kc * 256: kc * 256 + 128], h2[:, kc * HW + 1: kc * HW + 1 + n],
                              [wub, h2b], [pab], start=(kc == 0), stop=(kc == KC - 1))
                    for kc in range(KC):
                        kb.mm(ph[:, 0:2], wu[:, kc * 256: kc * 256 + 128], fap(h2, kc * HW, [[HW - 1, 2]]),
                              [wub, h2b], [phb], start=(kc == 0), stop=(kc == KC - 1))
                    for kc in range(KC):
                        kb.mm(pu[:], wu[:, kc * 256 + 128: kc * 256 + 256], h2[:, kc * HW + 1: kc * HW + 1 + n],
                              [wub, h2b], [pub], start=(kc == 0), stop=(kc == KC - 1))
                    A, Ab = A_sb.next()
                    c, cbb = c_sb.next()
                    sl, slb = s_sb.next()
                    kb.cp(A[:, 1:1 + n], pa[:], [pab], [Ab], eng=ACT)
                    kb.cp(fap(A, 0, [[HW - 1, 2]]), ph[:, 0:2], [phb], [Ab], eng=ACT)
                    kb.ts(c[:], A[:, 0:n], cw[:, fb:fb + 1], cb[:, fb:fb + 1], ALU.mult, ALU.add, [Ab, b_c], [cbb])
                    kb.stt(c[:], A[:, 1:1 + n], cw[:, NFB + fb:NFB + fb + 1], c[:], ALU.mult, ALU.add,
                           [Ab, b_c, cbb], [cbb])
                    kb.stt(c[:], A[:, 2:2 + n], cw[:, 2 * NFB + fb:2 * NFB + fb + 1], c[:], ALU.mult, ALU.add,
                           [Ab, b_c, cbb], [cbb])
                    kb.act(sl[:], c[:], AF.Silu, [cbb], [slb])
                    kb.tt(at[:, fi * n:(fi + 1) * n], pu[:], sl[:], ALU.mult, [pub, slb], [atb])
                for nb in range(4):
                    wd, wdb = wdn.next()
                    kb.dma(POOL, wd[:], dr["w_down_t"].ap()[g, nb], [], [wdb])
                    for tb in range(nblk):
                        s = 6 + unit % 2
                        unit += 1
                        pd, pdb = psum[s], psb[s]
                        for fi in range(11):
                            kb.mm(pd[:], at[:, fi * n + tb * 128: fi * n + (tb + 1) * 128], wd[:, fi * 512:(fi + 1) * 512],
                                  [atb, wdb], [pdb], start=(fi == 0), stop=(fi == 10))
                        a_t, a_b = acc[(t0 // n) % 2][tb]
                        cs = slice(nb * 512, (nb + 1) * 512)
                        if g == 0:
                            kb.cp(a_t[:, cs], pd[:], [pdb], [a_b], eng=ACT)
                        else:
                            kb.tt(a_t[:, cs], pd[:], a_t[:, cs], ALU.add, [pdb, a_b], [a_b])
            def epi(t0=t0):
                for tb in range(nblk):
                    a_t, a_b = acc[(t0 // n) % 2][tb]
                    r0 = t0 + tb * 128
                    xr, xrb = xmr.next()
                    kb.dma(SP, xr[:], out_d.ap()[r0:r0 + 128, :], [], [xrb])
                    kb.tt(a_t[:], a_t[:], gate2[:], ALU.mult, [a_b, b_c], [a_b])
                    kb.tt(xr[:], a_t[:], xr[:], ALU.add, [a_b, xrb], [xrb])
                    kb.dma(SP, out_d.ap()[r0:r0 + 128, :], xr[:], [xrb], [])
            prev_epi = epi
        if prev_epi is not None:
            prev_epi()


def tile_w(w, nb=512):
    K, N = w.shape
    return np.ascontiguousarray(w.reshape(K // 128, 128, N // nb, nb).transpose(2, 1, 0, 3)).reshape(
        N // nb, 128, (K // 128) * nb)


def featT(v):
    return np.ascontiguousarray(v.reshape(-1, 128).T)


def host_consts():
    ident = np.eye(128, dtype=np.float32)
    R = np.zeros((128, 128), np.float32)
    for m in range(128):
        if (m % 64) < 32:
            R[m, m + 32] = -1.0
        else:
            R[m, m - 32] = 1.0
    RT = np.ascontiguousarray(R.T)
    s = np.arange(128)[:, None]
    c = np.arange(128)[None, :]
    same = (s // 64) == (c // 64)
    mask_f = (same & (s <= c)).astype(np.float32)
    mask_b = (same & (s >= c)).astype(np.float32)
    am_prev = (s >= c).astype(np.float32)
    am_next = (s <= c).astype(np.float32)
    masks = np.concatenate([mask_f, mask_b, am_prev, am_next], axis=1)
    t = np.arange(T)
    rows = (t // 64).astype(np.float32)
    cols = (t % 64).astype(np.float32)
    inv_freq = (10000.0 ** (-np.arange(32, dtype=np.float32) / 32)).astype(np.float32)
    C = np.zeros((128, T), np.float32)
    S = np.zeros((128, T), np.float32)
    for d in range(128):
        pos = rows if d < 64 else cols
        ang = (pos * inv_freq[d % 32]).astype(np.float32)
        C[d] = np.cos(ang)
        S[d] = np.sin(ang)
    return ident, RT, masks, C, S


_CACHE = {}


def prep(x, c, ctx, c_ctx, w_ada, b_ada, norm1_g, w_in, hgrn_lower_bounds, hgrn_norm_g, q_norm_g,
         k_norm_g, attn_sink, w_rec_proj, w_att_proj, w_out, norm2_g, w_up, conv_w, conv_b, w_down):
    f = lambda a: np.asarray(a, dtype=np.float32)
    x, c, ctx, c_ctx = f(x), f(c), f(ctx), f(c_ctx)
    B = x.shape[0]
    ident, RT, masks, C, S = host_consts()
    w_in0 = f(w_in)[0]
    cols = []
    for h in range(8):
        for seg in (0, 1, 2, 4):
            cols += list(range(seg * 1024 + h * 128, seg * 1024 + (h + 1) * 128))
    cols += list(range(3072, 4096)) + list(range(5120, 6144)) + list(range(6144, 6656)) + list(range(6656, 10752))
    w_in_t = tile_w(w_in0[:, cols])
    w_ada_t = tile_w(f(w_ada)[0])
    b_ada0 = f(b_ada)[0]
    b_adaT = featT(b_ada0)
    b_gate = np.concatenate([b_ada0[2 * D:3 * D], b_ada0[5 * D:6 * D]])
    b_gate_bc = np.ascontiguousarray(np.broadcast_to(b_gate[None, :], (128, 2 * D)))
    w_up0 = f(w_up)[0]
    ucols = []
    for fb in range(NFB):
        ucols += list(range(fb * 128, (fb + 1) * 128)) + list(range(DFF + fb * 128, DFF + (fb + 1) * 128))
    w_up_t = tile_w(w_up0[:, ucols], 256)
    w_down0 = f(w_down)[0]
    w_down_t = np.ascontiguousarray(w_down0.reshape(4, 11, 128, 4, 512).transpose(0, 3, 2, 1, 4)).reshape(
        4, 4, 128, 11 * 512)
    lbraw = f(hgrn_lower_bounds)
    lbrawT = np.ascontiguousarray(lbraw.reshape(2, 2, 8, 128).transpose(3, 0, 1, 2)).reshape(128, 32)
    vec128 = np.stack([f(hgrn_norm_g)[0], f(q_norm_g)[0], f(k_norm_g)[0]], axis=1)
    sink_bc = np.ascontiguousarray(np.broadcast_to(f(attn_sink)[0][None, :], (128, 8)))
    cw = f(conv_w)[0]
    convwT = np.ascontiguousarray(cw.reshape(3, NFB, 128).transpose(2, 0, 1)).reshape(128, 3 * NFB)
    convbT = featT(f(conv_b)[0])
    shared = {
        "w_ada_t": w_ada_t, "b_adaT": b_adaT, "b_gate_bc": b_gate_bc, "g1T": featT(f(norm1_g)[0]),
        "g2T": featT(f(norm2_g)[0]), "w_in_t": w_in_t, "lbrawT": lbrawT, "vec128": np.ascontiguousarray(vec128),
        "sink_bc": sink_bc, "w_rec_t": tile_w(f(w_rec_proj)[0]), "w_att_t": tile_w(f(w_att_proj)[0]),
        "w_out_t": tile_w(f(w_out)[0]), "w_up_t": w_up_t, "convwT": convwT, "convbT": convbT,
        "w_down_t": w_down_t, "ident": ident, "ropeRT": RT, "masks": masks, "ropeC": C, "ropeS": S,
    }
    in_maps = []
    for b in range(B):
        cv = np.stack([c[b], c_ctx], axis=0)
        cvecT = np.ascontiguousarray(cv.reshape(2, KC, 128).transpose(2, 0, 1)).reshape(128, 32)
        m = dict(shared)
        m["x"] = np.ascontiguousarray(x[b])
        m["ctx"] = np.ascontiguousarray(ctx[b])
        m["cvecT"] = cvecT
        in_maps.append(m)
    return in_maps


def kernel(**inputs):
    in_maps = prep(**inputs)
    if "nc" not in _CACHE:
        _CACHE["nc"] = build_program()
    nc = _CACHE["nc"]
    res = run_bass_kernel_spmd(nc, in_maps, core_ids=list(range(len(in_maps))))
    return np.stack([np.asarray(r["out"], dtype=np.float32) for r in res.results], axis=0)
```
